# Optimizing a Trainium2 kernel written in Bass

```python
import math
import jax, jax.numpy as jnp
from jax import lax
import numpy as np

D_MODEL = 1024
BATCH = 2
SEQ = 8192
DEPTH = 1
DEC_BATCH = 128
DEC_SEQ = 1
PAST_LEN = 2048
PAGE_SIZE = 128

D_MIX = D_MODEL
HEAD_DIM = 64
D_A = D_MIX // 2
D_B = D_MIX - D_A
A_GROUPS = D_A // HEAD_DIM
CHUNK = 128
N_HEADS = D_B // HEAD_DIM
N_KV = 2
GQA = N_HEADS // N_KV
KV_W = N_KV * HEAD_DIM
ROT_DIM = HEAD_DIM // 4
ROPE_THETA = 500000.0
CMP_BLOCK = 32
CMP_STRIDE = 16
SEL_BLOCK = 64
N_SELECT = 16
WINDOW = 512
Q_BLOCK = 128
NORM_EPS = 1e-6
FORCE_SCORE = 1e4
NEG = -1e30
PROJ_SIZES = (D_A, D_A, D_A, D_B, KV_W, KV_W, KV_W, KV_W, KV_W, KV_W, 3 * N_HEADS, D_B)

kernel_name = 'hybrid_gmlp_nsa_decoder_step'


def rms_norm(x, g):
    xf = x.astype(jnp.float32)
    y = xf * lax.rsqrt(jnp.mean(xf * xf, -1, keepdims=True) + NORM_EPS)
    return (y * g.astype(jnp.float32)).astype(x.dtype)


def layer_norm(x, g, b):
    xf = x.astype(jnp.float32)
    xc = xf - jnp.mean(xf, -1, keepdims=True)
    y = xc * lax.rsqrt(jnp.mean(xc * xc, -1, keepdims=True) + NORM_EPS)
    return (y * g.astype(jnp.float32) + b.astype(jnp.float32)).astype(x.dtype)


def rope(x, pos):
    half = ROT_DIM // 2
    inv = jnp.power(jnp.float32(ROPE_THETA), -jnp.arange(half, dtype=jnp.float32) / half)
    ang = pos.astype(jnp.float32)[:, None] * inv[None, :]
    cos = jnp.cos(ang)[:, None, :]
    sin = jnp.sin(ang)[:, None, :]
    xr = x[..., :ROT_DIM].astype(jnp.float32)
    x1, x2 = xr[..., :half], xr[..., half:]
    rot = jnp.concatenate([x1 * cos - x2 * sin, x2 * cos + x1 * sin], -1).astype(x.dtype)
    return jnp.concatenate([rot, x[..., ROT_DIM:]], -1)


def split_projection(hn, w_in, pos):
    Bn, T = hn.shape[0], hn.shape[1]
    z = jnp.einsum('btd,de->bte', hn, w_in)
    offs = []
    acc = 0
    for s in PROJ_SIZES[:-1]:
        acc += s
        offs.append(acc)
    u, v, z_a, q, kc, vc, ks, vs, kw, vw, g, z_b = jnp.split(z, offs, axis=-1)
    heads = lambda a, n: a.reshape(Bn, T, n, HEAD_DIM)
    q = rope(heads(q, N_HEADS), pos)
    ks = rope(heads(ks, N_KV), pos)
    kw = rope(heads(kw, N_KV), pos)
    gates = jax.nn.sigmoid(g.astype(jnp.float32)).reshape(Bn, T, N_HEADS, 3).astype(hn.dtype)
    return (u, v, z_a, q, heads(kc, N_KV), heads(vc, N_KV), ks, heads(vs, N_KV),
            kw, heads(vw, N_KV), gates, z_b)


def spatial_gate(v, w_s, b_s):
    Bn, T = v.shape[0], v.shape[1]
    lc = min(CHUNK, T)
    n_chunk = -(-T // lc)
    pad = n_chunk * lc - T
    vp = jnp.pad(v, ((0, 0), (0, pad), (0, 0), (0, 0))).reshape(Bn, n_chunk, lc, A_GROUPS, HEAD_DIM)
    w = w_s[:, :lc, :lc] * jnp.tril(jnp.ones((lc, lc), w_s.dtype))
    s = jnp.einsum('gts,bnsgc->bntgc', w, vp) + b_s[:, :lc].T[None, None, :, :, None]
    return s.reshape(Bn, n_chunk * lc, A_GROUPS, HEAD_DIM)[:, :T]


def gmlp_branch(u, v, z_a, ln_g, ln_b, w_s, b_s):
    Bn, T = u.shape[0], u.shape[1]
    vn = layer_norm(v, ln_g, ln_b)
    s = spatial_gate(vn.reshape(Bn, T, A_GROUPS, HEAD_DIM), w_s, b_s).reshape(Bn, T, D_A)
    return u * s * jax.nn.silu(z_a), vn


def compress(k_raw, w_c, pe_c):
    L = k_raw.shape[1]
    n_cmp = (L - CMP_BLOCK) // CMP_STRIDE + 1
    starts = jnp.arange(n_cmp, dtype=jnp.int32) * CMP_STRIDE
    idx = starts[:, None] + jnp.arange(CMP_BLOCK, dtype=jnp.int32)[None, :]
    blocks = k_raw[:, idx] + pe_c[None, None, :, None, :]
    return jnp.einsum('bnlhd,lde->bnhe', blocks, w_c), starts


def to_blocks(k):
    Bn, L = k.shape[0], k.shape[1]
    n_blk = -(-L // SEL_BLOCK)
    kp = jnp.pad(k, ((0, 0), (0, n_blk * SEL_BLOCK - L), (0, 0), (0, 0)))
    return kp.reshape(Bn, n_blk, SEL_BLOCK, N_KV, HEAD_DIM).transpose(0, 3, 1, 2, 4)


def nsa_core(q, q_pos, gates, kc, vc, c_end, kb, vb, kw, vw, kw_pos):
    Bn, Tq = q.shape[0], q.shape[1]
    scale = HEAD_DIM ** -0.5
    qg = q.reshape(Bn, Tq, N_KV, GQA, HEAD_DIM)
    t = q_pos[:, None]
    m_c = (c_end[None, :] <= t)[None, :, None, None, :]
    s_c = jnp.einsum('bqhgd,bnhd->bqhgn', qg, kc).astype(jnp.float32) * scale
    p_c = jax.nn.softmax(jnp.where(m_c, s_c, NEG), -1) * m_c
    o_c = jnp.einsum('bqhgn,bnhd->bqhgd', p_c.astype(vc.dtype), vc)
    n_blk = kb.shape[2]
    cs = c_end - (CMP_BLOCK - 1)
    bs = jnp.arange(n_blk, dtype=jnp.int32) * SEL_BLOCK
    overlap = ((cs[:, None] < bs[None, :] + SEL_BLOCK) & (cs[:, None] + CMP_BLOCK > bs[None, :])).astype(jnp.float32)
    imp = jnp.einsum('bqhgn,nk->bqhk', p_c, overlap)
    blk = jnp.arange(n_blk, dtype=jnp.int32)[None, :]
    cur = t // SEL_BLOCK
    forced = ((blk == 0) | (blk == cur) | (blk == cur - 1))[None, :, None, :]
    valid = (bs[None, :] <= t)[None, :, None, :]
    imp = jnp.where(valid, jnp.where(forced, imp + FORCE_SCORE, imp), -jnp.inf)
    n_top = min(N_SELECT, n_blk)
    _, sel = lax.top_k(imp, n_top)
    sel_t = sel.transpose(0, 2, 1, 3)
    gather = jax.vmap(jax.vmap(lambda kk, ii: kk[ii]))
    kg = gather(kb, sel_t)
    vg = gather(vb, sel_t)
    tok = sel_t[..., None] * SEL_BLOCK + jnp.arange(SEL_BLOCK, dtype=jnp.int32)
    m_s = (tok <= q_pos[None, None, :, None, None]).transpose(0, 2, 1, 3, 4)[:, :, :, None]
    s_s = jnp.einsum('bqhgd,bhqksd->bqhgks', qg, kg).astype(jnp.float32) * scale
    p_s = jax.nn.softmax(jnp.where(m_s, s_s, NEG), axis=(-2, -1))
    o_s = jnp.einsum('bqhgks,bhqksd->bqhgd', p_s.astype(vg.dtype), vg)
    kp = kw_pos[None, :]
    m_w = ((kp <= t) & (kp > t - WINDOW) & (kp >= 0))[None, :, None, None, :]
    s_w = jnp.einsum('bqhgd,blhd->bqhgl', qg, kw).astype(jnp.float32) * scale
    p_w = jax.nn.softmax(jnp.where(m_w, s_w, NEG), -1)
    o_w = jnp.einsum('bqhgl,blhd->bqhgd', p_w.astype(vw.dtype), vw)
    gq = gates.reshape(Bn, Tq, N_KV, GQA, 3)
    o = gq[..., 0:1] * o_c + gq[..., 1:2] * o_s + gq[..., 2:3] * o_w
    return o.reshape(Bn, Tq, N_HEADS, HEAD_DIM)


def merge_out(x, a_out, o_b, z_b, w_out):
    Bn, T = x.shape[0], x.shape[1]
    mix = jnp.concatenate([a_out, o_b.reshape(Bn, T, D_B) * jax.nn.silu(z_b)], -1)
    return x + jnp.einsum('bte,ed->btd', mix, w_out)


def prompt_layer(x, norm_g, w_in, ln_v_g, ln_v_b, w_s, b_s, w_ck, pe_ck, w_cv, pe_cv, w_out):
    Bn, T = x.shape[0], x.shape[1]
    pos = jnp.arange(T, dtype=jnp.int32)
    hn = rms_norm(x, norm_g)
    u, v, z_a, q, kc, vc, ks, vs, kw, vw, gates, z_b = split_projection(hn, w_in, pos)
    a_out, _ = gmlp_branch(u, v, z_a, ln_v_g, ln_v_b, w_s, b_s)
    kcc, starts = compress(kc, w_ck, pe_ck)
    kcc = rope(kcc, starts)
    vcc, _ = compress(vc, w_cv, pe_cv)
    c_end = starts + CMP_BLOCK - 1
    kb, vb = to_blocks(ks), to_blocks(vs)
    padw = ((0, 0), (WINDOW, 0), (0, 0), (0, 0))
    kw_pad, vw_pad = jnp.pad(kw, padw), jnp.pad(vw, padw)
    n_qb = T // Q_BLOCK
    q_blocks = jnp.moveaxis(q.reshape(Bn, n_qb, Q_BLOCK, N_HEADS, HEAD_DIM), 1, 0)
    g_blocks = jnp.moveaxis(gates.reshape(Bn, n_qb, Q_BLOCK, N_HEADS, 3), 1, 0)
    q0s = jnp.arange(n_qb, dtype=jnp.int32) * Q_BLOCK

    def block_fn(args):
        qb, gb, q0 = args
        qpos = q0 + jnp.arange(Q_BLOCK, dtype=jnp.int32)
        kwb = lax.dynamic_slice_in_dim(kw_pad, q0, WINDOW + Q_BLOCK, axis=1)
        vwb = lax.dynamic_slice_in_dim(vw_pad, q0, WINDOW + Q_BLOCK, axis=1)
        kwpos = q0 - WINDOW + jnp.arange(WINDOW + Q_BLOCK, dtype=jnp.int32)
        return nsa_core(qb, qpos, gb, kcc, vcc, c_end, kb, vb, kwb, vwb, kwpos)

    o_b = lax.map(block_fn, (q_blocks, g_blocks, q0s))
    o_b = jnp.moveaxis(o_b, 0, 1).reshape(Bn, T, N_HEADS, HEAD_DIM)
    y = merge_out(x, a_out, o_b, z_b, w_out)
    keep = min(WINDOW, T)
    return y, (kc, vc, ks, vs, kw[:, T - keep:], vw[:, T - keep:])


def sample_layer(x, ck_cmp, cv_cmp, ck_sel, cv_sel, ck_win, cv_win, page_table,
                 norm_g, w_in, ln_v_g, ln_v_b, w_s, b_s, w_ck, pe_ck, w_cv, pe_cv, w_out):
    Bn, T = x.shape[0], x.shape[1]
    past = page_table.shape[1] * PAGE_SIZE
    pos = past + jnp.arange(T, dtype=jnp.int32)
    hn = rms_norm(x, norm_g)
    u, v, z_a, q, kc, vc, ks, vs, kw, vw, gates, z_b = split_projection(hn, w_in, pos)
    a_out, vn = gmlp_branch(u, v, z_a, ln_v_g, ln_v_b, w_s, b_s)
    pages = lambda c: c[page_table].reshape(Bn, past, N_KV, HEAD_DIM)
    kc_full = jnp.concatenate([pages(ck_cmp), kc], 1)
    vc_full = jnp.concatenate([pages(cv_cmp), vc], 1)
    ks_full = jnp.concatenate([pages(ck_sel), ks], 1)
    vs_full = jnp.concatenate([pages(cv_sel), vs], 1)
    kcc, starts = compress(kc_full, w_ck, pe_ck)
    kcc = rope(kcc, starts)
    vcc, _ = compress(vc_full, w_cv, pe_cv)
    c_end = starts + CMP_BLOCK - 1
    kb, vb = to_blocks(ks_full), to_blocks(vs_full)
    keep = ck_win.shape[1]
    kw_all = jnp.concatenate([ck_win, kw], 1)
    vw_all = jnp.concatenate([cv_win, vw], 1)
    kwpos = past - keep + jnp.arange(keep + T, dtype=jnp.int32)
    o_b = nsa_core(q, pos, gates, kcc, vcc, c_end, kb, vb, kw_all, vw_all, kwpos)
    y = merge_out(x, a_out, o_b, z_b, w_out)
    return y, (kc, vc, ks, vs, kw_all[:, T:], vw_all[:, T:], vn)


def setup_inputs(seed: int = 0) -> dict:
    key = jax.random.key(seed)
    ks = jax.random.split(key, 24)
    n_pages = PAST_LEN // PAGE_SIZE
    n_used = DEC_BATCH * n_pages
    n_pool = n_used + n_used // 4
    win_keep = min(WINDOW, PAST_LEN)
    d_in = sum(PROJ_SIZES)
    nrm = lambda k, shape, s: jax.random.normal(k, shape, jnp.float32) * s
    page_table = jax.random.permutation(ks[0], n_pool)[:n_used].reshape(DEC_BATCH, n_pages).astype(jnp.int32)
    paged = (DEPTH, n_pool, PAGE_SIZE, N_KV, HEAD_DIM)
    win = (DEPTH, DEC_BATCH, win_keep, N_KV, HEAD_DIM)
    return {
        'x_prompt': nrm(ks[1], (BATCH, SEQ, D_MODEL), 1.0),
        'x_sample': nrm(ks[2], (DEC_BATCH, DEC_SEQ, D_MODEL), 1.0),
        'cache_k_cmp': nrm(ks[3], paged, 1.0),
        'cache_v_cmp': nrm(ks[4], paged, 1.0),
        'cache_k_sel': nrm(ks[5], paged, 1.0),
        'cache_v_sel': nrm(ks[6], paged, 1.0),
        'cache_k_win': nrm(ks[7], win, 1.0),
        'cache_v_win': nrm(ks[8], win, 1.0),
        'page_table': page_table,
        'norm_g': 1.0 + nrm(ks[9], (DEPTH, D_MODEL), 0.02),
        'w_in': nrm(ks[10], (DEPTH, D_MODEL, d_in), D_MODEL ** -0.5),
        'ln_v_g': 1.0 + nrm(ks[11], (DEPTH, D_A), 0.02),
        'ln_v_b': nrm(ks[12], (DEPTH, D_A), 0.02),
        'w_s': nrm(ks[13], (DEPTH, A_GROUPS, CHUNK, CHUNK), CHUNK ** -0.5),
        'b_s': 1.0 + nrm(ks[14], (DEPTH, A_GROUPS, CHUNK), 0.02),
        'w_ck': nrm(ks[15], (DEPTH, CMP_BLOCK, HEAD_DIM, HEAD_DIM), (CMP_BLOCK * HEAD_DIM) ** -0.5),
        'pe_ck': nrm(ks[16], (DEPTH, CMP_BLOCK, HEAD_DIM), 0.02),
        'w_cv': nrm(ks[17], (DEPTH, CMP_BLOCK, HEAD_DIM, HEAD_DIM), (CMP_BLOCK * HEAD_DIM) ** -0.5),
        'pe_cv': nrm(ks[18], (DEPTH, CMP_BLOCK, HEAD_DIM), 0.02),
        'w_out': nrm(ks[19], (DEPTH, D_MIX, D_MODEL), D_MIX ** -0.5),
        'final_g': 1.0 + nrm(ks[20], (D_MODEL,), 0.02),
    }


def reference(x_prompt, x_sample, cache_k_cmp, cache_v_cmp, cache_k_sel, cache_v_sel,
              cache_k_win, cache_v_win, page_table, norm_g, w_in, ln_v_g, ln_v_b, w_s, b_s,
              w_ck, pe_ck, w_cv, pe_cv, w_out, final_g):
    hp, hs = x_prompt, x_sample
    sp = [[] for _ in range(6)]
    ss = [[] for _ in range(7)]
    for l in range(DEPTH):
        w = (norm_g[l], w_in[l], ln_v_g[l], ln_v_b[l], w_s[l], b_s[l],
             w_ck[l], pe_ck[l], w_cv[l], pe_cv[l], w_out[l])
        hp, st_p = prompt_layer(hp, *w)
        hs, st_s = sample_layer(hs, cache_k_cmp[l], cache_v_cmp[l], cache_k_sel[l], cache_v_sel[l],
                                cache_k_win[l], cache_v_win[l], page_table, *w)
        for i in range(6):
            sp[i].append(st_p[i])
        for i in range(7):
            ss[i].append(st_s[i])
    y_prompt = rms_norm(hp, final_g)
    y_sample = rms_norm(hs, final_g)
    sp = [jnp.stack(a, 0) for a in sp]
    ss = [jnp.stack(a, 0) for a in ss]
    return (y_prompt, y_sample, sp[0], sp[1], sp[2], sp[3], sp[4], sp[5],
            ss[0], ss[1], ss[2], ss[3], ss[4], ss[5], ss[6])
```

```python
from contextlib import ExitStack
import numpy as np
import ml_dtypes
import concourse.bass as bass
import concourse.mybir as mybir
from concourse.bass_utils import run_bass_kernel_spmd

F32 = mybir.dt.float32
BF16 = mybir.dt.bfloat16
I32 = mybir.dt.int32
AF = mybir.ActivationFunctionType
ALU = mybir.AluOpType

ENG_NAMES = ("pe", "act", "dve", "pool", "sp")


class Res:
    __slots__ = ("name", "last_w", "readers", "dma_sem", "dma_cnt", "dram")

    def __init__(self, name, dram=False):
        self.name = name
        self.last_w = None
        self.readers = []
        self.dma_sem = {}
        self.dma_cnt = {}
        self.dram = dram


class Op:
    __slots__ = ("eng", "fn", "reads", "writes", "dma", "idx", "seq", "waits", "dres", "dval", "dkind")

    def __init__(self, eng, fn, reads, writes, dma):
        self.eng = eng
        self.fn = fn
        self.reads = reads
        self.writes = writes
        self.dma = dma
        self.waits = []


class Sched:
    def __init__(self, nc, sem_stack):
        self.nc = nc
        self.ops = []
        self.flushed = 0
        self.sem_stack = sem_stack
        self.cnt = {e: 0 for e in ENG_NAMES}
        self.esem = {e: sem_stack.enter_context(nc.semaphore("es_" + e)) for e in ENG_NAMES if e != "sp"}
        self.out_res = Res("dramonly")
        self.known = {e: {} for e in ENG_NAMES}

    def op(self, eng, fn, reads=(), writes=(), dma=False):
        o = Op(eng, fn, [r for r in reads if r is not None], [r for r in writes if r is not None], dma)
        o.idx = len(self.ops)
        self.ops.append(o)
        return o

    def pe(self, fn, reads=(), writes=()):
        return self.op("pe", fn, reads, writes)

    def act(self, fn, reads=(), writes=()):
        return self.op("act", fn, reads, writes)

    def dve(self, fn, reads=(), writes=()):
        return self.op("dve", fn, reads, writes)

    def pool(self, fn, reads=(), writes=()):
        return self.op("pool", fn, reads, writes)

    def dma(self, fn, reads=(), writes=(), q="sp"):
        return self.op(q, fn, reads, writes, dma=True)

    def flush(self):
        nc = self.nc
        ops = self.ops
        new = ops[self.flushed:]
        self.flushed = len(ops)
        if not new:
            return
        cnt = self.cnt
        esem = self.esem
        known = self.known
        for o in new:
            if not o.dma:
                cnt[o.eng] += 1
                o.seq = cnt[o.eng]
            else:
                o.seq = None

        def token_of(o):
            if o.dma:
                return ("d", o.dres, o.dval, o.dkind)
            return ("e", o.eng, o.seq)

        def add_wait(o, tok):
            if tok[0] == "e":
                key = ("e", tok[1])
                if tok[1] == o.eng and o.eng == "pe" and not o.dma:
                    return
            else:
                key = ("d", id(tok[1]), tok[3])
            val = tok[2]
            if tok[0] == "d":
                val = tok[1].dma_cnt[tok[3]]
            k = known[o.eng]
            if k.get(key, 0) >= val:
                return
            k[key] = val
            o.waits.append((tok, val))

        touched = []
        seen = set()
        for o in new:
            for r in o.reads:
                if r.last_w is not None:
                    add_wait(o, token_of(ops[r.last_w]))
            for r in o.writes:
                if r.last_w is not None:
                    add_wait(o, token_of(ops[r.last_w]))
                for ri in r.readers:
                    if ri != o.idx:
                        add_wait(o, token_of(ops[ri]))
            if o.dma:
                dres = None
                for r in list(o.writes) + list(o.reads):
                    if not r.dram:
                        dres = r
                        break
                if dres is None:
                    dres = self.out_res
                kind = "sw" if o.eng == "pool" else "hw"
                if kind not in dres.dma_sem:
                    dres.dma_sem[kind] = self.sem_stack.enter_context(nc.semaphore("ds%s_%s" % (kind, dres.name)))
                    dres.dma_cnt[kind] = 0
                dres.dma_cnt[kind] += 16
                o.dres = dres
                o.dkind = kind
                o.dval = dres.dma_cnt[kind]
                if (id(dres), kind) not in seen:
                    seen.add((id(dres), kind))
                    touched.append((dres, kind))
            for r in o.reads:
                if r not in o.writes:
                    r.readers.append(o.idx)
            for r in o.writes:
                r.last_w = o.idx
                r.readers = []
        by_eng = {e: [o for o in new if o.eng == e] for e in ENG_NAMES}

        def run_engine(ename, eng):
            for o in by_eng[ename]:
                for tok, val in o.waits:
                    if tok[0] == "e":
                        eng.wait_ge(esem[tok[1]], val)
                    else:
                        eng.wait_ge(tok[1].dma_sem[tok[3]], val)
                ins = o.fn(eng)
                if o.dma:
                    ins.then_inc(o.dres.dma_sem[o.dkind], 16)
                else:
                    ins.then_inc(esem[ename], 1)
            if ename == "sp":
                for r, kind in touched:
                    eng.wait_ge(r.dma_sem[kind], r.dma_cnt[kind])
                    known["sp"][("d", id(r), kind)] = r.dma_cnt[kind]

        with nc.Block() as block:
            @block.sync
            def _(e):
                run_engine("sp", e)

            @block.tensor
            def _(e):
                run_engine("pe", e)

            @block.scalar
            def _(e):
                run_engine("act", e)

            @block.vector
            def _(e):
                run_engine("dve", e)

            @block.gpsimd
            def _(e):
                run_engine("pool", e)


D = 1024
SEQ = 8192
NT_ALL = SEQ // 128
NSLOT = 16
DIN = 3352
C_KV = 2584
C_Q, C_KS, C_KW, C_KC, C_VC, C_VS, C_VW = 0, 2584, 2712, 2840, 2968, 3096, 3224
C_U, C_V, C_ZA, C_ZB, C_G = 512, 1024, 1536, 2048, 2560
NEGM = -32768.0
SCALE = 0.125
EPS = 1e-6
NSAMP = 16
NPAGE = 16
QHEAD_ORDER = [0, 4, 1, 5, 2, 6, 3, 7]


def _col_perm():
    o = {}
    acc = 0
    for name, sz in (("u", 512), ("v", 512), ("za", 512), ("q", 512), ("kc", 128), ("vc", 128),
                     ("ks", 128), ("vs", 128), ("kw", 128), ("vw", 128), ("g", 24), ("zb", 512)):
        o[name] = (acc, sz)
        acc += sz
    perm = []
    for hd in QHEAD_ORDER:
        perm += list(range(o["q"][0] + hd * 64, o["q"][0] + hd * 64 + 64))
    for name in ("u", "v", "za", "zb", "g", "ks", "kw", "kc", "vc", "vs", "vw"):
        perm += list(range(o[name][0], o[name][0] + o[name][1]))
    return np.array(perm, dtype=np.int64)


def _rope_tables(pos):
    half = 8
    inv = np.power(np.float32(500000.0), -np.arange(half, dtype=np.float32) / np.float32(half)).astype(np.float32)
    ang = pos.astype(np.float32)[:, None] * inv[None, :]
    return np.cos(ang).astype(np.float32), np.sin(ang).astype(np.float32)


def sap(t, p0, pn, off, dims):
    fs = 1
    for d in t.shape[1:]:
        fs *= d
    return bass.AP(t, p0 * fs + off, [[fs, pn]] + [[a, b] for a, b in dims])


ZQ, ZU, ZV, ZZA, ZZB, ZG, ZW = 0, 512, 1024, 1536, 2048, 2560, 2584


def build_program(with_sample=True, n_slots=NSLOT, n_tiles_a=NT_ALL, dbg=(), n_samp=NSAMP, with_prompt=True):
    nc = bass.Bass("TRN2", target_bir_lowering=False)
    dt = nc.dram_tensor

    def din(name, shape, dtype=F32):
        return dt(name, list(shape), dtype, kind="ExternalInput").ap()

    def dout(name, shape, dtype=F32):
        return dt(name, list(shape), dtype, kind="ExternalOutput").ap()

    xb = din("xb", [SEQ, D])
    xo = din("xo", [NSLOT, 128, D])
    w_in = din("w_in", [D, DIN])
    w_out = din("w_out", [D, D])
    final_g = din("final_g", [1, D])
    ln_g = din("ln_g", [1, 512])
    ln_b = din("ln_b", [1, 512])
    w_s = din("w_s", [8, 128, 128])
    w_ck = din("w_ck", [64, 2048])
    w_cv = din("w_cv", [64, 2048])
    pe_ck = din("pe_ck", [64, 32])
    pe_cv = din("pe_cv", [64, 32])
    qposT = din("qposT", [128, NSLOT])
    b_sT = din("b_sT", [128, 8])
    g_col = din("g_col", [128, 8])
    ropeA = din("ropeA", [128, NT_ALL, 16])
    ropeO = din("ropeO", [128, NSLOT, 16])
    ropeC = din("ropeC", [128, 4, 16])
    qpos = din("qpos", [1, NSLOT * 128])
    if with_sample:
        xs = din("xs", [NSAMP, D])
        pt_e = din("pt_e", [128, NSAMP], I32)
        pm8 = din("pm8", [128, 1])
        ropeS = din("ropeS", [128, 16])
        ws00 = din("ws00", [1, 8])
        bs0 = din("bs0", [1, 8])
        fbs = din("fbs", [2, 33])
        bmask = din("bmask", [8, 2])
        pools = [din(nm, [20480, 2048]) for nm in ("kc_pool", "vc_pool", "ks_pool", "vs_pool")]
        kwin = din("kwin", [NSAMP, 512, 128])
        vwin = din("vwin", [NSAMP, 512, 128])
        ys = dout("ys", [NSAMP, D])
        kvs = dout("kvs", [NSAMP, 768])
        vns = dout("vns", [NSAMP, 512])
        kwin_o = dout("kwin_o", [NSAMP, 512, 128])
        vwin_o = dout("vwin_o", [NSAMP, 512, 128])
        scr = dout("scr", [3, 128, 65])
    y_o = dout("y_o", [NSLOT, 128, D])
    kv_all = dout("kv_all", [NT_ALL, 128, 768])

    gst = ExitStack()
    with gst:
        S = Sched(nc, gst)

        def mk_sb(stack):
            def sb(name, shape, dtype=F32):
                t = stack.enter_context(nc.sbuf_tensor(name, list(shape), dtype))
                return t, Res(name)
            return sb

        def mk_ps(stack):
            def ps(name, shape, dtype=F32):
                t = stack.enter_context(nc.psum_tensor(name, list(shape), dtype))
                return t, Res(name)
            return ps

        sb = mk_sb(gst)
        ps = mk_ps(gst)

        WI, rWI = sb("WI", [128, 8, C_KV], BF16)
        Wck, rWck = sb("Wck", [128, 32, 64], BF16)
        Wcv, rWcv = sb("Wcv", [128, 32, 64], BF16)
        identb, rIdb = sb("identb", [128, 128], BF16)
        fgB, rfgB = sb("fgB", [128, D])
        lngB, rlngB = sb("lngB", [128, 512])
        lnbB, rlnbB = sb("lnbB", [128, 512])
        rO, rrO = sb("ropeO_sb", [128, NSLOT, 16])
        rC, rrC = sb("ropeC_sb", [128, 4, 16])
        qposC, rqposC = sb("qposC", [128, NSLOT])
        kpos, rkpos = sb("kpos", [128, NT_ALL])
        cend, rcend = sb("cend", [128, 4])
        cKB, rcKB = sb("cKB", [128, 128])
        cVB, rcVB = sb("cVB", [128, 128])
        wsT, rwsT = sb("wsT", [128, 8, 128], BF16)
        bsT, rbsT = sb("bsT", [128, 8])
        epsT, repsT = sb("epsT", [128, 1])
        gcol, rgcol = sb("gcol", [128, 8])
        stat, rstat = sb("stat", [128, 8])
        rtmp_t, rrtmp = sb("ropetmp", [128, 4 * 96])

        P0, rP0 = ps("P0", [128, 512])
        P1, rP1 = ps("P1", [128, 512])
        S0, rS0 = ps("S0", [128, 512])
        S1, rS1 = ps("S1", [128, 512])
        CC, rCC = ps("CC", [128, 2, 512])
        AS, rAS = ps("AS", [128, 512])
        AW, rAW = ps("AW", [128, 512])
        Pb = [P0.bitcast(BF16), P1.bitcast(BF16)]
        Pf = [P0, P1]
        rP = [rP0, rP1]
        Sb = [S0, S1]
        rSb = [rS0, rS1]
        pctr = [0]

        def nextP():
            i = pctr[0] % 2
            pctr[0] += 1
            return i

        sctr = [0]

        def nextS():
            i = sctr[0] % 2
            sctr[0] += 1
            return i

        with ExitStack() as s0:
            sb0 = mk_sb(s0)
            S.pool(lambda e: e.memset(identb[:], 0.0), writes=[rIdb])
            S.pool(lambda e: e.affine_select(out=identb[:], in_=identb[:], pattern=[[-1, 128]],
                                             compare_op=ALU.not_equal, fill=1.0, base=0,
                                             channel_multiplier=1), reads=[rIdb], writes=[rIdb])
            S.pool(lambda e: e.memset(epsT[:], EPS), writes=[repsT])

            def bload(dst, rdst, src, n):
                S.dma(lambda e: e.dma_start(out=dst[:], in_=bass.AP(src.tensor, 0, [[0, 128], [1, n]])),
                      writes=[rdst])
            bload(fgB, rfgB, final_g, D)
            bload(lngB, rlngB, ln_g, 512)
            bload(lnbB, rlnbB, ln_b, 512)
            S.dma(lambda e: e.dma_start(out=rO[:], in_=ropeO[:, :, :]), writes=[rrO])
            S.dma(lambda e: e.dma_start(out=rC[:], in_=ropeC[:, :, :]), writes=[rrC])
            S.dma(lambda e: e.dma_start(out=qposC[:], in_=qposT[:, :]), writes=[rqposC])
            S.dma(lambda e: e.dma_start(out=bsT[:], in_=b_sT[:, :]), writes=[rbsT])
            S.dma(lambda e: e.dma_start(out=gcol[:], in_=g_col[:, :]), writes=[rgcol])
            wst = [sb0("wstage%d" % i, [128, 1024]) for i in range(2)]
            wi = 0

            def load_w(dst, rdst, col0, ncols, stages, wi):
                for kc in range(8):
                    for c0 in range(0, ncols, 1024):
                        cw = min(1024, ncols - c0)
                        stg, rstg = stages[wi % 2]
                        wi += 1
                        S.dma(lambda e, stg=stg, kc=kc, c0=c0, cw=cw: e.dma_start(
                            out=stg[:, 0:cw], in_=w_in[kc * 128:(kc + 1) * 128, col0 + c0:col0 + c0 + cw]),
                            writes=[rstg])
                        S.dve(lambda e, stg=stg, kc=kc, c0=c0, cw=cw: e.tensor_scalar(
                            out=dst[:, kc, c0:c0 + cw], in0=stg[:, 0:cw], scalar1=gcol[:, kc:kc + 1], scalar2=None,
                            op0=ALU.mult), reads=[rstg, rgcol], writes=[rdst])
                return wi
            if 'no_w' not in dbg:
                wi = load_w(WI, rWI, 0, C_KV, wst, wi)
            for (wsrc, Wc, rWc) in ((w_ck, Wck, rWck), (w_cv, Wcv, rWcv)) if 'no_wc' not in dbg else ():
                for lh in range(2):
                    stg, rstg = wst[wi % 2]
                    wi += 1
                    for h in range(2):
                        S.dma(lambda e, h=h, wsrc=wsrc, stg=stg, lh=lh: e.dma_start(
                            out=stg[64 * h:64 * h + 64, 0:1024],
                            in_=wsrc[:, lh * 1024:(lh + 1) * 1024]), writes=[rstg])
                    S.dve(lambda e, Wc=Wc, stg=stg, lh=lh: e.tensor_copy(
                        out=Wc[:, lh * 16:(lh + 1) * 16, :].rearrange("p a b -> p (a b)"),
                        in_=stg[:, 0:1024]), reads=[rstg], writes=[rWc])
            pe2, rpe2 = sb0("pe2", [128, 2, 32])
            pe2b, rpe2b = sb0("pe2b", [128, 2, 32], BF16)
            crow, rcrow = sb0("crow", [1, 128])
            crow2, rcrow2 = sb0("crow2", [1, 2, 128], BF16)
            ones_f, rones = sb0("ones_f", [1, 128], BF16)
            S.pool(lambda e: e.memset(ones_f[:], 1.0), writes=[rones])
            for i, psrc in enumerate((pe_ck, pe_cv) if 'no_pe' not in dbg else ()):
                for h in range(2):
                    S.dma(lambda e, h=h, i=i, psrc=psrc: e.dma_start(
                        out=pe2[64 * h:64 * h + 64, i, :], in_=psrc[:, :]), writes=[rpe2])
            if 'no_pe' not in dbg:
                S.dve(lambda e: e.tensor_copy(out=pe2b[:], in_=pe2[:]), reads=[rpe2], writes=[rpe2b])
            for i, (Wc, rWc, cB, rcB) in enumerate(((Wck, rWck, cKB, rcKB), (Wcv, rWcv, cVB, rcVB)) if 'no_pe' not in dbg else ()):
                for h in range(2):
                    for l in range(32):
                        S.pe(lambda e, l=l, h=h, Wc=Wc, i=i: e.matmul(
                            Sb[h][0:1, 0:64], lhsT=sap(pe2b, 64 * h, 64, 32 * i + l, [[1, 1]]),
                            rhs=sap(Wc, 64 * h, 64, l * 64, [[1, 64]]), start=(l == 0), stop=(l == 31)),
                            reads=[rpe2b, rWc], writes=[rSb[h]])
                    S.dve(lambda e, h=h: e.tensor_copy(out=crow[:, 64 * h:64 * h + 64], in_=Sb[h][0:1, 0:64]),
                          reads=[rSb[h]], writes=[rcrow])
                S.dve(lambda e: e.tensor_copy(out=crow2[:, 0, :], in_=crow[:]), reads=[rcrow], writes=[rcrow2])
                S.dve(lambda e: e.tensor_tensor(out=crow2[:, 1, :], in0=crow[:], in1=crow2[:, 0, :], op=ALU.subtract),
                      reads=[rcrow, rcrow2], writes=[rcrow2])
                for hl in range(2):
                    S.pe(lambda e, hl=hl: e.matmul(P1[:, 0:128], lhsT=ones_f[:], rhs=crow2[:, hl, :], start=(hl == 0), stop=(hl == 1)),
                         reads=[rones, rcrow2], writes=[rP1])
                S.dve(lambda e, cB=cB: e.tensor_copy(out=cB[:], in_=P1[:, 0:128]), reads=[rP1], writes=[rcB])
            wsl, rwsl = sb0("wsl", [128, 8, 128])
            wslb, rwslb = sb0("wslb", [128, 8, 128], BF16)
            if 'no_ws' not in dbg:
                S.dma(lambda e: e.dma_start(out=wsl[:], in_=bass.AP(w_s.tensor, 0, [[128, 128], [128 * 128, 8], [1, 128]])),
                      writes=[rwsl])
                S.pool(lambda e: e.affine_select(out=wsl[:], in_=wsl[:], pattern=[[0, 8], [-1, 128]],
                                                 compare_op=ALU.is_ge, fill=0.0, base=0, channel_multiplier=1),
                       reads=[rwsl], writes=[rwsl])
                S.dve(lambda e: e.tensor_copy(out=wslb[:], in_=wsl[:]), reads=[rwsl], writes=[rwslb])
                for g in range(8):
                    i = g % 2
                    S.pe(lambda e, g=g, i=i: e.transpose(out=Pb[i][:, 0:128], in_=wslb[:, g, :], identity=identb[:]),
                         reads=[rwslb, rIdb], writes=[rP[i]])
                    S.act(lambda e, g=g, i=i: e.copy(out=wsT[:, g, :], in_=Pb[i][:, 0:128]), reads=[rP[i]], writes=[rwsT])
            S.pool(lambda e: e.iota(kpos[:], pattern=[[128, NT_ALL]], base=0, channel_multiplier=1,
                                    allow_small_or_imprecise_dtypes=True), writes=[rkpos])
            S.pool(lambda e: e.iota(cend[:], pattern=[[2048, 4]], base=31, channel_multiplier=16,
                                    allow_small_or_imprecise_dtypes=True), writes=[rcend])
            S.flush()

        def rms_rstd(xt, rxt, junk, rjunk, part="all"):
            if part == "act":
                S.act(lambda e: e.activation(out=stat[:, 1:2], in_=stat[:, 3:4], func=AF.Ln,
                                             bias=epsT[:], scale=1.0 / D), reads=[rstat, repsT], writes=[rstat])
                S.act(lambda e: e.activation(out=stat[:, 2:3], in_=stat[:, 1:2], func=AF.Exp,
                                             scale=-0.5), reads=[rstat], writes=[rstat])
                return
            if part == "dve":
                S.dve(lambda e: e.scalar_tensor_tensor(out=junk[:], in0=xt[:], scalar=1.0, in1=xt[:], op0=ALU.mult,
                                                       op1=ALU.mult, accum_out=stat[:, 3:4]),
                      reads=[rxt], writes=[rjunk, rstat])
                return
            if "no_rms" in dbg:
                S.dve(lambda e: e.memset(stat[:, 0:3], 1.0), writes=[rstat])
                return
            if "no_accum" in dbg:
                S.dve(lambda e: e.tensor_tensor(out=junk[:], in0=xt[:], in1=xt[:], op=ALU.mult), reads=[rxt], writes=[rjunk])
                S.dve(lambda e: e.memset(stat[:, 0:1], 1024.0), writes=[rstat])
                S.act(lambda e: e.activation(out=stat[:, 1:2], in_=stat[:, 0:1], func=AF.Ln,
                                             bias=epsT[:], scale=1.0 / D), reads=[rstat, repsT], writes=[rstat])
                S.act(lambda e: e.activation(out=stat[:, 2:3], in_=stat[:, 1:2], func=AF.Exp,
                                             scale=-0.5), reads=[rstat], writes=[rstat])
                return
            S.dve(lambda e: e.scalar_tensor_tensor(out=junk[:], in0=xt[:], scalar=1.0, in1=xt[:], op0=ALU.mult,
                                                   op1=ALU.mult, accum_out=stat[:, 0:1]),
                  reads=[rxt], writes=[rjunk, rstat])
            S.act(lambda e: e.activation(out=stat[:, 1:2], in_=stat[:, 0:1], func=AF.Ln,
                                         bias=epsT[:], scale=1.0 / D), reads=[rstat, repsT], writes=[rstat])
            S.act(lambda e: e.activation(out=stat[:, 2:3], in_=stat[:, 1:2], func=AF.Exp,
                                         scale=-0.5), reads=[rstat], writes=[rstat])

        def rope(zt, rzt, c0, nh, tab, rtab, tcol):
            if "no_rope" in dbg:
                return
            x1 = sap(zt, 0, 128, c0, [[64, nh], [1, 8]])
            x2 = sap(zt, 0, 128, c0 + 8, [[64, nh], [1, 8]])
            cos = sap(tab, 0, 128, tcol * 16, [[0, nh], [1, 8]])
            sin = sap(tab, 0, 128, tcol * 16 + 8, [[0, nh], [1, 8]])
            t = [sap(rtmp_t, 0, 128, i * 96, [[8, nh], [1, 8]]) for i in range(4)]
            S.dve(lambda e: e.tensor_tensor(out=t[0], in0=x1, in1=cos, op=ALU.mult), reads=[rzt, rtab], writes=[rrtmp])
            S.dve(lambda e: e.tensor_tensor(out=t[1], in0=x2, in1=sin, op=ALU.mult), reads=[rzt, rtab], writes=[rrtmp])
            S.dve(lambda e: e.tensor_tensor(out=t[2], in0=x2, in1=cos, op=ALU.mult), reads=[rzt, rtab], writes=[rrtmp])
            S.dve(lambda e: e.tensor_tensor(out=t[3], in0=x1, in1=sin, op=ALU.mult), reads=[rzt, rtab], writes=[rrtmp])
            S.dve(lambda e: e.tensor_tensor(out=x1, in0=t[0], in1=t[1], op=ALU.subtract), reads=[rrtmp], writes=[rzt])
            S.dve(lambda e: e.tensor_tensor(out=x2, in0=t[2], in1=t[3], op=ALU.add), reads=[rrtmp], writes=[rzt])

        if with_sample:
          with ExitStack() as scs:
            sbc = mk_sb(scs)
            WOc, rWOc = sbc("WOc", [128, 8, D], BF16)
            WKVc, rWKVc = sbc("WKVc", [128, 8, 768], BF16)
            xS, rxS = sbc("xS", [128, D])
            zS, rzS = sbc("zS", [128, DIN])
            hnS, rhnS = sbc("hnS", [128, D], BF16)
            hnTS, rhnTS = sbc("hnTS", [128, 8, 128], BF16)
            mixS, rmixS = sbc("mixS", [128, D])
            stgc = [sbc("stgc%d" % i, [128, 1024]) for i in range(2)]
            qbS, rqbS = sbc("qbS", [128, 512], BF16)
            qTzz, rqTzz = sbc("qTzz", [128, 2, 4, 128], BF16)
            gateS, rgateS = sbc("gateS", [128, 24])
            ropeS_sb, rropeS = sbc("ropeS_sb", [128, 1, 16])
            ws00B, rws00B = sbc("ws00B", [128, 8])
            bs0B, rbs0B = sbc("bs0B", [128, 8])
            pte, rpte = sbc("pte", [128, NSAMP], I32)
            pm8t, rpm8 = sbc("pm8t", [128, 1])
            idxf, ridxf = sbc("idxf", [128, NSAMP])
            idxi, ridxi = sbc("idxi", [128, NSAMP], I32)
            fbs_t, rfbs = sbc("fbs_t", [2, 33])
            bmask_t, rbmask = sbc("bmask_t", [8, 2])
            maskw, rmaskw = sbc("maskw", [128, 4, 8])
            Ps32, rPs32 = sbc("Ps32", [128, 16, 8])
            Pw32, rPw32 = sbc("Pw32", [128, 4, 8])
            onesc, ronesc = sbc("onesc", [128, 1], BF16)
            identf, ridf = sbc("identf", [128, 128])
            bnS, rbnS = sbc("bnS", [128, 8])
            prodS, rprodS = mixS[:, 512:1024], rmixS
            pnew, rpnew = sbc("pnew", [128, 2, 8])
            G = [sbc("G%d" % i, [128, 16, 128]) for i in range(4)]
            XB = [sbc("XB%d" % i, [128, 16, 128], BF16) for i in range(3)]
            XT = [sbc("XT%d" % i, [128, 16, 128], BF16) for i in range(3)]
            vsA, rvsA = sbc("vsA", [128, 16, 2, 65], BF16)
            GW = [sbc("GW%d" % i, [128, 4, 128]) for i in range(2)]
            kwB, rkwB = sbc("kwB", [128, 4, 128], BF16)
            kwT, rkwT = sbc("kwT", [128, 4, 128], BF16)
            vwA, rvwA = sbc("vwA", [128, 4, 2, 65], BF16)
            kccf, rkccf2 = sbc("kccf_s", [128, 128])
            kccb, rkccb2 = sbc("kccb_s", [128, 128], BF16)
            kccTs, rkccTs = sbc("kccTs", [128, 128], BF16)
            VCs, rVCs = sbc("VCs", [128, 2, 98], BF16)
            ovt, rovt = sbc("ovt", [128, 33])
            ovt2, rovt2 = sbc("ovt2", [128, 33])
            ovs, rovs = sbc("ovs", [128, 33], BF16)
            PcTs, rPcTs = sbc("PcTs", [128, 8], BF16)
            sm8, rsm8 = sbc("sm8", [8, 4])
            Rm, rRm = sbc("Rm", [8, 2], BF16)
            Pm, rPm = sbc("Pm", [8, 128], BF16)
            Pn2s, rPn2s = sbc("Pn2s", [128, 2], BF16)
            impS, rimpS = sbc("impS", [2, 33])
            impS2, rimpS2 = sbc("impS2", [2, 33])
            mxS, rmxS = sbc("mxS", [2, 16])
            m01, rm01 = sbc("m01", [2, 33])
            mexp, rmexp = sbc("mexp", [2, 128], BF16)
            maskTs, rmaskTs = sbc("maskTs", [128, 2])
            PsTs, rPsTs = sbc("PsTs", [128, 16, 8], BF16)
            PwTs, rPwTs = sbc("PwTs", [128, 4, 8], BF16)
            OsT, rOsT = sbc("OsT", [65, 128])
            Osb, rOsb = sbc("Osb", [128, 65])
            Otok, rOtok = sbc("Otok", [128, 3, 8, 65])
            rdS, rrdS = sbc("rdS", [128, 8])
            facS, rfacS = sbc("facS", [128, 8])
            XBK, rXBK = CC[:, 1, :], Res("XBK")
            CCc, rCCc = CC[:, 0, :], Res("CCc")

            S.pool(lambda e: e.memset(xS[:], 0.0), writes=[rxS])
            S.dma(lambda e: e.dma_start(out=xS[0:NSAMP, :], in_=xs[:, :]), reads=[], writes=[rxS])
            for kc in range(8):
                stg, rstg = stgc[kc % 2]
                S.dma(lambda e, stg=stg, kc=kc: e.dma_start(out=stg[:], in_=w_out[kc * 128:(kc + 1) * 128, :]), writes=[rstg])
                S.dve(lambda e, stg=stg, kc=kc: e.tensor_copy(out=WOc[:, kc, :], in_=stg[:]), reads=[rstg], writes=[rWOc])
            load_w(WKVc, rWKVc, C_KV, 768, stgc, 0)
            S.dma(lambda e: e.dma_start(out=ropeS_sb[:, 0, :], in_=ropeS[:, :]), writes=[rropeS])
            S.dma(lambda e: e.dma_start(out=ws00B[:], in_=bass.AP(ws00.tensor, 0, [[0, 128], [1, 8]])), writes=[rws00B])
            S.dma(lambda e: e.dma_start(out=bs0B[:], in_=bass.AP(bs0.tensor, 0, [[0, 128], [1, 8]])), writes=[rbs0B])
            S.dma(lambda e: e.dma_start(out=pte[:], in_=pt_e[:, :]), writes=[rpte])
            S.dma(lambda e: e.dma_start(out=pm8t[:], in_=pm8[:, :]), writes=[rpm8])
            S.dma(lambda e: e.dma_start(out=fbs_t[:], in_=fbs[:, :]), writes=[rfbs])
            S.dma(lambda e: e.dma_start(out=bmask_t[:], in_=bmask[:, :]), writes=[rbmask])
            S.dve(lambda e: e.tensor_copy(out=idxf[:], in_=pte[:]), reads=[rpte], writes=[ridxf])
            S.dve(lambda e: e.tensor_scalar(out=idxf[:], in0=idxf[:], scalar1=8.0, scalar2=pm8t[:, 0:1], op0=ALU.mult, op1=ALU.add),
                  reads=[ridxf, rpm8], writes=[ridxf])
            S.dve(lambda e: e.tensor_copy(out=idxi[:], in_=idxf[:]), reads=[ridxf], writes=[ridxi])
            S.pool(lambda e: e.memset(maskw[:], 1.0), writes=[rmaskw])
            S.pool(lambda e: e.memset(maskw[0:1, 0, :], 0.0), reads=[rmaskw], writes=[rmaskw])
            S.pool(lambda e: e.memset(onesc[:], 1.0), writes=[ronesc])
            S.pool(lambda e: e.memset(identf[:], 0.0), writes=[ridf])
            S.pool(lambda e: e.affine_select(out=identf[:], in_=identf[:], pattern=[[-1, 128]], compare_op=ALU.not_equal,
                                             fill=1.0, base=0, channel_multiplier=1), reads=[ridf], writes=[ridf])
            S.pool(lambda e: e.memset(qTzz[:], 0.0), writes=[rqTzz])
            S.pool(lambda e: e.memset(vsA[:, :, :, 64:65], 1.0), writes=[rvsA])
            S.pool(lambda e: e.memset(vwA[:, :, :, 64:65], 1.0), writes=[rvwA])
            S.pool(lambda e: e.memset(VCs[:], 0.0), writes=[rVCs])
            S.pool(lambda e: e.memset(VCs[:, :, 64:65], 1.0), reads=[rVCs], writes=[rVCs])
            S.pool(lambda e: e.memset(kccf[:], 0.0), writes=[rkccf2])
            S.pool(lambda e: e.memset(Otok[:], 1.0), writes=[rOtok])
            zc, rzc = sbc("zc", [128, 128], BF16)
            S.pool(lambda e: e.memset(zc[:], 0.0), writes=[rzc])
            for (ACC_, rACC_) in ((CCc, rCCc), (AS, rAS), (AW, rAW)):
                S.pe(lambda e, ACC_=ACC_: e.matmul(ACC_[:, 0:128], lhsT=zc[:], rhs=zc[:], start=True, stop=True), reads=[rzc], writes=[rACC_])
            S.pool(lambda e: e.iota(ovt[:], pattern=[[-64, 33]], base=0, channel_multiplier=16,
                                    allow_small_or_imprecise_dtypes=True), writes=[rovt])
            S.dve(lambda e: e.tensor_scalar(out=ovt2[:], in0=ovt[:], scalar1=-32.0, scalar2=None, op0=ALU.is_gt),
                  reads=[rovt], writes=[rovt2])
            S.dve(lambda e: e.scalar_tensor_tensor(out=ovt[:], in0=ovt[:], scalar=64.0, in1=ovt2[:], op0=ALU.is_lt, op1=ALU.mult),
                  reads=[rovt, rovt2], writes=[rovt])
            S.dve(lambda e: e.tensor_copy(out=ovs[:], in_=ovt[:]), reads=[rovt], writes=[rovs])
            for h in range(2):
                S.dve(lambda e, h=h: e.tensor_copy(out=VCs[:, h, 65:98], in_=ovt[:]), reads=[rovt], writes=[rVCs])

            rms_rstd(xS, rxS, hnS, rhnS)
            S.dve(lambda e: e.tensor_scalar(out=hnS[:], in0=xS[:], scalar1=stat[:, 2:3], scalar2=None, op0=ALU.mult),
                  reads=[rxS, rstat], writes=[rhnS])
            i = nextP()
            for kc in range(8):
                S.pe(lambda e, kc=kc, i=i: e.transpose(out=Pb[i][:, kc * 128:(kc + 1) * 128], in_=hnS[:, kc * 128:(kc + 1) * 128],
                                                       identity=identb[:]), reads=[rhnS, rIdb], writes=[rP[i]])
            S.act(lambda e, i=i: e.copy(out=hnTS[:].rearrange("p a b -> p (a b)"), in_=Pb[i][:, :]), reads=[rP[i]], writes=[rhnTS])
            blocks = [(WI, rWI, 512 * k, 512, 512 * k) for k in range(5)] + [(WI, rWI, C_G, 24, ZG)] + \
                     [(WKVc, rWKVc, 0, 512, C_KV), (WKVc, rWKVc, 512, 256, C_KV + 512)]
            for bi, (W_, rW_, c0, cw, z0) in enumerate(blocks):
                i = nextP()
                for kc in range(8):
                    S.pe(lambda e, kc=kc, i=i, c0=c0, cw=cw, W_=W_: e.matmul(Pf[i][:, 0:cw], lhsT=hnTS[:, kc, :], rhs=W_[:, kc, c0:c0 + cw],
                                                                            start=(kc == 0), stop=(kc == 7)),
                         reads=[rhnTS, rW_], writes=[rP[i]])
                if bi % 2 == 0:
                    S.act(lambda e, i=i, z0=z0, cw=cw: e.copy(out=zS[:, z0:z0 + cw], in_=Pf[i][:, 0:cw]), reads=[rP[i]], writes=[rzS])
                else:
                    S.dve(lambda e, i=i, z0=z0, cw=cw: e.tensor_copy(out=zS[:, z0:z0 + cw], in_=Pf[i][:, 0:cw]), reads=[rP[i]], writes=[rzS])
            rope(zS, rzS, 0, 8, ropeS_sb, rropeS, 0)
            rope(zS, rzS, C_KV, 4, ropeS_sb, rropeS, 0)
            S.act(lambda e: e.copy(out=qbS[:], in_=zS[:, 0:512]), reads=[rzS], writes=[rqbS])
            i = nextP()
            for a in range(4):
                S.pe(lambda e, a=a, i=i: e.transpose(out=Pb[i][:, a * 128:(a + 1) * 128], in_=qbS[:, a * 128:(a + 1) * 128],
                                                     identity=identb[:]), reads=[rqbS, rIdb], writes=[rP[i]])
            for h in range(2):
                S.act(lambda e, i=i, h=h: e.copy(out=qTzz[64 * h:64 * h + 64, h, :, :],
                                                 in_=Pb[i][64 * h:64 * h + 64, 0:512].rearrange("p (a b) -> p a b", a=4)),
                      reads=[rP[i]], writes=[rqTzz])
            S.act(lambda e: e.activation(out=gateS[:], in_=zS[:, ZG:ZG + 24], func=AF.Exp, scale=-1.0), reads=[rzS], writes=[rgateS])
            S.dve(lambda e: e.tensor_scalar(out=gateS[:], in0=gateS[:], scalar1=1.0, scalar2=None, op0=ALU.add), reads=[rgateS], writes=[rgateS])
            S.dve(lambda e: e.reciprocal(out=gateS[:], in_=gateS[:]), reads=[rgateS], writes=[rgateS])
            S.act(lambda e: e.activation(out=mixS[:], in_=zS[:, ZZA:ZZA + 1024], func=AF.Exp, scale=-1.0), reads=[rzS], writes=[rmixS])
            S.dve(lambda e: e.tensor_scalar(out=mixS[:], in0=mixS[:], scalar1=1.0, scalar2=None, op0=ALU.add), reads=[rmixS], writes=[rmixS])
            S.dve(lambda e: e.reciprocal(out=mixS[:], in_=mixS[:]), reads=[rmixS], writes=[rmixS])
            S.dve(lambda e: e.tensor_tensor(out=zS[:, ZZA:ZZA + 1024], in0=mixS[:], in1=zS[:, ZZA:ZZA + 1024], op=ALU.mult),
                  reads=[rmixS, rzS], writes=[rzS])
            zvS = zS[:, ZV:ZV + 512]
            S.dve(lambda e: e.bn_stats(out=bnS[:, 0:6], in_=zvS), reads=[rzS], writes=[rbnS])
            S.dve(lambda e: e.bn_aggr(out=stat[:, 4:6], in_=bnS[:, 0:6]), reads=[rbnS], writes=[rstat])
            S.act(lambda e: e.activation(out=stat[:, 6:7], in_=stat[:, 5:6], func=AF.Ln, bias=epsT[:], scale=1.0), reads=[rstat, repsT], writes=[rstat])
            S.act(lambda e: e.activation(out=stat[:, 7:8], in_=stat[:, 6:7], func=AF.Exp, scale=-0.5), reads=[rstat], writes=[rstat])
            S.dve(lambda e: e.tensor_scalar(out=zvS, in0=zvS, scalar1=stat[:, 4:5], scalar2=stat[:, 7:8], op0=ALU.subtract, op1=ALU.mult),
                  reads=[rzS, rstat], writes=[rzS])
            S.dve(lambda e: e.tensor_tensor(out=zvS, in0=zvS, in1=lngB[:], op=ALU.mult), reads=[rzS, rlngB], writes=[rzS])
            S.dve(lambda e: e.tensor_tensor(out=zvS, in0=zvS, in1=lnbB[:], op=ALU.add), reads=[rzS, rlnbB], writes=[rzS])
            S.dma(lambda e: e.dma_start(out=kvs[:, :], in_=zS[0:NSAMP, C_KV:C_KV + 768]), reads=[rzS], q="pool")
            kvdr = Res("dram_win", dram=True)
            S.dma(lambda e: e.dma_start(out=kwin_o[:, 0:511, :], in_=kwin[:, 1:512, :]), reads=[kvdr], writes=[])
            S.dma(lambda e: e.dma_start(out=vwin_o[:, 0:511, :], in_=vwin[:, 1:512, :]), reads=[kvdr], writes=[])
            S.dma(lambda e: e.dma_start(out=kwin_o[:, 511, :], in_=zS[0:NSAMP, C_KW:C_KW + 128]), reads=[rzS], q="pool")
            S.dma(lambda e: e.dma_start(out=vwin_o[:, 511, :], in_=zS[0:NSAMP, C_VW:C_VW + 128]), reads=[rzS], q="pool")
            S.dma(lambda e: e.dma_start(out=vns[:, :], in_=zS[0:NSAMP, ZV:ZV + 512]), reads=[rzS], q="pool")
            for g in range(8):
                S.dve(lambda e, g=g: e.tensor_scalar(out=mixS[:, g * 64:(g + 1) * 64], in0=zS[:, ZV + g * 64:ZV + (g + 1) * 64],
                                                     scalar1=ws00B[:, g:g + 1], scalar2=bs0B[:, g:g + 1], op0=ALU.mult, op1=ALU.add),
                      reads=[rzS, rws00B, rbs0B], writes=[rmixS])
            S.dve(lambda e: e.tensor_tensor(out=mixS[:, 0:512], in0=mixS[:, 0:512], in1=zS[:, ZU:ZU + 512], op=ALU.mult),
                  reads=[rmixS, rzS], writes=[rmixS])
            for wi_, kcol in enumerate((C_KS, C_KW)):
                S.dve(lambda e, kcol=kcol: e.tensor_tensor(
                    out=prodS.rearrange("p (a h d) -> p a h d", a=4, h=2),
                    in0=zS[:, 0:512].rearrange("p (a h d) -> p a h d", a=4, h=2),
                    in1=sap(zS, 0, 128, kcol, [[0, 4], [64, 2], [1, 64]]), op=ALU.mult), reads=[rzS], writes=[rprodS])
                S.dve(lambda e, wi_=wi_: e.tensor_reduce(
                    out=sap(pnew, 0, 128, wi_ * 8, [[1, 4], [4, 2]]),
                    in_=prodS.rearrange("p (a h d) -> p a h d", a=4, h=2), axis=mybir.AxisListType.X, op=ALU.add),
                    reads=[rprodS], writes=[rpnew])
            S.act(lambda e: e.activation(out=pnew[:], in_=pnew[:], func=AF.Exp, scale=SCALE), reads=[rpnew], writes=[rpnew])

            for b in range(n_samp):
                qbd = sap(qTzz, 0, 128, b, [[512, 2], [128, 4]])
                for ci in range(4):
                    g_, rg_ = G[ci]
                    S.dma(lambda e, ci=ci, g_=g_, b=b: e.indirect_dma_start(
                        out=g_[:].rearrange("p a b -> p (a b)"), out_offset=None, in_=pools[ci][:, :],
                        in_offset=bass.IndirectOffsetOnAxis(ap=idxi[:, b:b + 1], axis=0)),
                        reads=[ridxi], writes=[rg_], q="pool")
                for wi_, wsrc in enumerate((kwin, vwin)):
                    gw, rgw = GW[wi_]
                    S.dma(lambda e, gw=gw, wsrc=wsrc, b=b: e.dma_start(
                        out=gw[:], in_=wsrc[b, :, :].rearrange("(p c) f -> p c f", c=4)), writes=[rgw])
                for ci in range(3):
                    S.dve(lambda e, ci=ci: e.tensor_copy(out=XB[ci][0][:], in_=G[ci][0][:]), reads=[G[ci][1]], writes=[XB[ci][1]])
                S.act(lambda e: e.copy(out=vsA[:, :, :, 0:64], in_=G[3][0][:].rearrange("p c (h d) -> p c h d", h=2)),
                      reads=[G[3][1]], writes=[rvsA])
                S.dve(lambda e: e.tensor_copy(out=kwB[:], in_=GW[0][0][:]), reads=[GW[0][1]], writes=[rkwB])
                S.act(lambda e: e.copy(out=vwA[:, :, :, 0:64], in_=GW[1][0][:].rearrange("p c (h d) -> p c h d", h=2)),
                      reads=[GW[1][1]], writes=[rvwA])
                for ci in range(3):
                    for half in range(2):
                        i = nextP()
                        for cc in range(8):
                            c = half * 8 + cc
                            S.pe(lambda e, ci=ci, c=c, cc=cc, i=i: e.transpose(out=Pb[i][:, cc * 128:(cc + 1) * 128], in_=XB[ci][0][:, c, :],
                                                                                identity=identb[:]), reads=[XB[ci][1], rIdb], writes=[rP[i]])
                        S.act(lambda e, ci=ci, half=half, i=i: e.copy(
                            out=XT[ci][0][:, half * 8:(half + 1) * 8, :].rearrange("p a b -> p (a b)"), in_=Pb[i][:, :]),
                            reads=[rP[i]], writes=[XT[ci][1]])
                i = nextP()
                for c in range(4):
                    S.pe(lambda e, c=c, i=i: e.transpose(out=Pb[i][:, c * 128:(c + 1) * 128], in_=kwB[:, c, :], identity=identb[:]),
                         reads=[rkwB, rIdb], writes=[rP[i]])
                S.act(lambda e, i=i: e.copy(out=kwT[:].rearrange("p a b -> p (a b)"), in_=Pb[i][:, 0:512]), reads=[rP[i]], writes=[rkwT])
                for kv in range(2):
                    Wc_, rWc_ = (Wck, rWck) if kv == 0 else (Wcv, rWcv)
                    for h in range(2):
                        for l in range(32):
                            c, sh = l % 16, l // 16
                            S.pe(lambda e, l=l, c=c, sh=sh, h=h, kv=kv, Wc_=Wc_: e.matmul(
                                Sb[h][0:127, 0:64], lhsT=sap(XT[kv][0], 64 * h, 64, c * 128 + sh, [[1, 127]]),
                                rhs=sap(Wc_, 64 * h, 64, l * 64, [[1, 64]]), start=(l == 0), stop=(l == 31)),
                                reads=[XT[kv][1], rWc_], writes=[rSb[h]])
                    for h in range(2):
                        if kv == 0:
                            S.dve(lambda e, h=h: e.tensor_tensor(out=kccf[0:127, 64 * h:64 * h + 64], in0=Sb[h][0:127, 0:64],
                                                                 in1=cKB[0:127, 64 * h:64 * h + 64], op=ALU.add),
                                  reads=[rSb[h], rcKB], writes=[rkccf2])
                        else:
                            S.dve(lambda e, h=h: e.tensor_tensor(out=VCs[0:127, h, 0:64], in0=Sb[h][0:127, 0:64],
                                                                 in1=cVB[0:127, 64 * h:64 * h + 64], op=ALU.add),
                                  reads=[rSb[h], rcVB], writes=[rVCs])
                rope(kccf, rkccf2, 0, 2, rC, rrC, 0)
                S.dve(lambda e: e.tensor_copy(out=kccb[:], in_=kccf[:]), reads=[rkccf2], writes=[rkccb2])
                i = nextP()
                S.pe(lambda e, i=i: e.transpose(out=Pb[i][:, 0:128], in_=kccb[:], identity=identb[:]), reads=[rkccb2, rIdb], writes=[rP[i]])
                S.act(lambda e, i=i: e.copy(out=kccTs[:], in_=Pb[i][:, 0:128]), reads=[rP[i]], writes=[rkccTs])
                S.pe(lambda e, qbd=qbd: e.matmul(XBK[0:127, 0:8], lhsT=kccTs[:, 0:127], rhs=qbd, start=True, stop=True),
                     reads=[rkccTs, rqTzz], writes=[rXBK])
                S.act(lambda e: e.activation(out=PcTs[0:127, :], in_=XBK[0:127, 0:8], func=AF.Exp, scale=SCALE), reads=[rXBK], writes=[rPcTs])
                for h in range(2):
                    S.pe(lambda e, h=h, b=b: e.matmul(CCc[0:65, b * 8 + 4 * h:b * 8 + 4 * h + 4], lhsT=VCs[0:127, h, 0:65],
                                                     rhs=PcTs[0:127, 4 * h:4 * h + 4], start=True, stop=True),
                         reads=[rVCs, rPcTs], writes=[rCCc])
                S.pe(lambda e: e.matmul(XBK[0:8, 16:17], lhsT=PcTs[0:127, :], rhs=onesc[0:127, :], start=True, stop=True),
                     reads=[rPcTs, ronesc], writes=[rXBK])
                S.dve(lambda e: e.reciprocal(out=sm8[:, 0:1], in_=XBK[0:8, 16:17]), reads=[rXBK], writes=[rsm8])
                S.dve(lambda e: e.tensor_scalar(out=Rm[:], in0=bmask_t[:], scalar1=sm8[:, 0:1], scalar2=None, op0=ALU.mult),
                      reads=[rbmask, rsm8], writes=[rRm])
                i = nextP()
                S.pe(lambda e, i=i: e.transpose(out=Pb[i][0:8, 0:127], in_=PcTs[0:127, :], identity=identb[0:127, 0:127]),
                     reads=[rPcTs, rIdb], writes=[rP[i]])
                S.act(lambda e, i=i: e.copy(out=Pm[:, 0:127], in_=Pb[i][0:8, 0:127]), reads=[rP[i]], writes=[rPm])
                S.pe(lambda e: e.matmul(XBK[0:127, 24:26], lhsT=Pm[:, 0:127], rhs=Rm[:], start=True, stop=True),
                     reads=[rPm, rRm], writes=[rXBK])
                S.act(lambda e: e.copy(out=Pn2s[0:127, :], in_=XBK[0:127, 24:26]), reads=[rXBK], writes=[rPn2s])
                S.pe(lambda e: e.matmul(XBK[0:2, 32:65], lhsT=Pn2s[0:127, :], rhs=ovs[0:127, :], start=True, stop=True),
                     reads=[rPn2s, rovs], writes=[rXBK])
                S.dve(lambda e: e.tensor_tensor(out=impS[:], in0=XBK[0:2, 32:65], in1=fbs_t[:], op=ALU.add), reads=[rXBK, rfbs], writes=[rimpS])
                S.dve(lambda e: e.max(out=mxS[:, 0:8], in_=impS[:]), reads=[rimpS], writes=[rmxS])
                S.dve(lambda e: e.match_replace(out=impS2[:], in_to_replace=mxS[:, 0:8], in_values=impS[:], imm_value=-3e38),
                      reads=[rimpS, rmxS], writes=[rimpS2])
                S.dve(lambda e: e.max(out=mxS[:, 8:16], in_=impS2[:]), reads=[rimpS2], writes=[rmxS])
                S.dve(lambda e: e.tensor_scalar(out=m01[:], in0=impS[:], scalar1=mxS[:, 15:16], scalar2=None, op0=ALU.is_ge),
                      reads=[rimpS, rmxS], writes=[rm01])
                for r4 in range(4):
                    S.dve(lambda e, r4=r4: e.tensor_copy(out=sap(mexp, 0, 2, r4, [[4, 32]]), in_=m01[:, 0:32]), reads=[rm01], writes=[rmexp])
                S.pe(lambda e: e.matmul(XBK[:, 72:74], lhsT=mexp[:, :], rhs=identb[0:2, 0:2], start=True, stop=True),
                     reads=[rmexp, rIdb], writes=[rXBK])
                S.dve(lambda e: e.tensor_copy(out=maskTs[:], in_=XBK[:, 72:74]), reads=[rXBK], writes=[rmaskTs])
                i = nextP()
                for c in range(16):
                    S.pe(lambda e, c=c, i=i, qbd=qbd: e.matmul(Pf[i][:, c * 8:(c + 1) * 8], lhsT=XT[2][0][:, c, :], rhs=qbd, start=True, stop=True),
                         reads=[XT[2][1], rqTzz], writes=[rP[i]])
                S.act(lambda e, i=i: e.activation(out=Ps32[:].rearrange("p a b -> p (a b)"), in_=Pf[i][:, 0:128], func=AF.Exp, scale=SCALE),
                      reads=[rP[i]], writes=[rPs32])
                for h in range(2):
                    S.dve(lambda e, h=h: e.tensor_scalar(out=PsTs[:, :, 4 * h:4 * h + 4], in0=Ps32[:, :, 4 * h:4 * h + 4],
                                                         scalar1=maskTs[:, h:h + 1], scalar2=None, op0=ALU.mult),
                          reads=[rPs32, rmaskTs], writes=[rPsTs])
                for h in range(2):
                    for c in range(16):
                        S.pe(lambda e, h=h, c=c, b=b: e.matmul(AS[0:65, b * 8 + 4 * h:b * 8 + 4 * h + 4], lhsT=vsA[:, c, h, :],
                                                              rhs=PsTs[:, c, 4 * h:4 * h + 4], start=(c == 0), stop=(c == 15)),
                             reads=[rvsA, rPsTs], writes=[rAS])
                i = nextP()
                for c in range(4):
                    S.pe(lambda e, c=c, i=i, qbd=qbd: e.matmul(Pf[i][:, c * 8:(c + 1) * 8], lhsT=kwT[:, c, :], rhs=qbd, start=True, stop=True),
                         reads=[rkwT, rqTzz], writes=[rP[i]])
                S.act(lambda e, i=i: e.activation(out=Pw32[:].rearrange("p a b -> p (a b)"), in_=Pf[i][:, 0:32], func=AF.Exp, scale=SCALE),
                      reads=[rP[i]], writes=[rPw32])
                S.dve(lambda e: e.tensor_tensor(out=PwTs[:], in0=Pw32[:], in1=maskw[:], op=ALU.mult), reads=[rPw32, rmaskw], writes=[rPwTs])
                for h in range(2):
                    for c in range(4):
                        S.pe(lambda e, h=h, c=c, b=b: e.matmul(AW[0:65, b * 8 + 4 * h:b * 8 + 4 * h + 4], lhsT=vwA[:, c, h, :],
                                                              rhs=PwTs[:, c, 4 * h:4 * h + 4], start=(c == 0), stop=(c == 3)),
                             reads=[rvwA, rPwTs], writes=[rAW])

            rscr = Res("scr_dram", dram=True)
            for br, (ACC, rACC) in enumerate(((CCc, rCCc), (AS, rAS), (AW, rAW))):
                S.act(lambda e, ACC=ACC: e.copy(out=OsT[:, :], in_=ACC[0:65, 0:128]), reads=[rACC], writes=[rOsT])
                i = nextP()
                S.pe(lambda e, i=i: e.transpose(out=Pf[i][:, 0:65], in_=OsT[:, :], identity=identf[0:65, 0:65]),
                     reads=[rOsT, ridf], writes=[rP[i]])
                S.dve(lambda e, i=i: e.tensor_copy(out=Osb[:], in_=Pf[i][:, 0:65]), reads=[rP[i]], writes=[rOsb])
                S.dma(lambda e, br=br: e.dma_start(out=scr[br, :, :], in_=Osb[:]), reads=[rOsb], writes=[rscr])
                S.dma(lambda e, br=br: e.dma_start(out=Otok[0:NSAMP, br, :, :].rearrange("p a b -> p (a b)"),
                                                   in_=scr[br, :, :].rearrange("(b h) d -> b (h d)", h=8)),
                      reads=[rscr], writes=[rOtok])
            for wi_, (br, vcol) in enumerate(((1, C_VS), (2, C_VW))):
                for hd in range(8):
                    S.dve(lambda e, br=br, hd=hd, vcol=vcol, wi_=wi_: e.scalar_tensor_tensor(
                        out=Otok[:, br, hd, 0:64], in0=zS[:, vcol + 64 * (hd // 4):vcol + 64 * (hd // 4) + 64],
                        scalar=pnew[:, wi_, hd:hd + 1], in1=Otok[:, br, hd, 0:64], op0=ALU.mult, op1=ALU.add),
                        reads=[rzS, rpnew, rOtok], writes=[rOtok])
                S.dve(lambda e, br=br, wi_=wi_: e.tensor_tensor(out=Otok[:, br, :, 64], in0=Otok[:, br, :, 64], in1=pnew[:, wi_, :], op=ALU.add),
                      reads=[rOtok, rpnew], writes=[rOtok])
            for br in range(3):
                S.dve(lambda e, br=br: e.tensor_scalar(out=rdS[:], in0=Otok[:, br, :, 64], scalar1=1e-30, scalar2=None, op0=ALU.max),
                      reads=[rOtok], writes=[rrdS])
                S.dve(lambda e: e.reciprocal(out=rdS[:], in_=rdS[:]), reads=[rrdS], writes=[rrdS])
                S.dve(lambda e, br=br: e.tensor_tensor(out=facS[:], in0=rdS[:], in1=sap(gateS, 0, 128, br, [[3, 8]]), op=ALU.mult),
                      reads=[rrdS, rgateS], writes=[rfacS])
                for hd in range(8):
                    dst = mixS[:, 512 + hd * 64:512 + hd * 64 + 64]
                    if br == 0:
                        S.dve(lambda e, hd=hd, dst=dst: e.tensor_scalar(out=dst, in0=Otok[:, 0, hd, 0:64], scalar1=facS[:, hd:hd + 1],
                                                                        scalar2=None, op0=ALU.mult), reads=[rOtok, rfacS], writes=[rmixS])
                    else:
                        S.dve(lambda e, hd=hd, dst=dst, br=br: e.scalar_tensor_tensor(out=dst, in0=Otok[:, br, hd, 0:64], scalar=facS[:, hd:hd + 1],
                                                                                      in1=dst, op0=ALU.mult, op1=ALU.add),
                              reads=[rOtok, rfacS, rmixS], writes=[rmixS])
            S.dve(lambda e: e.tensor_tensor(out=hnS[:], in0=mixS[:], in1=zS[:, ZZA:ZZA + 1024], op=ALU.mult), reads=[rmixS, rzS], writes=[rhnS])
            i = nextP()
            for kc in range(8):
                S.pe(lambda e, kc=kc, i=i: e.transpose(out=Pb[i][:, kc * 128:(kc + 1) * 128], in_=hnS[:, kc * 128:(kc + 1) * 128],
                                                       identity=identb[:]), reads=[rhnS, rIdb], writes=[rP[i]])
            S.act(lambda e, i=i: e.copy(out=hnTS[:].rearrange("p a b -> p (a b)"), in_=Pb[i][:, :]), reads=[rP[i]], writes=[rhnTS])
            for cb in range(2):
                i = nextP()
                for kc in range(8):
                    S.pe(lambda e, kc=kc, i=i, cb=cb: e.matmul(Pf[i][:, :], lhsT=hnTS[:, kc, :], rhs=WOc[:, kc, cb * 512:(cb + 1) * 512],
                                                               start=(kc == 0), stop=(kc == 7)), reads=[rhnTS, rWOc], writes=[rP[i]])
                S.dve(lambda e, i=i, cb=cb: e.tensor_tensor(out=xS[:, cb * 512:(cb + 1) * 512], in0=Pf[i][:, :],
                                                            in1=xS[:, cb * 512:(cb + 1) * 512], op=ALU.add), reads=[rP[i], rxS], writes=[rxS])
            rms_rstd(xS, rxS, hnS, rhnS)
            S.dve(lambda e: e.scalar_tensor_tensor(out=xS[:], in0=xS[:], scalar=stat[:, 2:3], in1=fgB[:], op0=ALU.mult, op1=ALU.mult),
                  reads=[rxS, rstat, rfgB], writes=[rxS])
            S.dma(lambda e: e.dma_start(out=ys[:, :], in_=xS[0:NSAMP, :]), reads=[rxS], q="pool")
            S.flush()

        with ExitStack() as sab:
            if "nosab" in dbg or not with_prompt:
                return nc, dict(S.cnt)
            sbab = mk_sb(sab)
            KT2, rKT2 = sbab("KT2", [128, 2, SEQ], BF16)
            rKT = [[Res("KT_%d_%d" % (i, t)) for t in range(NT_ALL)] for i in range(4)]
            VA, rVAfull = sbab("VA", [128, NT_ALL, 4, 65], BF16)
            kccT, rkccT = sbab("kccT", [128, 4, 128], BF16)
            VC, rVC = sbab("VC", [128, 4, 2, 193], BF16)

            with ExitStack() as sa:
                sba = mk_sb(sa)
                KcT, rKcT = sba("KcT", [128, 2, SEQ], BF16)
                rA, rrA = sba("ropeA_sb", [128, NT_ALL, 16])
                xA = [sba("xA%d" % i, [128, D]) for i in range(2)]
                hnA = [sba("hnA%d" % i, [128, D], BF16) for i in range(2)]
                hnTA = [sba("hnTA%d" % i, [128, 8, 128], BF16) for i in range(2)]
                zkv = [sba("zkv%d" % i, [128, 768]) for i in range(2)]
                kin = [sba("kin%d" % i, [128, 512], BF16) for i in range(2)]
                WKV, rWKV = sba("WKV", [128, 8, 768], BF16)
                load_w(WKV, rWKV, C_KV, 768, xA, 0)
                ovl, rovl = zkv[1][0][:, 0:512].rearrange("p (a b) -> p a b", a=4), zkv[1][1]
                ovl2, rovl2 = zkv[0][0][:, 0:512].rearrange("p (a b) -> p a b", a=4), zkv[0][1]
                kcc_f, rkccf = sba("kcc_f", [128, 128])
                kcc_b, rkccb = sba("kcc_b", [128, 128], BF16)
                S.dma(lambda e: e.dma_start(out=rA[:], in_=ropeA[:, :, :]), writes=[rrA])
                S.pool(lambda e: e.memset(VC[:], 0.0), writes=[rVC])
                S.pool(lambda e: e.memset(kccT[:], 0.0), writes=[rkccT])
                S.pool(lambda e: e.iota(ovl, pattern=[[2048, 4], [-64, 128]], base=0, channel_multiplier=16,
                                        allow_small_or_imprecise_dtypes=True), writes=[rovl])
                S.dve(lambda e: e.tensor_scalar(out=ovl2, in0=ovl, scalar1=-32.0, scalar2=None, op0=ALU.is_gt),
                      reads=[rovl], writes=[rovl2])
                S.dve(lambda e: e.scalar_tensor_tensor(out=ovl, in0=ovl, scalar=64.0, in1=ovl2,
                                                       op0=ALU.is_lt, op1=ALU.mult), reads=[rovl, rovl2], writes=[rovl])
                for h in range(2):
                    S.dve(lambda e, h=h: e.tensor_copy(out=VC[:, :, h, 65:193], in_=ovl), reads=[rovl], writes=[rVC])
                S.dve(lambda e: e.memset(VC[:, :, :, 64:65], 1.0), reads=[], writes=[rVC])
                S.dve(lambda e: e.memset(VA[:, :, :, 64:65], 1.0), writes=[rVAfull])
                S.dve(lambda e: e.memset(kcc_f[:], 0.0), writes=[rkccf])
                Sbb = [S0.bitcast(BF16), S1.bitcast(BF16)]
                Abb = [(AS.bitcast(BF16), rAS), (AW.bitcast(BF16), rAW)]

                hnTA3 = hnTA + [sba("hnTA2", [128, 8, 128], BF16)]

                def stage1(t):
                    x_t, rx = xA[t % 2]
                    hn, rhn = hnA[t % 2]
                    hnT, rhnT = hnTA3[t % 3]
                    S.dma(lambda e: e.dma_start(out=x_t[:], in_=xb[t * 128:(t + 1) * 128, :]), writes=[rx])
                    rms_rstd(x_t, rx, hn, rhn)
                    S.dve(lambda e: e.tensor_scalar(out=hn[:], in0=x_t[:], scalar1=stat[:, 2:3], scalar2=None, op0=ALU.mult),
                          reads=[rx, rstat], writes=[rhn])
                    bk, rbk = Sbb[t % 2], rSb[t % 2]
                    for kc in range(8):
                        S.pe(lambda e, kc=kc: e.transpose(out=bk[:, kc * 128:(kc + 1) * 128], in_=hn[:, kc * 128:(kc + 1) * 128],
                                                          identity=identb[:]), reads=[rhn, rIdb], writes=[rbk])
                    S.act(lambda e: e.copy(out=hnT[:].rearrange("p a b -> p (a b)"), in_=bk[:, :]), reads=[rbk], writes=[rhnT])

                def stageMM(t):
                    hnT, rhnT = hnTA3[t % 3]
                    z, rz = zkv[t % 2]
                    for kc in range(8):
                        S.pe(lambda e, kc=kc: e.matmul(P0[:, :], lhsT=hnT[:, kc, :], rhs=WKV[:, kc, 0:512],
                                                       start=(kc == 0), stop=(kc == 7)), reads=[rhnT, rWKV], writes=[rP0])
                    S.act(lambda e: e.copy(out=z[:, 0:512], in_=P0[:, :]), reads=[rP0], writes=[rz])
                    for kc in range(8):
                        S.pe(lambda e, kc=kc: e.matmul(P1[:, 0:256], lhsT=hnT[:, kc, :], rhs=WKV[:, kc, 512:768],
                                                       start=(kc == 0), stop=(kc == 7)), reads=[rhnT, rWKV], writes=[rP1])
                    S.act(lambda e: e.copy(out=z[:, 512:768], in_=P1[:, 0:256]), reads=[rP1], writes=[rz])

                def stageR(t):
                    z, rz = zkv[t % 2]
                    kb, rkb = kin[t % 2]
                    rope(z, rz, 0, 4, rA, rrA, t)
                    S.dma(lambda e: e.dma_start(out=kv_all[t, :, :], in_=z[:]), reads=[rz], q="pool")
                    S.dve(lambda e: e.tensor_copy(out=kb[:], in_=z[:, 0:512]), reads=[rz], writes=[rkb])
                    S.dve(lambda e: e.tensor_copy(out=VA[:, t, :, 0:64], in_=z[:, 512:768].rearrange("p (a b) -> p a b", a=4)),
                          reads=[rz], writes=[rVAfull])

                def stageKT(t):
                    kb, rkb = kin[t % 2]
                    ab, rab = Abb[t % 2]
                    for a in range(4):
                        S.pe(lambda e, a=a: e.transpose(out=ab[:, a * 128:(a + 1) * 128], in_=kb[:, a * 128:(a + 1) * 128],
                                                        identity=identb[:]), reads=[rkb, rIdb], writes=[rab])
                    for a in range(2):
                        S.act(lambda e, a=a: e.copy(out=KT2[:, a, t * 128:(t + 1) * 128], in_=ab[:, a * 128:(a + 1) * 128]),
                              reads=[rab], writes=[rKT[a][t]])
                        S.act(lambda e, a=a: e.copy(out=KcT[:, a, t * 128:(t + 1) * 128], in_=ab[:, 256 + a * 128:256 + (a + 1) * 128]),
                              reads=[rab], writes=[rKT[2 + a][t]])

                for t in range(min(2, n_tiles_a)):
                    stage1(t)
                for t in range(n_tiles_a):
                    stageMM(t)
                    if t + 2 < n_tiles_a:
                        stage1(t + 2)
                    stageR(t)
                    if t >= 1:
                        stageKT(t - 1)
                if n_tiles_a > 0:
                    stageKT(n_tiles_a - 1)

                for N in range(4):
                    if 16 * N + 16 > n_tiles_a - (1 if N < 3 else 0):
                        break
                    nn = 128 if N < 3 else 127
                    rk_dep = [rKT[2][t] for t in range(16 * N, min(16 * N + 17, NT_ALL))]
                    rv_dep = [rKT[3][t] for t in range(16 * N, min(16 * N + 17, NT_ALL))]
                    cbanks = [(Sb[0], rSb[0]), (Sb[1], rSb[1]), (AS, rAS), (AW, rAW)]
                    for kv, (src_i, Wc_, rWc_, rdep) in enumerate(((0, Wck, rWck, rk_dep), (1, Wcv, rWcv, rv_dep))):
                        for h in range(2):
                            bk, rbk = cbanks[2 * kv + h]
                            for l in range(32):
                                S.pe(lambda e, l=l, h=h, N=N, nn=nn, bk=bk, src_i=src_i, Wc_=Wc_: e.matmul(
                                    bk[0:nn, 0:64],
                                    lhsT=sap(KcT, 64 * h, 64, src_i * SEQ + 2048 * N + l, [[16, nn]]),
                                    rhs=sap(Wc_, 64 * h, 64, l * 64, [[1, 64]]), start=(l == 0), stop=(l == 31)),
                                    reads=rdep + [rWc_], writes=[rbk])
                    for h in range(2):
                        bk, rbk = cbanks[h]
                        S.dve(lambda e, h=h, nn=nn, bk=bk: e.tensor_tensor(out=kcc_f[0:nn, 64 * h:64 * h + 64], in0=bk[0:nn, 0:64],
                                                                          in1=cKB[0:nn, 64 * h:64 * h + 64], op=ALU.add),
                              reads=[rbk, rcKB], writes=[rkccf])
                        bk, rbk = cbanks[2 + h]
                        S.dve(lambda e, h=h, nn=nn, N=N, bk=bk: e.tensor_tensor(
                            out=VC[0:nn, N, h, 0:64], in0=bk[0:nn, 0:64], in1=cVB[0:nn, 64 * h:64 * h + 64], op=ALU.add),
                            reads=[rbk, rcVB], writes=[rVC])
                    rope(kcc_f, rkccf, 0, 2, rC, rrC, N)
                    S.dve(lambda e: e.tensor_copy(out=kcc_b[:], in_=kcc_f[:]), reads=[rkccf], writes=[rkccb])
                    i = nextP()
                    S.pe(lambda e, i=i: e.transpose(out=Pb[i][:, 0:128], in_=kcc_b[:], identity=identb[:]),
                         reads=[rkccb, rIdb], writes=[rP[i]])
                    S.act(lambda e, i=i, N=N: e.copy(out=kccT[:, N, :], in_=Pb[i][:, 0:128]), reads=[rP[i]], writes=[rkccT])
                S.flush()

            with ExitStack() as sbs:
                sbb = mk_sb(sbs)
                WO, rWO = sbb("WO", [128, 8, D], BF16)
                EWq, rEWq = sbb("EWq", [128, 2048], BF16)
                xO = [sbb("xO%d" % i, [128, D]) for i in range(2)]
                zO, rzO = sbb("zO", [128, ZW])
                hnX = [sbb("hnX%d" % i, [128, D], BF16) for i in range(2)]
                hnTX = [sbb("hnTX%d" % i, [128, 8, 128], BF16) for i in range(2)]
                qb, rqb = sbb("qb", [128, 512], BF16)
                qTzX = [[sbb("qTz%d_%d" % (j, i), [128, 512], BF16) for i in range(2)] for j in range(2)]
                silB, rsilB = sbb("silB", [128, 512], BF16)
                zb, rzb = sbb("zb", [128, 512], BF16)
                qpsX = [sbb("qps%d" % i, [128, 128]) for i in range(2)]
                cbcX = [sbb("cbc%d" % i, [128, 4, 128], BF16) for i in range(2)]
                cbsX = [sbb("cbs%d" % i, [128, 4, 128], BF16) for i in range(2)]
                cbwX = [sbb("cbw%d" % i, [128, 8, 128], BF16) for i in range(2)]
                wtmp, rwtmp = sbb("wtmp", [128, 128])
                PsT = [sbb("PsT%d" % i, [128, 512], BF16) for i in range(3)]
                rden, rrden = sbb("rden", [128, 16])
                fac, rfac = sbb("fac", [128, 4])
                imp, rimp = sbb("imp", [128, 128])
                imp2, rimp2 = wtmp, rwtmp
                mx8, rmx8 = sbb("mx8", [128, 16])
                nmask, rnmask = sbb("nmask", [128, 128], BF16)
                nmaskT = [[sbb("nmaskT%d_%d" % (i, gq), [128, 128], BF16) for gq in range(4)] for i in range(2)]
                gateX = [sbb("gate%d" % i, [128, 24]) for i in range(2)]
                mixf, rmixf = sbb("mixf", [128, D])
                rsg = Res("mixf_gmlp_half")
                vnb, rvnb = qb, rqb
                bnst, rbnst = sbb("bnst", [128, 8])
                sil, rsil = zO[:, ZZA:ZZA + 1024], rzO
                k64, rk64 = sbb("k64", [128, 128])
                e0, re0 = sbb("e0", [128, 128])
                ddt, rddt = sbb("ddt", [128, 128])
                fa, rfa = sbb("fa", [128, 128])
                ff, rff = sbb("ff", [128, 128])
                fbX = [sbb("fb%d" % i, [128, 128]) for i in range(2)]
                wst = xO
                for kc in range(8):
                    stg, rstg = wst[kc % 2]
                    S.dma(lambda e, stg=stg, kc=kc: e.dma_start(out=stg[:], in_=w_out[kc * 128:(kc + 1) * 128, :]),
                          writes=[rstg])
                    S.dve(lambda e, stg=stg, kc=kc: e.tensor_copy(out=WO[:, kc, :], in_=stg[:]),
                           reads=[rstg], writes=[rWO])
                S.pool(lambda e: e.memset(EWq[:], 1.0), writes=[rEWq])
                for gq in range(4):
                    S.pool(lambda e, gq=gq: e.affine_select(out=EWq[32 * gq:32 * gq + 32, :], in_=EWq[32 * gq:32 * gq + 32, :],
                                                            pattern=[[1, 2048]], compare_op=ALU.is_ge, fill=0.0, base=0,
                                                            channel_multiplier=-64), reads=[rEWq], writes=[rEWq])
                    S.pool(lambda e, gq=gq: e.affine_select(out=EWq[32 * gq:32 * gq + 32, :], in_=EWq[32 * gq:32 * gq + 32, :],
                                                            pattern=[[-1, 2048]], compare_op=ALU.is_ge, fill=0.0, base=63,
                                                            channel_multiplier=64), reads=[rEWq], writes=[rEWq])
                S.pool(lambda e: e.iota(k64[:], pattern=[[64, 128]], base=0, channel_multiplier=0,
                                        allow_small_or_imprecise_dtypes=True), writes=[rk64])
                S.pool(lambda e: e.memset(e0[:], 0.0), writes=[re0])
                S.pool(lambda e: e.memset(e0[:, 0:1], 1.0), reads=[re0], writes=[re0])
                psctr = [0]
                pcctr = [0]
                sbctr = [0]
                SB4 = [(S0, rS0), (S1, rS1), (P0, rP0)]
                TPb, rTP = Pb[1], rP1
                S.pool(lambda e: e.memset(zb[:], 0.0), writes=[rzb])
                for j_ in range(2):
                    for h in range(2):
                        S.pool(lambda e, h=h, j_=j_: e.memset(qTzX[j_][h][0][:], 0.0), writes=[qTzX[j_][h][1]])
                for h in range(2):
                    for gq in range(4):
                        S.pool(lambda e, h=h, gq=gq: e.memset(nmaskT[h][gq][0][:], 0.0), writes=[nmaskT[h][gq][1]])

                def pre(s, part):
                    x_t, rx = xO[s % 2]
                    hnO, rhnO = hnX[s % 2]
                    hnTO, rhnTO = hnTX[s % 2]
                    qps, rqps = qpsX[s % 2]
                    cbc, rcbc = cbcX[s % 2]
                    cbs, rcbs = cbsX[s % 2]
                    cbw, rcbw = cbwX[s % 2]
                    fb, rfb = fbX[s % 2]
                    if part == "pe":
                        rms_rstd(x_t, rx, hnO, rhnO, part="act")
                        S.dve(lambda e: e.tensor_scalar(out=hnO[:], in0=x_t[:], scalar1=stat[:, 2:3], scalar2=None, op0=ALU.mult),
                              reads=[rx, rstat], writes=[rhnO])
                        for kc in range(8):
                            S.pe(lambda e, kc=kc: e.transpose(out=TPb[:, kc * 128:(kc + 1) * 128], in_=hnO[:, kc * 128:(kc + 1) * 128],
                                                              identity=identb[:]), reads=[rhnO, rIdb], writes=[rTP])
                        S.act(lambda e: e.copy(out=hnTO[:].rearrange("p a b -> p (a b)"), in_=TPb[:, :]), reads=[rTP], writes=[rhnTO])
                        return
                    S.dma(lambda e: e.dma_start(out=x_t[:], in_=xo[s, :, :]), writes=[rx])
                    S.dma(lambda e: e.dma_start(out=qps[:], in_=bass.AP(qpos.tensor, s * 128, [[0, 128], [1, 128]])), writes=[rqps])
                    rms_rstd(x_t, rx, hnO, rhnO, part="dve")
                    S.dve(lambda e: e.tensor_scalar(out=ddt[:], in0=k64[:], scalar1=qposC[:, s:s + 1], scalar2=None,
                                                    op0=ALU.subtract), reads=[rk64, rqposC], writes=[rddt])
                    S.dve(lambda e: e.tensor_scalar(out=fa[:], in0=ddt[:], scalar1=0.0, scalar2=None, op0=ALU.is_le),
                          reads=[rddt], writes=[rfa])
                    S.dve(lambda e: e.scalar_tensor_tensor(out=ff[:], in0=ddt[:], scalar=-128.0, in1=fa[:],
                                                           op0=ALU.is_gt, op1=ALU.mult), reads=[rddt, rfa], writes=[rff])
                    S.dve(lambda e: e.tensor_tensor(out=ff[:], in0=ff[:], in1=e0[:], op=ALU.max), reads=[rff, re0], writes=[rff])
                    S.dve(lambda e: e.tensor_scalar(out=fa[:], in0=fa[:], scalar1=1.0, scalar2=1e30, op0=ALU.subtract,
                                                    op1=ALU.mult), reads=[rfa], writes=[rfa])
                    S.dve(lambda e: e.scalar_tensor_tensor(out=fb[:], in0=ff[:], scalar=1e4, in1=fa[:],
                                                           op0=ALU.mult, op1=ALU.add), reads=[rff, rfa], writes=[rfb])
                    n_ct = min(4, (32 * s + 31 + 127) // 128)
                    for nt in range(n_ct):
                        S.dve(lambda e, nt=nt: e.tensor_scalar(out=cbc[:, nt, :], in0=qps[:], scalar1=cend[:, nt:nt + 1],
                                                               scalar2=NEGM, op0=ALU.is_lt, op1=ALU.mult),
                              reads=[rqps, rcend], writes=[rcbc])
                    for jj in range(4):
                        j = 4 * s + jj
                        S.dve(lambda e, jj=jj, j=j: e.tensor_scalar(out=cbs[:, jj, :], in0=qps[:],
                                                                    scalar1=kpos[:, j:j + 1], scalar2=NEGM,
                                                                    op0=ALU.is_lt, op1=ALU.mult),
                              reads=[rqps, rkpos], writes=[rcbs])
                    for jj in range(8):
                        j = 4 * s - 4 + jj
                        if j < 0:
                            continue
                        S.dve(lambda e, j=j: e.tensor_scalar(out=wtmp[:], in0=qps[:], scalar1=-512.0,
                                                             scalar2=kpos[:, j:j + 1], op0=ALU.add, op1=ALU.is_ge),
                              reads=[rqps, rkpos], writes=[rwtmp])
                        if jj >= 4:
                            S.dve(lambda e, jj=jj: e.scalar_tensor_tensor(out=cbw[:, jj, :], in0=wtmp[:], scalar=NEGM,
                                                                          in1=cbs[:, jj - 4, :], op0=ALU.mult, op1=ALU.add),
                                  reads=[rwtmp, rcbs], writes=[rcbw])
                        else:
                            S.dve(lambda e, jj=jj: e.tensor_scalar(out=cbw[:, jj, :], in0=wtmp[:], scalar1=NEGM, scalar2=None,
                                                                   op0=ALU.mult), reads=[rwtmp], writes=[rcbw])

                def proj(s, part):
                    hnTO, rhnTO = hnTX[s % 2]
                    qTz = qTzX[s % 2]
                    gate, rgate = gateX[s % 2]
                    if part == "c":
                        i = 1
                        for a in range(4):
                            S.pe(lambda e, a=a, i=i: e.transpose(out=Pb[i][:, a * 128:(a + 1) * 128],
                                                                 in_=qb[:, a * 128:(a + 1) * 128], identity=identb[:]),
                                 reads=[rqb, rIdb], writes=[rP[i]])
                        for h in range(2):
                            S.act(lambda e, i=i, h=h: e.copy(out=qTz[h][0][64 * h:64 * h + 64, :], in_=Pb[i][64 * h:64 * h + 64, 0:512]),
                                  reads=[rP[i]], writes=[qTz[h][1]])
                        return
                    blocks = [(512 * k, 512, 512 * k) for k in range(5)] + [(C_G, 24, ZG)]
                    for bi, (c0, cw, z0) in enumerate(blocks):
                        i = 1
                        for kc in range(8):
                            S.pe(lambda e, kc=kc, i=i, c0=c0, cw=cw: e.matmul(Pf[i][:, 0:cw], lhsT=hnTO[:, kc, :],
                                                                             rhs=WI[:, kc, c0:c0 + cw],
                                                                             start=(kc == 0), stop=(kc == 7)),
                                 reads=[rhnTO, rWI], writes=[rP[i]])
                        S.dve(lambda e, i=i, z0=z0, cw=cw: e.tensor_copy(out=zO[:, z0:z0 + cw], in_=Pf[i][:, 0:cw]),
                              reads=[rP[i]], writes=[rzO])
                    rope(zO, rzO, 0, 8, rO, rrO, s)
                    S.dve(lambda e: e.tensor_copy(out=qb[:], in_=zO[:, 0:512]), reads=[rzO], writes=[rqb])
                    S.act(lambda e: e.activation(out=gate[:], in_=zO[:, ZG:ZG + 24], func=AF.Exp, scale=-1.0),
                          reads=[rzO], writes=[rgate])
                    S.dve(lambda e: e.tensor_scalar(out=gate[:], in0=gate[:], scalar1=1.0, scalar2=None, op0=ALU.add),
                          reads=[rgate], writes=[rgate])
                    S.dve(lambda e: e.reciprocal(out=gate[:], in_=gate[:]), reads=[rgate], writes=[rgate])

                if n_slots > 0:
                    pre(0, "dve")
                    pre(0, "pe")
                    proj(0, "a")
                    proj(0, "c")
                def slot_body(s):
                    x_t, rx = xO[s % 2]
                    hnO, rhnO = hnX[s % 2]
                    hnTO, rhnTO = hnTX[s % 2]
                    qps, rqps = qpsX[s % 2]
                    cbc, rcbc = cbcX[s % 2]
                    cbs, rcbs = cbsX[s % 2]
                    cbw, rcbw = cbwX[s % 2]
                    fb, rfb = fbX[s % 2]
                    qTz = qTzX[s % 2]
                    gate, rgate = gateX[s % 2]
                    n_ct = min(4, (32 * s + 31 + 127) // 128)

                    def bc4(t_, col):
                        return sap(t_, 0, 128, col * 128, [[0, 4], [1, 128]])

                    def qTh(h):
                        return qTz[h][0][:, :]

                    def rqTh(h):
                        return qTz[h][1]

                    def zero_init(ACC, rACC, ncols):
                        S.pe(lambda e: e.matmul(ACC[:, 0:ncols], lhsT=zb[:, 0:128], rhs=zb[:, 0:ncols], start=True, stop=False),
                             reads=[rzb], writes=[rACC])

                    def gate_fac(h, br, den_ap, rsrc, clamp):
                        if clamp:
                            S.dve(lambda e: e.tensor_scalar(out=rden[:, 0:4], in0=den_ap, scalar1=1e-30, scalar2=None,
                                                            op0=ALU.max), reads=[rsrc], writes=[rrden])
                            S.dve(lambda e: e.reciprocal(out=rden[:, 0:4], in_=rden[:, 0:4]), reads=[rrden], writes=[rrden])
                        else:
                            S.dve(lambda e: e.reciprocal(out=rden[:, 0:4], in_=den_ap), reads=[rsrc], writes=[rrden])
                        S.dve(lambda e: e.tensor_tensor(out=fac[:], in0=rden[:, 0:4],
                                                        in1=sap(gate, 0, 128, 12 * h + br, [[3, 4]]), op=ALU.mult),
                              reads=[rrden, rgate], writes=[rfac])

                    items = []

                    def cmp_post(h):
                        den = sap(CC, 0, 128, 64, [[512, 2], [193, 2]])
                        gate_fac(h, 0, den, rCC, True)
                        for g in range(4):
                            hd = 4 * h + g
                            S.dve(lambda e, g=g, hd=hd: e.tensor_scalar(
                                out=mixf[:, 512 + hd * 64:512 + hd * 64 + 64],
                                in0=CC[:, g // 2, (g % 2) * 193:(g % 2) * 193 + 64],
                                scalar1=fac[:, g:g + 1], scalar2=None, op0=ALU.mult), reads=[rCC, rfac], writes=[rmixf])
                        for g in range(4):
                            src = fb[:] if g == 0 else imp[:]
                            S.dve(lambda e, g=g, src=src: e.scalar_tensor_tensor(
                                out=imp[:], in0=CC[:, g // 2, (g % 2) * 193 + 65:(g % 2) * 193 + 193],
                                scalar=rden[:, g:g + 1], in1=src, op0=ALU.mult, op1=ALU.add),
                                reads=[rCC, rrden, rfb, rimp], writes=[rimp])
                        S.dve(lambda e: e.max(out=mx8[:, 0:8], in_=imp[:]), reads=[rimp], writes=[rmx8])
                        S.dve(lambda e: e.match_replace(out=imp2[:], in_to_replace=mx8[:, 0:8], in_values=imp[:],
                                                        imm_value=-3e38), reads=[rimp, rmx8], writes=[rimp2])
                        S.dve(lambda e: e.max(out=mx8[:, 8:16], in_=imp2[:]), reads=[rimp2], writes=[rmx8])
                        S.dve(lambda e: e.tensor_scalar(out=nmask[:], in0=imp[:], scalar1=mx8[:, 15:16], scalar2=NEGM,
                                                        op0=ALU.is_lt, op1=ALU.mult), reads=[rimp, rmx8], writes=[rnmask])
                        S.pe(lambda e: e.transpose(out=TPb[:, 0:128], in_=nmask[:], identity=identb[:]),
                             reads=[rnmask, rIdb], writes=[rTP])
                        for gq in range(4):
                            nmT, rnmT = nmaskT[h][gq]
                            S.act(lambda e, nmT=nmT, gq=gq: e.copy(out=nmT[32 * gq:32 * gq + 32, :], in_=TPb[32 * gq:32 * gq + 32, 0:128]),
                                  reads=[rTP], writes=[rnmT])

                    def finish(ACC, rACC, br, h):
                        den = sap(ACC, 0, 128, 64, [[65, 4]])
                        gate_fac(h, br, den, rACC, False)
                        for g in range(4):
                            hd = 4 * h + g
                            dst = mixf[:, 512 + hd * 64:512 + hd * 64 + 64]
                            S.dve(lambda e, g=g, dst=dst: e.scalar_tensor_tensor(
                                out=dst, in0=ACC[:, g * 65:g * 65 + 64], scalar=fac[:, g:g + 1], in1=dst,
                                op0=ALU.mult, op1=ALU.add), reads=[rACC, rfac, rmixf], writes=[rmixf])

                    for h in range(2):
                        for nt in range(n_ct):
                            items.append(dict(
                                h=h, lhs_k=kccT[:, nt, :], rk=[rkccT], biases=[(identb[:], bc4(cbc, nt), [rIdb, rcbc])],
                                pv=[(CC[:, g // 2, (g % 2) * 193:(g % 2) * 193 + 193], g, VC[:, nt, h, :], rVC, rCC, g % 2 == 1) for g in range(4)],
                                pre=(lambda: [S.pe(lambda e, bkk=bkk: e.matmul(CC[:, bkk, 0:386], lhsT=zb[:, 0:128], rhs=zb[:, 0:386],
                                                                              start=True, stop=False), reads=[rzb], writes=[rCC])
                                              for bkk in range(2)]) if nt == 0 else None,
                                post=(lambda h=h: cmp_post(h)) if nt == n_ct - 1 else None))
                    for h in range(2):
                        jl = [jj for jj in range(8) if 4 * s - 4 + jj >= 0]
                        for ii, jj in enumerate(jl):
                            j = 4 * s - 4 + jj
                            items.append(dict(
                                h=h, lhs_k=KT2[:, 1, j * 128:(j + 1) * 128], rk=[rKT[1][j]],
                                biases=[(identb[:], bc4(cbw, jj), [rIdb, rcbw])],
                                pv=[(AW[:, g * 65:(g + 1) * 65], g, VA[:, j, 2 + h, :], rVAfull, rAW, g == 3) for g in range(4)],
                                pre=(lambda: zero_init(AW, rAW, 260)) if ii == 0 else None,
                                post=(lambda h=h: finish(AW, rAW, 2, h)) if ii == len(jl) - 1 else None))
                    for h in range(2):
                        nj = 4 * s + 4
                        for j in range(nj):
                            gq = j // 16
                            nmT, rnmT = nmaskT[h][gq]
                            biases = [(EWq[:, (j % 16) * 128:(j % 16 + 1) * 128], bc4(nmT, 0), [rEWq, rnmT])]
                            if j >= 4 * s:
                                biases.append((identb[:], bc4(cbs, j - 4 * s), [rIdb, rcbs]))
                            items.append(dict(
                                h=h, lhs_k=KT2[:, 0, j * 128:(j + 1) * 128], rk=[rKT[0][j]], biases=biases,
                                pv=[(AS[:, g * 65:(g + 1) * 65], g, VA[:, j, h, :], rVAfull, rAS, g == 3) for g in range(4)],
                                pre=(lambda: zero_init(AS, rAS, 260)) if j == 0 else None,
                                post=(lambda h=h: finish(AS, rAS, 1, h)) if j == nj - 1 else None))

                    def emit_scores(it):
                        si = sbctr[0] % len(SB4)
                        sbctr[0] += 1
                        bank, rbank = SB4[si]
                        nb = len(it["biases"])
                        S.pe(lambda e: e.matmul(bank[:, :], lhsT=it["lhs_k"], rhs=qTh(it["h"]), start=True, stop=(nb == 0)),
                             reads=it["rk"] + [rqTh(it["h"])], writes=[rbank])
                        for bi, (bl, br_, rdeps) in enumerate(it["biases"]):
                            S.pe(lambda e, bl=bl, br_=br_, bi=bi: e.matmul(bank[:, :], lhsT=bl, rhs=br_, start=False, stop=(bi == nb - 1)),
                                 reads=rdeps, writes=[rbank])
                        pt, rpt = PsT[psctr[0] % len(PsT)]
                        psctr[0] += 1
                        S.act(lambda e: e.activation(out=pt[:], in_=bank[:, :], func=AF.Exp, scale=SCALE), reads=[rbank], writes=[rpt])
                        it["pt"] = (pt, rpt)

                    def emit_pv(it):
                        if it["pre"] is not None:
                            it["pre"]()
                        pt, rpt = it["pt"]
                        npv = len(it["pv"])
                        for k_, (out_ap, g, rhs_ap, rrhs, racc, lastg) in enumerate(it["pv"]):
                            S.pe(lambda e, out_ap=out_ap, g=g, rhs_ap=rhs_ap, lastg=lastg: e.matmul(
                                out_ap, lhsT=pt[:, g * 128:(g + 1) * 128], rhs=rhs_ap, start=False,
                                stop=(it["post"] is not None and lastg)), reads=[rpt, rrhs], writes=[racc])
                        if it["post"] is not None:
                            it["post"]()

                    def mid_work(part):
                        if part == "pe":
                            mid_pe()
                            return
                        zv = zO[:, ZV:ZV + 512]
                        silu_half(0)
                        S.dve(lambda e: e.bn_stats(out=bnst[:, 0:6], in_=zv), reads=[rzO], writes=[rbnst])
                        S.dve(lambda e: e.bn_aggr(out=stat[:, 4:6], in_=bnst[:, 0:6]), reads=[rbnst], writes=[rstat])

                    def silu_half(hf):
                        zz = zO[:, ZZA + 512 * hf:ZZA + 512 * (hf + 1)]
                        tmp = mixf[:, 0:512]
                        S.act(lambda e: e.activation(out=tmp, in_=zz, func=AF.Exp, scale=-1.0), reads=[rzO], writes=[rsg])
                        S.dve(lambda e: e.tensor_scalar(out=tmp, in0=tmp, scalar1=1.0, scalar2=None, op0=ALU.add), reads=[rsg], writes=[rsg])
                        S.dve(lambda e: e.reciprocal(out=tmp, in_=tmp), reads=[rsg], writes=[rsg])
                        S.dve(lambda e: e.tensor_tensor(out=zz, in0=tmp, in1=zz, op=ALU.mult), reads=[rsg, rzO], writes=[rzO])

                    def mid_pe():
                        zv = zO[:, ZV:ZV + 512]
                        silu_half(1)
                        S.act(lambda e: e.activation(out=stat[:, 6:7], in_=stat[:, 5:6], func=AF.Ln, bias=epsT[:], scale=1.0),
                              reads=[rstat, repsT], writes=[rstat])
                        S.act(lambda e: e.activation(out=stat[:, 7:8], in_=stat[:, 6:7], func=AF.Exp, scale=-0.5),
                              reads=[rstat], writes=[rstat])
                        S.dve(lambda e: e.tensor_scalar(out=zv, in0=zv, scalar1=stat[:, 4:5],
                                                        scalar2=stat[:, 7:8], op0=ALU.subtract, op1=ALU.mult),
                              reads=[rzO, rstat], writes=[rzO])
                        S.dve(lambda e: e.tensor_tensor(out=zv, in0=zv, in1=lngB[:], op=ALU.mult), reads=[rzO, rlngB], writes=[rzO])
                        S.dve(lambda e: e.tensor_tensor(out=vnb[:], in0=zv, in1=lnbB[:], op=ALU.add), reads=[rzO, rlnbB], writes=[rvnb])
                        i = 1
                        for g in range(8):
                            S.pe(lambda e, g=g, i=i: e.matmul(Pf[i][:, g * 64:(g + 1) * 64], lhsT=wsT[:, g, :],
                                                             rhs=vnb[:, g * 64:(g + 1) * 64], start=True, stop=True),
                                 reads=[rwsT, rvnb], writes=[rP[i]])
                        for g in range(8):
                            S.dve(lambda e, i=i, g=g: e.scalar_tensor_tensor(
                                out=mixf[:, g * 64:(g + 1) * 64], in0=Pf[i][:, g * 64:(g + 1) * 64], scalar=bsT[:, g:g + 1],
                                in1=zO[:, ZU + g * 64:ZU + (g + 1) * 64], op0=ALU.add, op1=ALU.mult),
                                reads=[rP[i], rbsT, rzO], writes=[rsg])
                        S.dve(lambda e: e.tensor_tensor(out=hnO[:, 0:512], in0=mixf[:, 0:512], in1=zO[:, ZZA:ZZA + 512], op=ALU.mult),
                              reads=[rsg, rzO], writes=[rhnO])
                        S.dve(lambda e: e.tensor_copy(out=silB[:], in_=zO[:, ZZB:ZZB + 512]), reads=[rzO], writes=[rsilB])

                    DEPTH = 2
                    n_cmp_items = 2 * n_ct
                    E_ = n_cmp_items + DEPTH + 1
                    L_ = max(E_ + 2, len(items) + DEPTH - 1 - 14)
                    for k_ in range(len(items) + DEPTH):
                        if k_ < len(items):
                            emit_scores(items[k_])
                        if k_ >= DEPTH:
                            emit_pv(items[k_ - DEPTH])
                        if k_ == E_:
                            mid_work("dve")
                            if s + 1 < n_slots:
                                pre(s + 1, "dve")
                        if k_ == L_:
                            mid_work("pe")
                            if s + 1 < n_slots:
                                pre(s + 1, "pe")
                                proj(s + 1, "a")
                    if s + 1 < n_slots:
                        proj(s + 1, "c")

                    S.dve(lambda e: e.tensor_tensor(out=hnO[:, 512:1024], in0=mixf[:, 512:1024], in1=silB[:], op=ALU.mult),
                          reads=[rmixf, rsilB], writes=[rhnO])

                    i = nextP()
                    for kc in range(8):
                        S.pe(lambda e, kc=kc, i=i: e.transpose(out=Pb[i][:, kc * 128:(kc + 1) * 128],
                                                               in_=hnO[:, kc * 128:(kc + 1) * 128], identity=identb[:]),
                             reads=[rhnO, rIdb], writes=[rP[i]])
                    S.act(lambda e, i=i: e.copy(out=hnTO[:].rearrange("p a b -> p (a b)"), in_=Pb[i][:, :]),
                          reads=[rP[i]], writes=[rhnTO])
                    for cb in range(2):
                        i = nextP()
                        for kc in range(8):
                            S.pe(lambda e, kc=kc, i=i, cb=cb: e.matmul(Pf[i][:, :], lhsT=hnTO[:, kc, :],
                                                                       rhs=WO[:, kc, cb * 512:(cb + 1) * 512],
                                                                       start=(kc == 0), stop=(kc == 7)),
                                 reads=[rhnTO, rWO], writes=[rP[i]])
                        S.dve(lambda e, i=i, cb=cb, x_t=x_t: e.tensor_tensor(
                            out=x_t[:, cb * 512:(cb + 1) * 512], in0=Pf[i][:, :], in1=x_t[:, cb * 512:(cb + 1) * 512],
                            op=ALU.add), reads=[rP[i], rx], writes=[rx])
                    rms_rstd(x_t, rx, hnO, rhnO)
                    S.dve(lambda e, x_t=x_t: e.scalar_tensor_tensor(out=x_t[:], in0=x_t[:], scalar=stat[:, 2:3], in1=fgB[:],
                                                                    op0=ALU.mult, op1=ALU.mult),
                          reads=[rx, rstat, rfgB], writes=[rx])
                    S.dma(lambda e, s=s, x_t=x_t: e.dma_start(out=y_o[s, :, :], in_=x_t[:]), reads=[rx], q=("sp" if s == n_slots - 1 else "pool"))

                for s_ in range(n_slots):
                    slot_body(s_)
                S.flush()
    return nc, dict(S.cnt)


_PROG = {}


def _get_prog(key=("full",)):
    if key not in _PROG:
        _PROG[key] = build_program()
    return _PROG[key]


def prep_inputs(inp, with_sample=True, with_prompt=True):
    perm = _col_perm()
    samp_shared = {}
    if with_sample:
        cS, sS = _rope_tables(np.full((128,), 2048, dtype=np.int64))
        fbs = np.zeros((2, 33), np.float32)
        fbs[:, [0, 31, 32]] = 1e4
        bmask = np.zeros((8, 2), np.float32)
        bmask[0:4, 0] = 1.0
        bmask[4:8, 1] = 1.0
        samp_shared = {
            "pm8": (np.arange(128) % 8).astype(np.float32).reshape(128, 1),
            "ropeS": np.ascontiguousarray(np.concatenate([cS, sS], -1)),
            "ws00": np.ascontiguousarray(inp["w_s"][0][:, 0, 0][None, :]),
            "bs0": np.ascontiguousarray(inp["b_s"][0][:, 0][None, :]),
            "fbs": fbs,
            "bmask": bmask,
            "kc_pool": inp["cache_k_cmp"][0].reshape(20480, 2048),
            "vc_pool": inp["cache_v_cmp"][0].reshape(20480, 2048),
            "ks_pool": inp["cache_k_sel"][0].reshape(20480, 2048),
            "vs_pool": inp["cache_v_sel"][0].reshape(20480, 2048),
        }
    w_in_p = np.ascontiguousarray(inp["w_in"][0][:, perm])
    pos_all = np.arange(SEQ, dtype=np.int64)
    cA, sA = _rope_tables(pos_all)
    ropeA = np.concatenate([cA, sA], -1).reshape(NT_ALL, 128, 16).transpose(1, 0, 2)
    posC = 16 * np.arange(512, dtype=np.int64)
    cC, sC = _rope_tables(posC)
    ropeC = np.concatenate([cC, sC], -1).reshape(4, 128, 16).transpose(1, 0, 2)
    shared = {
        "w_in": w_in_p,
        "w_out": np.ascontiguousarray(inp["w_out"][0]),
        "final_g": np.ascontiguousarray(inp["final_g"][None, :]),
        "ln_g": np.ascontiguousarray(inp["ln_v_g"][0][None, :]),
        "ln_b": np.ascontiguousarray(inp["ln_v_b"][0][None, :]),
        "w_s": np.ascontiguousarray(inp["w_s"][0]),
        "b_sT": np.ascontiguousarray(inp["b_s"][0].T),
        "g_col": np.ascontiguousarray(inp["norm_g"][0].reshape(8, 128).T),
        "w_ck": np.ascontiguousarray(inp["w_ck"][0].transpose(1, 0, 2).reshape(64, 2048)),
        "w_cv": np.ascontiguousarray(inp["w_cv"][0].transpose(1, 0, 2).reshape(64, 2048)),
        "pe_ck": np.ascontiguousarray(inp["pe_ck"][0].T),
        "pe_cv": np.ascontiguousarray(inp["pe_cv"][0].T),
        "ropeA": np.ascontiguousarray(ropeA),
        "ropeC": np.ascontiguousarray(ropeC),
    }
    maps = []
    for c in range(8):
        b, r = c // 4, c % 4
        xbat = np.ascontiguousarray(inp["x_prompt"][b])
        tiles = xbat.reshape(NT_ALL, 128, D)
        own = np.arange(NSLOT) * 4 + r
        qp = (own[:, None] * 128 + np.arange(128)[None, :]).astype(np.int64)
        cO, sO = _rope_tables(qp.reshape(-1))
        ropeO = np.concatenate([cO, sO], -1).reshape(NSLOT, 128, 16).transpose(1, 0, 2)
        m = dict(shared)
        m["xb"] = xbat
        m["xo"] = np.ascontiguousarray(tiles[own])
        m["ropeO"] = np.ascontiguousarray(ropeO)
        m["qpos"] = np.ascontiguousarray(qp.reshape(1, -1).astype(np.float32))
        m["qposT"] = np.ascontiguousarray(qp.T.astype(np.float32))
        if with_sample:
            sl = slice(NSAMP * c, NSAMP * (c + 1))
            m["xs"] = np.ascontiguousarray(inp["x_sample"][sl, 0, :])
            pt = inp["page_table"][sl].astype(np.int32)
            m["pt_e"] = np.ascontiguousarray(np.repeat(pt.T, 8, axis=0))
            m["kwin"] = np.ascontiguousarray(inp["cache_k_win"][0, sl].reshape(NSAMP, 512, 128))
            m["vwin"] = np.ascontiguousarray(inp["cache_v_win"][0, sl].reshape(NSAMP, 512, 128))
            m.update(samp_shared)
        maps.append(m)
    return maps


def kernel(**inp):
    inp = {k: np.asarray(v) for k, v in inp.items()}
    nc, _ = _get_prog()
    maps = prep_inputs(inp)
    res = run_bass_kernel_spmd(nc, maps, core_ids=list(range(8)))
    return assemble(res.results)


def assemble(results, with_sample=True, with_prompt=True):
    B = 2
    outs_p = ()
    if with_prompt:
        y_prompt = np.zeros((B, SEQ, D), np.float32)
        kvp = np.zeros((B, SEQ, 768), np.float32)
        for c in range(8):
            b, r = c // 4, c % 4
            own = np.arange(NSLOT) * 4 + r
            y_prompt[b].reshape(NT_ALL, 128, D)[own] = results[c]["y_o"]
            if r == 0:
                kvp[b] = results[c]["kv_all"].reshape(SEQ, 768)

        def sl(c0):
            return np.ascontiguousarray(kvp[:, :, c0 - C_KV:c0 - C_KV + 128]).reshape(1, B, SEQ, 2, 64)
        kc, vc, ks, vs = sl(C_KC), sl(C_VC), sl(C_KS), sl(C_VS)
        kw = np.ascontiguousarray(sl(C_KW)[:, :, SEQ - 512:])
        vw = np.ascontiguousarray(sl(C_VW)[:, :, SEQ - 512:])
        outs_p = (y_prompt, kc, vc, ks, vs, kw, vw)
    if not with_sample:
        return outs_p
    ys = np.concatenate([results[c]["ys"] for c in range(8)], 0).reshape(128, 1, D)
    kvs = np.concatenate([results[c]["kvs"] for c in range(8)], 0)
    vns = np.concatenate([results[c]["vns"] for c in range(8)], 0).reshape(1, 128, 1, 512)
    kwo = np.concatenate([results[c]["kwin_o"] for c in range(8)], 0).reshape(1, 128, 512, 2, 64)
    vwo = np.concatenate([results[c]["vwin_o"] for c in range(8)], 0).reshape(1, 128, 512, 2, 64)

    def ss(c0):
        return np.ascontiguousarray(kvs[:, c0 - C_KV:c0 - C_KV + 128]).reshape(1, 128, 1, 2, 64)
    outs_s = (ss(C_KC), ss(C_VC), ss(C_KS), ss(C_VS), kwo, vwo, vns)
    if not with_prompt:
        return (ys,) + outs_s
    return (outs_p[0], ys) + outs_p[1:] + outs_s
```

```python
from contextlib import ExitStack
import numpy as np
import ml_dtypes
import concourse.bass as bass
import concourse.mybir as mybir
from concourse.bass_utils import run_bass_kernel_spmd

F32 = mybir.dt.float32
BF16 = mybir.dt.bfloat16
I32 = mybir.dt.int32
AF = mybir.ActivationFunctionType
ALU = mybir.AluOpType

ENG_NAMES = ("pe", "act", "dve", "pool", "sp")


class Res:
    __slots__ = ("name", "last_w", "readers", "dma_sem", "dma_cnt", "dram")

    def __init__(self, name, dram=False):
        self.name = name
        self.last_w = None
        self.readers = []
        self.dma_sem = {}
        self.dma_cnt = {}
        self.dram = dram


class Op:
    __slots__ = ("eng", "fn", "reads", "writes", "dma", "idx", "seq", "waits", "dres", "dval", "dkind")

    def __init__(self, eng, fn, reads, writes, dma):
        self.eng = eng
        self.fn = fn
        self.reads = reads
        self.writes = writes
        self.dma = dma
        self.waits = []


class Sched:
    def __init__(self, nc, sem_stack):
        self.nc = nc
        self.ops = []
        self.flushed = 0
        self.sem_stack = sem_stack
        self.cnt = {e: 0 for e in ENG_NAMES}
        self.esem = {e: sem_stack.enter_context(nc.semaphore("es_" + e)) for e in ENG_NAMES if e != "sp"}
        self.out_res = Res("dramonly")
        self.known = {e: {} for e in ENG_NAMES}

    def op(self, eng, fn, reads=(), writes=(), dma=False):
        o = Op(eng, fn, [r for r in reads if r is not None], [r for r in writes if r is not None], dma)
        o.idx = len(self.ops)
        self.ops.append(o)
        return o

    def pe(self, fn, reads=(), writes=()):
        return self.op("pe", fn, reads, writes)

    def act(self, fn, reads=(), writes=()):
        return self.op("act", fn, reads, writes)

    def dve(self, fn, reads=(), writes=()):
        return self.op("dve", fn, reads, writes)

    def pool(self, fn, reads=(), writes=()):
        return self.op("pool", fn, reads, writes)

    def dma(self, fn, reads=(), writes=(), q="sp"):
        return self.op(q, fn, reads, writes, dma=True)

    def flush(self):
        nc = self.nc
        ops = self.ops
        new = ops[self.flushed:]
        self.flushed = len(ops)
        if not new:
            return
        cnt = self.cnt
        esem = self.esem
        known = self.known
        for o in new:
            if not o.dma:
                cnt[o.eng] += 1
                o.seq = cnt[o.eng]
            else:
                o.seq = None

        def token_of(o):
            if o.dma:
                return ("d", o.dres, o.dval, o.dkind)
            return ("e", o.eng, o.seq)

        def add_wait(o, tok):
            if tok[0] == "e":
                key = ("e", tok[1])
                if tok[1] == o.eng and o.eng == "pe" and not o.dma:
                    return
            else:
                key = ("d", id(tok[1]), tok[3])
            val = tok[2]
            if tok[0] == "d":
                val = tok[1].dma_cnt[tok[3]]
            k = known[o.eng]
            if k.get(key, 0) >= val:
                return
            k[key] = val
            o.waits.append((tok, val))

        touched = []
        seen = set()
        for o in new:
            for r in o.reads:
                if r.last_w is not None:
                    add_wait(o, token_of(ops[r.last_w]))
            for r in o.writes:
                if r.last_w is not None:
                    add_wait(o, token_of(ops[r.last_w]))
                for ri in r.readers:
                    if ri != o.idx:
                        add_wait(o, token_of(ops[ri]))
            if o.dma:
                dres = None
                for r in list(o.writes) + list(o.reads):
                    if not r.dram:
                        dres = r
                        break
                if dres is None:
                    dres = self.out_res
                kind = "sw" if o.eng == "pool" else "hw"
                if kind not in dres.dma_sem:
                    dres.dma_sem[kind] = self.sem_stack.enter_context(nc.semaphore("ds%s_%s" % (kind, dres.name)))
                    dres.dma_cnt[kind] = 0
                dres.dma_cnt[kind] += 16
                o.dres = dres
                o.dkind = kind
                o.dval = dres.dma_cnt[kind]
                if (id(dres), kind) not in seen:
                    seen.add((id(dres), kind))
                    touched.append((dres, kind))
            for r in o.reads:
                if r not in o.writes:
                    r.readers.append(o.idx)
            for r in o.writes:
                r.last_w = o.idx
                r.readers = []
        by_eng = {e: [o for o in new if o.eng == e] for e in ENG_NAMES}

        def run_engine(ename, eng):
            for o in by_eng[ename]:
                for tok, val in o.waits:
                    if tok[0] == "e":
                        eng.wait_ge(esem[tok[1]], val)
                    else:
                        eng.wait_ge(tok[1].dma_sem[tok[3]], val)
                ins = o.fn(eng)
                if o.dma:
                    ins.then_inc(o.dres.dma_sem[o.dkind], 16)
                else:
                    ins.then_inc(esem[ename], 1)
            if ename == "sp":
                for r, kind in touched:
                    eng.wait_ge(r.dma_sem[kind], r.dma_cnt[kind])
                    known["sp"][("d", id(r), kind)] = r.dma_cnt[kind]

        with nc.Block() as block:
            @block.sync
            def _(e):
                run_engine("sp", e)

            @block.tensor
            def _(e):
                run_engine("pe", e)

            @block.scalar
            def _(e):
                run_engine("act", e)

            @block.vector
            def _(e):
                run_engine("dve", e)

            @block.gpsimd
            def _(e):
                run_engine("pool", e)


D = 1024
SEQ = 8192
NT_ALL = SEQ // 128
NSLOT = 16
DIN = 3352
C_KV = 2584
C_Q, C_KS, C_KW, C_KC, C_VC, C_VS, C_VW = 0, 2584, 2712, 2840, 2968, 3096, 3224
C_U, C_V, C_ZA, C_ZB, C_G = 512, 1024, 1536, 2048, 2560
NEGM = -32768.0
SCALE = 0.125
EPS = 1e-6
NSAMP = 16
NPAGE = 16
QHEAD_ORDER = [0, 4, 1, 5, 2, 6, 3, 7]


def _col_perm():
    o = {}
    acc = 0
    for name, sz in (("u", 512), ("v", 512), ("za", 512), ("q", 512), ("kc", 128), ("vc", 128),
                     ("ks", 128), ("vs", 128), ("kw", 128), ("vw", 128), ("g", 24), ("zb", 512)):
        o[name] = (acc, sz)
        acc += sz
    perm = []
    for hd in QHEAD_ORDER:
        perm += list(range(o["q"][0] + hd * 64, o["q"][0] + hd * 64 + 64))
    for name in ("u", "v", "za", "zb", "g", "ks", "kw", "kc", "vc", "vs", "vw"):
        perm += list(range(o[name][0], o[name][0] + o[name][1]))
    return np.array(perm, dtype=np.int64)


def _rope_tables(pos):
    half = 8
    inv = np.power(np.float32(500000.0), -np.arange(half, dtype=np.float32) / np.float32(half)).astype(np.float32)
    ang = pos.astype(np.float32)[:, None] * inv[None, :]
    return np.cos(ang).astype(np.float32), np.sin(ang).astype(np.float32)


def sap(t, p0, pn, off, dims):
    fs = 1
    for d in t.shape[1:]:
        fs *= d
    return bass.AP(t, p0 * fs + off, [[fs, pn]] + [[a, b] for a, b in dims])


ZQ, ZU, ZV, ZZA, ZZB, ZG, ZW = 0, 512, 1024, 1536, 2048, 2560, 2584


def build_program(with_sample=True, n_slots=NSLOT, n_tiles_a=NT_ALL, dbg=(), n_samp=NSAMP, with_prompt=True):
    nc = bass.Bass("TRN2", target_bir_lowering=False)
    dt = nc.dram_tensor

    def din(name, shape, dtype=F32):
        return dt(name, list(shape), dtype, kind="ExternalInput").ap()

    def dout(name, shape, dtype=F32):
        return dt(name, list(shape), dtype, kind="ExternalOutput").ap()

    xb = din("xb", [SEQ, D])
    xo = din("xo", [NSLOT, 128, D])
    w_in = din("w_in", [D, DIN])
    w_out = din("w_out", [D, D])
    final_g = din("final_g", [1, D])
    ln_g = din("ln_g", [1, 512])
    ln_b = din("ln_b", [1, 512])
    w_s = din("w_s", [8, 128, 128])
    w_ck = din("w_ck", [64, 2048])
    w_cv = din("w_cv", [64, 2048])
    pe_ck = din("pe_ck", [64, 32])
    pe_cv = din("pe_cv", [64, 32])
    qposT = din("qposT", [128, NSLOT])
    b_sT = din("b_sT", [128, 8])
    g_col = din("g_col", [128, 8])
    ropeA = din("ropeA", [128, NT_ALL, 16])
    ropeO = din("ropeO", [128, NSLOT, 16])
    ropeC = din("ropeC", [128, 4, 16])
    qpos = din("qpos", [1, NSLOT * 128])
    if with_sample:
        xs = din("xs", [NSAMP, D])
        pt_e = din("pt_e", [128, NSAMP], I32)
        pm8 = din("pm8", [128, 1])
        ropeS = din("ropeS", [128, 16])
        ws00 = din("ws00", [1, 8])
        bs0 = din("bs0", [1, 8])
        fbs = din("fbs", [2, 33])
        bmask = din("bmask", [8, 2])
        pools = [din(nm, [20480, 2048]) for nm in ("kc_pool", "vc_pool", "ks_pool", "vs_pool")]
        kwin = din("kwin", [NSAMP, 512, 128])
        vwin = din("vwin", [NSAMP, 512, 128])
        ys = dout("ys", [NSAMP, D])
        kvs = dout("kvs", [NSAMP, 768])
        vns = dout("vns", [NSAMP, 512])
        kwin_o = dout("kwin_o", [NSAMP, 512, 128])
        vwin_o = dout("vwin_o", [NSAMP, 512, 128])
        scr = dout("scr", [3, 128, 65])
    y_o = dout("y_o", [NSLOT, 128, D])
    kv_all = dout("kv_all", [NT_ALL, 128, 768])

    gst = ExitStack()
    with gst:
        S = Sched(nc, gst)

        def mk_sb(stack):
            def sb(name, shape, dtype=F32):
                t = stack.enter_context(nc.sbuf_tensor(name, list(shape), dtype))
                return t, Res(name)
            return sb

        def mk_ps(stack):
            def ps(name, shape, dtype=F32):
                t = stack.enter_context(nc.psum_tensor(name, list(shape), dtype))
                return t, Res(name)
            return ps

        sb = mk_sb(gst)
        ps = mk_ps(gst)

        WI, rWI = sb("WI", [128, 8, C_KV], BF16)
        Wck, rWck = sb("Wck", [128, 32, 64], BF16)
        Wcv, rWcv = sb("Wcv", [128, 32, 64], BF16)
        identb, rIdb = sb("identb", [128, 128], BF16)
        fgB, rfgB = sb("fgB", [128, D])
        lngB, rlngB = sb("lngB", [128, 512])
        lnbB, rlnbB = sb("lnbB", [128, 512])
        rO, rrO = sb("ropeO_sb", [128, NSLOT, 16])
        rC, rrC = sb("ropeC_sb", [128, 4, 16])
        qposC, rqposC = sb("qposC", [128, NSLOT])
        kpos, rkpos = sb("kpos", [128, NT_ALL])
        cend, rcend = sb("cend", [128, 4])
        cKB, rcKB = sb("cKB", [128, 128])
        cVB, rcVB = sb("cVB", [128, 128])
        wsT, rwsT = sb("wsT", [128, 8, 128], BF16)
        bsT, rbsT = sb("bsT", [128, 8])
        epsT, repsT = sb("epsT", [128, 1])
        gcol, rgcol = sb("gcol", [128, 8])
        stat, rstat = sb("stat", [128, 8])
        rtmp_t, rrtmp = sb("ropetmp", [128, 4 * 96])

        P0, rP0 = ps("P0", [128, 512])
        P1, rP1 = ps("P1", [128, 512])
        S0, rS0 = ps("S0", [128, 512])
        S1, rS1 = ps("S1", [128, 512])
        CC, rCC = ps("CC", [128, 2, 512])
        AS, rAS = ps("AS", [128, 512])
        AW, rAW = ps("AW", [128, 512])
        Pb = [P0.bitcast(BF16), P1.bitcast(BF16)]
        Pf = [P0, P1]
        rP = [rP0, rP1]
        Sb = [S0, S1]
        rSb = [rS0, rS1]
        pctr = [0]

        def nextP():
            i = pctr[0] % 2
            pctr[0] += 1
            return i

        sctr = [0]

        def nextS():
            i = sctr[0] % 2
            sctr[0] += 1
            return i

        with ExitStack() as s0:
            sb0 = mk_sb(s0)
            S.pool(lambda e: e.memset(identb[:], 0.0), writes=[rIdb])
            S.pool(lambda e: e.affine_select(out=identb[:], in_=identb[:], pattern=[[-1, 128]],
                                             compare_op=ALU.not_equal, fill=1.0, base=0,
                                             channel_multiplier=1), reads=[rIdb], writes=[rIdb])
            S.pool(lambda e: e.memset(epsT[:], EPS), writes=[repsT])

            def bload(dst, rdst, src, n):
                S.dma(lambda e: e.dma_start(out=dst[:], in_=bass.AP(src.tensor, 0, [[0, 128], [1, n]])),
                      writes=[rdst])
            bload(fgB, rfgB, final_g, D)
            bload(lngB, rlngB, ln_g, 512)
            bload(lnbB, rlnbB, ln_b, 512)
            S.dma(lambda e: e.dma_start(out=rO[:], in_=ropeO[:, :, :]), writes=[rrO])
            S.dma(lambda e: e.dma_start(out=rC[:], in_=ropeC[:, :, :]), writes=[rrC])
            S.dma(lambda e: e.dma_start(out=qposC[:], in_=qposT[:, :]), writes=[rqposC])
            S.dma(lambda e: e.dma_start(out=bsT[:], in_=b_sT[:, :]), writes=[rbsT])
            S.dma(lambda e: e.dma_start(out=gcol[:], in_=g_col[:, :]), writes=[rgcol])
            wst = [sb0("wstage%d" % i, [128, 1024]) for i in range(2)]
            wi = 0

            def load_w(dst, rdst, col0, ncols, stages, wi):
                for kc in range(8):
                    for c0 in range(0, ncols, 1024):
                        cw = min(1024, ncols - c0)
                        stg, rstg = stages[wi % 2]
                        wi += 1
                        S.dma(lambda e, stg=stg, kc=kc, c0=c0, cw=cw: e.dma_start(
                            out=stg[:, 0:cw], in_=w_in[kc * 128:(kc + 1) * 128, col0 + c0:col0 + c0 + cw]),
                            writes=[rstg])
                        S.dve(lambda e, stg=stg, kc=kc, c0=c0, cw=cw: e.tensor_scalar(
                            out=dst[:, kc, c0:c0 + cw], in0=stg[:, 0:cw], scalar1=gcol[:, kc:kc + 1], scalar2=None,
                            op0=ALU.mult), reads=[rstg, rgcol], writes=[rdst])
                return wi
            if 'no_w' not in dbg:
                wi = load_w(WI, rWI, 0, C_KV, wst, wi)
            for (wsrc, Wc, rWc) in ((w_ck, Wck, rWck), (w_cv, Wcv, rWcv)) if 'no_wc' not in dbg else ():
                for lh in range(2):
                    stg, rstg = wst[wi % 2]
                    wi += 1
                    for h in range(2):
                        S.dma(lambda e, h=h, wsrc=wsrc, stg=stg, lh=lh: e.dma_start(
                            out=stg[64 * h:64 * h + 64, 0:1024],
                            in_=wsrc[:, lh * 1024:(lh + 1) * 1024]), writes=[rstg])
                    S.dve(lambda e, Wc=Wc, stg=stg, lh=lh: e.tensor_copy(
                        out=Wc[:, lh * 16:(lh + 1) * 16, :].rearrange("p a b -> p (a b)"),
                        in_=stg[:, 0:1024]), reads=[rstg], writes=[rWc])
            pe2, rpe2 = sb0("pe2", [128, 2, 32])
            pe2b, rpe2b = sb0("pe2b", [128, 2, 32], BF16)
            crow, rcrow = sb0("crow", [1, 128])
            crow2, rcrow2 = sb0("crow2", [1, 2, 128], BF16)
            ones_f, rones = sb0("ones_f", [1, 128], BF16)
            S.pool(lambda e: e.memset(ones_f[:], 1.0), writes=[rones])
            for i, psrc in enumerate((pe_ck, pe_cv) if 'no_pe' not in dbg else ()):
                for h in range(2):
                    S.dma(lambda e, h=h, i=i, psrc=psrc: e.dma_start(
                        out=pe2[64 * h:64 * h + 64, i, :], in_=psrc[:, :]), writes=[rpe2])
            if 'no_pe' not in dbg:
                S.dve(lambda e: e.tensor_copy(out=pe2b[:], in_=pe2[:]), reads=[rpe2], writes=[rpe2b])
            for i, (Wc, rWc, cB, rcB) in enumerate(((Wck, rWck, cKB, rcKB), (Wcv, rWcv, cVB, rcVB)) if 'no_pe' not in dbg else ()):
                for h in range(2):
                    for l in range(32):
                        S.pe(lambda e, l=l, h=h, Wc=Wc, i=i: e.matmul(
                            Sb[h][0:1, 0:64], lhsT=sap(pe2b, 64 * h, 64, 32 * i + l, [[1, 1]]),
                            rhs=sap(Wc, 64 * h, 64, l * 64, [[1, 64]]), start=(l == 0), stop=(l == 31)),
                            reads=[rpe2b, rWc], writes=[rSb[h]])
                    S.dve(lambda e, h=h: e.tensor_copy(out=crow[:, 64 * h:64 * h + 64], in_=Sb[h][0:1, 0:64]),
                          reads=[rSb[h]], writes=[rcrow])
                S.dve(lambda e: e.tensor_copy(out=crow2[:, 0, :], in_=crow[:]), reads=[rcrow], writes=[rcrow2])
                S.dve(lambda e: e.tensor_tensor(out=crow2[:, 1, :], in0=crow[:], in1=crow2[:, 0, :], op=ALU.subtract),
                      reads=[rcrow, rcrow2], writes=[rcrow2])
                for hl in range(2):
                    S.pe(lambda e, hl=hl: e.matmul(P1[:, 0:128], lhsT=ones_f[:], rhs=crow2[:, hl, :], start=(hl == 0), stop=(hl == 1)),
                         reads=[rones, rcrow2], writes=[rP1])
                S.dve(lambda e, cB=cB: e.tensor_copy(out=cB[:], in_=P1[:, 0:128]), reads=[rP1], writes=[rcB])
            wsl, rwsl = sb0("wsl", [128, 8, 128])
            wslb, rwslb = sb0("wslb", [128, 8, 128], BF16)
            if 'no_ws' not in dbg:
                S.dma(lambda e: e.dma_start(out=wsl[:], in_=bass.AP(w_s.tensor, 0, [[128, 128], [128 * 128, 8], [1, 128]])),
                      writes=[rwsl])
                S.pool(lambda e: e.affine_select(out=wsl[:], in_=wsl[:], pattern=[[0, 8], [-1, 128]],
                                                 compare_op=ALU.is_ge, fill=0.0, base=0, channel_multiplier=1),
                       reads=[rwsl], writes=[rwsl])
                S.dve(lambda e: e.tensor_copy(out=wslb[:], in_=wsl[:]), reads=[rwsl], writes=[rwslb])
                for g in range(8):
                    i = g % 2
                    S.pe(lambda e, g=g, i=i: e.transpose(out=Pb[i][:, 0:128], in_=wslb[:, g, :], identity=identb[:]),
                         reads=[rwslb, rIdb], writes=[rP[i]])
                    S.act(lambda e, g=g, i=i: e.copy(out=wsT[:, g, :], in_=Pb[i][:, 0:128]), reads=[rP[i]], writes=[rwsT])
            S.pool(lambda e: e.iota(kpos[:], pattern=[[128, NT_ALL]], base=0, channel_multiplier=1,
                                    allow_small_or_imprecise_dtypes=True), writes=[rkpos])
            S.pool(lambda e: e.iota(cend[:], pattern=[[2048, 4]], base=31, channel_multiplier=16,
                                    allow_small_or_imprecise_dtypes=True), writes=[rcend])
            S.flush()

        def rms_rstd(xt, rxt, junk, rjunk):
            if "no_rms" in dbg:
                S.dve(lambda e: e.memset(stat[:, 0:3], 1.0), writes=[rstat])
                return
            if "no_accum" in dbg:
                S.dve(lambda e: e.tensor_tensor(out=junk[:], in0=xt[:], in1=xt[:], op=ALU.mult), reads=[rxt], writes=[rjunk])
                S.dve(lambda e: e.memset(stat[:, 0:1], 1024.0), writes=[rstat])
                S.act(lambda e: e.activation(out=stat[:, 1:2], in_=stat[:, 0:1], func=AF.Ln,
                                             bias=epsT[:], scale=1.0 / D), reads=[rstat, repsT], writes=[rstat])
                S.act(lambda e: e.activation(out=stat[:, 2:3], in_=stat[:, 1:2], func=AF.Exp,
                                             scale=-0.5), reads=[rstat], writes=[rstat])
                return
            S.dve(lambda e: e.scalar_tensor_tensor(out=junk[:], in0=xt[:], scalar=1.0, in1=xt[:], op0=ALU.mult,
                                                   op1=ALU.mult, accum_out=stat[:, 0:1]),
                  reads=[rxt], writes=[rjunk, rstat])
            S.act(lambda e: e.activation(out=stat[:, 1:2], in_=stat[:, 0:1], func=AF.Ln,
                                         bias=epsT[:], scale=1.0 / D), reads=[rstat, repsT], writes=[rstat])
            S.act(lambda e: e.activation(out=stat[:, 2:3], in_=stat[:, 1:2], func=AF.Exp,
                                         scale=-0.5), reads=[rstat], writes=[rstat])

        def rope(zt, rzt, c0, nh, tab, rtab, tcol):
            if "no_rope" in dbg:
                return
            x1 = sap(zt, 0, 128, c0, [[64, nh], [1, 8]])
            x2 = sap(zt, 0, 128, c0 + 8, [[64, nh], [1, 8]])
            cos = sap(tab, 0, 128, tcol * 16, [[0, nh], [1, 8]])
            sin = sap(tab, 0, 128, tcol * 16 + 8, [[0, nh], [1, 8]])
            t = [sap(rtmp_t, 0, 128, i * 96, [[8, nh], [1, 8]]) for i in range(4)]
            S.dve(lambda e: e.tensor_tensor(out=t[0], in0=x1, in1=cos, op=ALU.mult), reads=[rzt, rtab], writes=[rrtmp])
            S.dve(lambda e: e.tensor_tensor(out=t[1], in0=x2, in1=sin, op=ALU.mult), reads=[rzt, rtab], writes=[rrtmp])
            S.dve(lambda e: e.tensor_tensor(out=t[2], in0=x2, in1=cos, op=ALU.mult), reads=[rzt, rtab], writes=[rrtmp])
            S.dve(lambda e: e.tensor_tensor(out=t[3], in0=x1, in1=sin, op=ALU.mult), reads=[rzt, rtab], writes=[rrtmp])
            S.dve(lambda e: e.tensor_tensor(out=x1, in0=t[0], in1=t[1], op=ALU.subtract), reads=[rrtmp], writes=[rzt])
            S.dve(lambda e: e.tensor_tensor(out=x2, in0=t[2], in1=t[3], op=ALU.add), reads=[rrtmp], writes=[rzt])

        if with_sample:
          with ExitStack() as scs:
            sbc = mk_sb(scs)
            WOc, rWOc = sbc("WOc", [128, 8, D], BF16)
            WKVc, rWKVc = sbc("WKVc", [128, 8, 768], BF16)
            xS, rxS = sbc("xS", [128, D])
            zS, rzS = sbc("zS", [128, DIN])
            hnS, rhnS = sbc("hnS", [128, D], BF16)
            hnTS, rhnTS = sbc("hnTS", [128, 8, 128], BF16)
            mixS, rmixS = sbc("mixS", [128, D])
            stgc = [sbc("stgc%d" % i, [128, 1024]) for i in range(2)]
            qbS, rqbS = sbc("qbS", [128, 512], BF16)
            qTzz, rqTzz = sbc("qTzz", [128, 2, 4, 128], BF16)
            gateS, rgateS = sbc("gateS", [128, 24])
            ropeS_sb, rropeS = sbc("ropeS_sb", [128, 1, 16])
            ws00B, rws00B = sbc("ws00B", [128, 8])
            bs0B, rbs0B = sbc("bs0B", [128, 8])
            pte, rpte = sbc("pte", [128, NSAMP], I32)
            pm8t, rpm8 = sbc("pm8t", [128, 1])
            idxf, ridxf = sbc("idxf", [128, NSAMP])
            idxi, ridxi = sbc("idxi", [128, NSAMP], I32)
            fbs_t, rfbs = sbc("fbs_t", [2, 33])
            bmask_t, rbmask = sbc("bmask_t", [8, 2])
            maskw, rmaskw = sbc("maskw", [128, 4, 8])
            Ps32, rPs32 = sbc("Ps32", [128, 16, 8])
            Pw32, rPw32 = sbc("Pw32", [128, 4, 8])
            onesc, ronesc = sbc("onesc", [128, 1], BF16)
            identf, ridf = sbc("identf", [128, 128])
            bnS, rbnS = sbc("bnS", [128, 8])
            prodS, rprodS = mixS[:, 512:1024], rmixS
            pnew, rpnew = sbc("pnew", [128, 2, 8])
            G = [sbc("G%d" % i, [128, 16, 128]) for i in range(4)]
            XB = [sbc("XB%d" % i, [128, 16, 128], BF16) for i in range(3)]
            XT = [sbc("XT%d" % i, [128, 16, 128], BF16) for i in range(3)]
            vsA, rvsA = sbc("vsA", [128, 16, 2, 65], BF16)
            GW = [sbc("GW%d" % i, [128, 4, 128]) for i in range(2)]
            kwB, rkwB = sbc("kwB", [128, 4, 128], BF16)
            kwT, rkwT = sbc("kwT", [128, 4, 128], BF16)
            vwA, rvwA = sbc("vwA", [128, 4, 2, 65], BF16)
            kccf, rkccf2 = sbc("kccf_s", [128, 128])
            kccb, rkccb2 = sbc("kccb_s", [128, 128], BF16)
            kccTs, rkccTs = sbc("kccTs", [128, 128], BF16)
            VCs, rVCs = sbc("VCs", [128, 2, 98], BF16)
            ovt, rovt = sbc("ovt", [128, 33])
            ovt2, rovt2 = sbc("ovt2", [128, 33])
            ovs, rovs = sbc("ovs", [128, 33], BF16)
            PcTs, rPcTs = sbc("PcTs", [128, 8], BF16)
            sm8, rsm8 = sbc("sm8", [8, 4])
            Rm, rRm = sbc("Rm", [8, 2], BF16)
            Pm, rPm = sbc("Pm", [8, 128], BF16)
            Pn2s, rPn2s = sbc("Pn2s", [128, 2], BF16)
            impS, rimpS = sbc("impS", [2, 33])
            impS2, rimpS2 = sbc("impS2", [2, 33])
            mxS, rmxS = sbc("mxS", [2, 16])
            m01, rm01 = sbc("m01", [2, 33])
            mexp, rmexp = sbc("mexp", [2, 128], BF16)
            maskTs, rmaskTs = sbc("maskTs", [128, 2])
            PsTs, rPsTs = sbc("PsTs", [128, 16, 8], BF16)
            PwTs, rPwTs = sbc("PwTs", [128, 4, 8], BF16)
            OsT, rOsT = sbc("OsT", [65, 128])
            Osb, rOsb = sbc("Osb", [128, 65])
            Otok, rOtok = sbc("Otok", [128, 3, 8, 65])
            rdS, rrdS = sbc("rdS", [128, 8])
            facS, rfacS = sbc("facS", [128, 8])
            XBK, rXBK = CC[:, 1, :], Res("XBK")
            CCc, rCCc = CC[:, 0, :], Res("CCc")

            S.pool(lambda e: e.memset(xS[:], 0.0), writes=[rxS])
            S.dma(lambda e: e.dma_start(out=xS[0:NSAMP, :], in_=xs[:, :]), reads=[], writes=[rxS])
            for kc in range(8):
                stg, rstg = stgc[kc % 2]
                S.dma(lambda e, stg=stg, kc=kc: e.dma_start(out=stg[:], in_=w_out[kc * 128:(kc + 1) * 128, :]), writes=[rstg])
                S.dve(lambda e, stg=stg, kc=kc: e.tensor_copy(out=WOc[:, kc, :], in_=stg[:]), reads=[rstg], writes=[rWOc])
            load_w(WKVc, rWKVc, C_KV, 768, stgc, 0)
            S.dma(lambda e: e.dma_start(out=ropeS_sb[:, 0, :], in_=ropeS[:, :]), writes=[rropeS])
            S.dma(lambda e: e.dma_start(out=ws00B[:], in_=bass.AP(ws00.tensor, 0, [[0, 128], [1, 8]])), writes=[rws00B])
            S.dma(lambda e: e.dma_start(out=bs0B[:], in_=bass.AP(bs0.tensor, 0, [[0, 128], [1, 8]])), writes=[rbs0B])
            S.dma(lambda e: e.dma_start(out=pte[:], in_=pt_e[:, :]), writes=[rpte])
            S.dma(lambda e: e.dma_start(out=pm8t[:], in_=pm8[:, :]), writes=[rpm8])
            S.dma(lambda e: e.dma_start(out=fbs_t[:], in_=fbs[:, :]), writes=[rfbs])
            S.dma(lambda e: e.dma_start(out=bmask_t[:], in_=bmask[:, :]), writes=[rbmask])
            S.dve(lambda e: e.tensor_copy(out=idxf[:], in_=pte[:]), reads=[rpte], writes=[ridxf])
            S.dve(lambda e: e.tensor_scalar(out=idxf[:], in0=idxf[:], scalar1=8.0, scalar2=pm8t[:, 0:1], op0=ALU.mult, op1=ALU.add),
                  reads=[ridxf, rpm8], writes=[ridxf])
            S.dve(lambda e: e.tensor_copy(out=idxi[:], in_=idxf[:]), reads=[ridxf], writes=[ridxi])
            S.pool(lambda e: e.memset(maskw[:], 1.0), writes=[rmaskw])
            S.pool(lambda e: e.memset(maskw[0:1, 0, :], 0.0), reads=[rmaskw], writes=[rmaskw])
            S.pool(lambda e: e.memset(onesc[:], 1.0), writes=[ronesc])
            S.pool(lambda e: e.memset(identf[:], 0.0), writes=[ridf])
            S.pool(lambda e: e.affine_select(out=identf[:], in_=identf[:], pattern=[[-1, 128]], compare_op=ALU.not_equal,
                                             fill=1.0, base=0, channel_multiplier=1), reads=[ridf], writes=[ridf])
            S.pool(lambda e: e.memset(qTzz[:], 0.0), writes=[rqTzz])
            S.pool(lambda e: e.memset(vsA[:, :, :, 64:65], 1.0), writes=[rvsA])
            S.pool(lambda e: e.memset(vwA[:, :, :, 64:65], 1.0), writes=[rvwA])
            S.pool(lambda e: e.memset(VCs[:], 0.0), writes=[rVCs])
            S.pool(lambda e: e.memset(VCs[:, :, 64:65], 1.0), reads=[rVCs], writes=[rVCs])
            S.pool(lambda e: e.memset(kccf[:], 0.0), writes=[rkccf2])
            S.pool(lambda e: e.memset(Otok[:], 1.0), writes=[rOtok])
            zc, rzc = sbc("zc", [128, 128], BF16)
            S.pool(lambda e: e.memset(zc[:], 0.0), writes=[rzc])
            for (ACC_, rACC_) in ((CCc, rCCc), (AS, rAS), (AW, rAW)):
                S.pe(lambda e, ACC_=ACC_: e.matmul(ACC_[:, 0:128], lhsT=zc[:], rhs=zc[:], start=True, stop=True), reads=[rzc], writes=[rACC_])
            S.pool(lambda e: e.iota(ovt[:], pattern=[[-64, 33]], base=0, channel_multiplier=16,
                                    allow_small_or_imprecise_dtypes=True), writes=[rovt])
            S.dve(lambda e: e.tensor_scalar(out=ovt2[:], in0=ovt[:], scalar1=-32.0, scalar2=None, op0=ALU.is_gt),
                  reads=[rovt], writes=[rovt2])
            S.dve(lambda e: e.scalar_tensor_tensor(out=ovt[:], in0=ovt[:], scalar=64.0, in1=ovt2[:], op0=ALU.is_lt, op1=ALU.mult),
                  reads=[rovt, rovt2], writes=[rovt])
            S.dve(lambda e: e.tensor_copy(out=ovs[:], in_=ovt[:]), reads=[rovt], writes=[rovs])
            for h in range(2):
                S.dve(lambda e, h=h: e.tensor_copy(out=VCs[:, h, 65:98], in_=ovt[:]), reads=[rovt], writes=[rVCs])

            rms_rstd(xS, rxS, hnS, rhnS)
            S.dve(lambda e: e.tensor_scalar(out=hnS[:], in0=xS[:], scalar1=stat[:, 2:3], scalar2=None, op0=ALU.mult),
                  reads=[rxS, rstat], writes=[rhnS])
            i = nextP()
            for kc in range(8):
                S.pe(lambda e, kc=kc, i=i: e.transpose(out=Pb[i][:, kc * 128:(kc + 1) * 128], in_=hnS[:, kc * 128:(kc + 1) * 128],
                                                       identity=identb[:]), reads=[rhnS, rIdb], writes=[rP[i]])
            S.act(lambda e, i=i: e.copy(out=hnTS[:].rearrange("p a b -> p (a b)"), in_=Pb[i][:, :]), reads=[rP[i]], writes=[rhnTS])
            blocks = [(WI, rWI, 512 * k, 512, 512 * k) for k in range(5)] + [(WI, rWI, C_G, 24, ZG)] + \
                     [(WKVc, rWKVc, 0, 512, C_KV), (WKVc, rWKVc, 512, 256, C_KV + 512)]
            for bi, (W_, rW_, c0, cw, z0) in enumerate(blocks):
                i = nextP()
                for kc in range(8):
                    S.pe(lambda e, kc=kc, i=i, c0=c0, cw=cw, W_=W_: e.matmul(Pf[i][:, 0:cw], lhsT=hnTS[:, kc, :], rhs=W_[:, kc, c0:c0 + cw],
                                                                            start=(kc == 0), stop=(kc == 7)),
                         reads=[rhnTS, rW_], writes=[rP[i]])
                if bi % 2 == 0:
                    S.act(lambda e, i=i, z0=z0, cw=cw: e.copy(out=zS[:, z0:z0 + cw], in_=Pf[i][:, 0:cw]), reads=[rP[i]], writes=[rzS])
                else:
                    S.dve(lambda e, i=i, z0=z0, cw=cw: e.tensor_copy(out=zS[:, z0:z0 + cw], in_=Pf[i][:, 0:cw]), reads=[rP[i]], writes=[rzS])
            rope(zS, rzS, 0, 8, ropeS_sb, rropeS, 0)
            rope(zS, rzS, C_KV, 4, ropeS_sb, rropeS, 0)
            S.act(lambda e: e.copy(out=qbS[:], in_=zS[:, 0:512]), reads=[rzS], writes=[rqbS])
            i = nextP()
            for a in range(4):
                S.pe(lambda e, a=a, i=i: e.transpose(out=Pb[i][:, a * 128:(a + 1) * 128], in_=qbS[:, a * 128:(a + 1) * 128],
                                                     identity=identb[:]), reads=[rqbS, rIdb], writes=[rP[i]])
            for h in range(2):
                S.act(lambda e, i=i, h=h: e.copy(out=qTzz[64 * h:64 * h + 64, h, :, :],
                                                 in_=Pb[i][64 * h:64 * h + 64, 0:512].rearrange("p (a b) -> p a b", a=4)),
                      reads=[rP[i]], writes=[rqTzz])
            S.act(lambda e: e.activation(out=gateS[:], in_=zS[:, ZG:ZG + 24], func=AF.Exp, scale=-1.0), reads=[rzS], writes=[rgateS])
            S.dve(lambda e: e.tensor_scalar(out=gateS[:], in0=gateS[:], scalar1=1.0, scalar2=None, op0=ALU.add), reads=[rgateS], writes=[rgateS])
            S.dve(lambda e: e.reciprocal(out=gateS[:], in_=gateS[:]), reads=[rgateS], writes=[rgateS])
            S.act(lambda e: e.activation(out=mixS[:], in_=zS[:, ZZA:ZZA + 1024], func=AF.Exp, scale=-1.0), reads=[rzS], writes=[rmixS])
            S.dve(lambda e: e.tensor_scalar(out=mixS[:], in0=mixS[:], scalar1=1.0, scalar2=None, op0=ALU.add), reads=[rmixS], writes=[rmixS])
            S.dve(lambda e: e.reciprocal(out=mixS[:], in_=mixS[:]), reads=[rmixS], writes=[rmixS])
            S.dve(lambda e: e.tensor_tensor(out=zS[:, ZZA:ZZA + 1024], in0=mixS[:], in1=zS[:, ZZA:ZZA + 1024], op=ALU.mult),
                  reads=[rmixS, rzS], writes=[rzS])
            zvS = zS[:, ZV:ZV + 512]
            S.dve(lambda e: e.bn_stats(out=bnS[:, 0:6], in_=zvS), reads=[rzS], writes=[rbnS])
            S.dve(lambda e: e.bn_aggr(out=stat[:, 4:6], in_=bnS[:, 0:6]), reads=[rbnS], writes=[rstat])
            S.act(lambda e: e.activation(out=stat[:, 6:7], in_=stat[:, 5:6], func=AF.Ln, bias=epsT[:], scale=1.0), reads=[rstat, repsT], writes=[rstat])
            S.act(lambda e: e.activation(out=stat[:, 7:8], in_=stat[:, 6:7], func=AF.Exp, scale=-0.5), reads=[rstat], writes=[rstat])
            S.dve(lambda e: e.tensor_scalar(out=zvS, in0=zvS, scalar1=stat[:, 4:5], scalar2=stat[:, 7:8], op0=ALU.subtract, op1=ALU.mult),
                  reads=[rzS, rstat], writes=[rzS])
            S.dve(lambda e: e.tensor_tensor(out=zvS, in0=zvS, in1=lngB[:], op=ALU.mult), reads=[rzS, rlngB], writes=[rzS])
            S.dve(lambda e: e.tensor_tensor(out=zvS, in0=zvS, in1=lnbB[:], op=ALU.add), reads=[rzS, rlnbB], writes=[rzS])
            S.dma(lambda e: e.dma_start(out=kvs[:, :], in_=zS[0:NSAMP, C_KV:C_KV + 768]), reads=[rzS], q="pool")
            kvdr = Res("dram_win", dram=True)
            S.dma(lambda e: e.dma_start(out=kwin_o[:, 0:511, :], in_=kwin[:, 1:512, :]), reads=[kvdr], writes=[])
            S.dma(lambda e: e.dma_start(out=vwin_o[:, 0:511, :], in_=vwin[:, 1:512, :]), reads=[kvdr], writes=[])
            S.dma(lambda e: e.dma_start(out=kwin_o[:, 511, :], in_=zS[0:NSAMP, C_KW:C_KW + 128]), reads=[rzS], q="pool")
            S.dma(lambda e: e.dma_start(out=vwin_o[:, 511, :], in_=zS[0:NSAMP, C_VW:C_VW + 128]), reads=[rzS], q="pool")
            S.dma(lambda e: e.dma_start(out=vns[:, :], in_=zS[0:NSAMP, ZV:ZV + 512]), reads=[rzS], q="pool")
            for g in range(8):
                S.dve(lambda e, g=g: e.tensor_scalar(out=mixS[:, g * 64:(g + 1) * 64], in0=zS[:, ZV + g * 64:ZV + (g + 1) * 64],
                                                     scalar1=ws00B[:, g:g + 1], scalar2=bs0B[:, g:g + 1], op0=ALU.mult, op1=ALU.add),
                      reads=[rzS, rws00B, rbs0B], writes=[rmixS])
            S.dve(lambda e: e.tensor_tensor(out=mixS[:, 0:512], in0=mixS[:, 0:512], in1=zS[:, ZU:ZU + 512], op=ALU.mult),
                  reads=[rmixS, rzS], writes=[rmixS])
            for wi_, kcol in enumerate((C_KS, C_KW)):
                S.dve(lambda e, kcol=kcol: e.tensor_tensor(
                    out=prodS.rearrange("p (a h d) -> p a h d", a=4, h=2),
                    in0=zS[:, 0:512].rearrange("p (a h d) -> p a h d", a=4, h=2),
                    in1=sap(zS, 0, 128, kcol, [[0, 4], [64, 2], [1, 64]]), op=ALU.mult), reads=[rzS], writes=[rprodS])
                S.dve(lambda e, wi_=wi_: e.tensor_reduce(
                    out=sap(pnew, 0, 128, wi_ * 8, [[1, 4], [4, 2]]),
                    in_=prodS.rearrange("p (a h d) -> p a h d", a=4, h=2), axis=mybir.AxisListType.X, op=ALU.add),
                    reads=[rprodS], writes=[rpnew])
            S.act(lambda e: e.activation(out=pnew[:], in_=pnew[:], func=AF.Exp, scale=SCALE), reads=[rpnew], writes=[rpnew])

            for b in range(n_samp):
                qbd = sap(qTzz, 0, 128, b, [[512, 2], [128, 4]])
                for ci in range(4):
                    g_, rg_ = G[ci]
                    S.dma(lambda e, ci=ci, g_=g_, b=b: e.indirect_dma_start(
                        out=g_[:].rearrange("p a b -> p (a b)"), out_offset=None, in_=pools[ci][:, :],
                        in_offset=bass.IndirectOffsetOnAxis(ap=idxi[:, b:b + 1], axis=0)),
                        reads=[ridxi], writes=[rg_], q="pool")
                for wi_, wsrc in enumerate((kwin, vwin)):
                    gw, rgw = GW[wi_]
                    S.dma(lambda e, gw=gw, wsrc=wsrc, b=b: e.dma_start(
                        out=gw[:], in_=wsrc[b, :, :].rearrange("(p c) f -> p c f", c=4)), writes=[rgw])
                for ci in range(3):
                    S.dve(lambda e, ci=ci: e.tensor_copy(out=XB[ci][0][:], in_=G[ci][0][:]), reads=[G[ci][1]], writes=[XB[ci][1]])
                S.act(lambda e: e.copy(out=vsA[:, :, :, 0:64], in_=G[3][0][:].rearrange("p c (h d) -> p c h d", h=2)),
                      reads=[G[3][1]], writes=[rvsA])
                S.dve(lambda e: e.tensor_copy(out=kwB[:], in_=GW[0][0][:]), reads=[GW[0][1]], writes=[rkwB])
                S.act(lambda e: e.copy(out=vwA[:, :, :, 0:64], in_=GW[1][0][:].rearrange("p c (h d) -> p c h d", h=2)),
                      reads=[GW[1][1]], writes=[rvwA])
                for ci in range(3):
                    for half in range(2):
                        i = nextP()
                        for cc in range(8):
                            c = half * 8 + cc
                            S.pe(lambda e, ci=ci, c=c, cc=cc, i=i: e.transpose(out=Pb[i][:, cc * 128:(cc + 1) * 128], in_=XB[ci][0][:, c, :],
                                                                                identity=identb[:]), reads=[XB[ci][1], rIdb], writes=[rP[i]])
                        S.act(lambda e, ci=ci, half=half, i=i: e.copy(
                            out=XT[ci][0][:, half * 8:(half + 1) * 8, :].rearrange("p a b -> p (a b)"), in_=Pb[i][:, :]),
                            reads=[rP[i]], writes=[XT[ci][1]])
                i = nextP()
                for c in range(4):
                    S.pe(lambda e, c=c, i=i: e.transpose(out=Pb[i][:, c * 128:(c + 1) * 128], in_=kwB[:, c, :], identity=identb[:]),
                         reads=[rkwB, rIdb], writes=[rP[i]])
                S.act(lambda e, i=i: e.copy(out=kwT[:].rearrange("p a b -> p (a b)"), in_=Pb[i][:, 0:512]), reads=[rP[i]], writes=[rkwT])
                for kv in range(2):
                    Wc_, rWc_ = (Wck, rWck) if kv == 0 else (Wcv, rWcv)
                    for h in range(2):
                        for l in range(32):
                            c, sh = l % 16, l // 16
                            S.pe(lambda e, l=l, c=c, sh=sh, h=h, kv=kv, Wc_=Wc_: e.matmul(
                                Sb[h][0:127, 0:64], lhsT=sap(XT[kv][0], 64 * h, 64, c * 128 + sh, [[1, 127]]),
                                rhs=sap(Wc_, 64 * h, 64, l * 64, [[1, 64]]), start=(l == 0), stop=(l == 31)),
                                reads=[XT[kv][1], rWc_], writes=[rSb[h]])
                    for h in range(2):
                        if kv == 0:
                            S.dve(lambda e, h=h: e.tensor_tensor(out=kccf[0:127, 64 * h:64 * h + 64], in0=Sb[h][0:127, 0:64],
                                                                 in1=cKB[0:127, 64 * h:64 * h + 64], op=ALU.add),
                                  reads=[rSb[h], rcKB], writes=[rkccf2])
                        else:
                            S.dve(lambda e, h=h: e.tensor_tensor(out=VCs[0:127, h, 0:64], in0=Sb[h][0:127, 0:64],
                                                                 in1=cVB[0:127, 64 * h:64 * h + 64], op=ALU.add),
                                  reads=[rSb[h], rcVB], writes=[rVCs])
                rope(kccf, rkccf2, 0, 2, rC, rrC, 0)
                S.dve(lambda e: e.tensor_copy(out=kccb[:], in_=kccf[:]), reads=[rkccf2], writes=[rkccb2])
                i = nextP()
                S.pe(lambda e, i=i: e.transpose(out=Pb[i][:, 0:128], in_=kccb[:], identity=identb[:]), reads=[rkccb2, rIdb], writes=[rP[i]])
                S.act(lambda e, i=i: e.copy(out=kccTs[:], in_=Pb[i][:, 0:128]), reads=[rP[i]], writes=[rkccTs])
                S.pe(lambda e, qbd=qbd: e.matmul(XBK[0:127, 0:8], lhsT=kccTs[:, 0:127], rhs=qbd, start=True, stop=True),
                     reads=[rkccTs, rqTzz], writes=[rXBK])
                S.act(lambda e: e.activation(out=PcTs[0:127, :], in_=XBK[0:127, 0:8], func=AF.Exp, scale=SCALE), reads=[rXBK], writes=[rPcTs])
                for h in range(2):
                    S.pe(lambda e, h=h, b=b: e.matmul(CCc[0:65, b * 8 + 4 * h:b * 8 + 4 * h + 4], lhsT=VCs[0:127, h, 0:65],
                                                     rhs=PcTs[0:127, 4 * h:4 * h + 4], start=True, stop=True),
                         reads=[rVCs, rPcTs], writes=[rCCc])
                S.pe(lambda e: e.matmul(XBK[0:8, 16:17], lhsT=PcTs[0:127, :], rhs=onesc[0:127, :], start=True, stop=True),
                     reads=[rPcTs, ronesc], writes=[rXBK])
                S.dve(lambda e: e.reciprocal(out=sm8[:, 0:1], in_=XBK[0:8, 16:17]), reads=[rXBK], writes=[rsm8])
                S.dve(lambda e: e.tensor_scalar(out=Rm[:], in0=bmask_t[:], scalar1=sm8[:, 0:1], scalar2=None, op0=ALU.mult),
                      reads=[rbmask, rsm8], writes=[rRm])
                i = nextP()
                S.pe(lambda e, i=i: e.transpose(out=Pb[i][0:8, 0:127], in_=PcTs[0:127, :], identity=identb[0:127, 0:127]),
                     reads=[rPcTs, rIdb], writes=[rP[i]])
                S.act(lambda e, i=i: e.copy(out=Pm[:, 0:127], in_=Pb[i][0:8, 0:127]), reads=[rP[i]], writes=[rPm])
                S.pe(lambda e: e.matmul(XBK[0:127, 24:26], lhsT=Pm[:, 0:127], rhs=Rm[:], start=True, stop=True),
                     reads=[rPm, rRm], writes=[rXBK])
                S.act(lambda e: e.copy(out=Pn2s[0:127, :], in_=XBK[0:127, 24:26]), reads=[rXBK], writes=[rPn2s])
                S.pe(lambda e: e.matmul(XBK[0:2, 32:65], lhsT=Pn2s[0:127, :], rhs=ovs[0:127, :], start=True, stop=True),
                     reads=[rPn2s, rovs], writes=[rXBK])
                S.dve(lambda e: e.tensor_tensor(out=impS[:], in0=XBK[0:2, 32:65], in1=fbs_t[:], op=ALU.add), reads=[rXBK, rfbs], writes=[rimpS])
                S.dve(lambda e: e.max(out=mxS[:, 0:8], in_=impS[:]), reads=[rimpS], writes=[rmxS])
                S.dve(lambda e: e.match_replace(out=impS2[:], in_to_replace=mxS[:, 0:8], in_values=impS[:], imm_value=-3e38),
                      reads=[rimpS, rmxS], writes=[rimpS2])
                S.dve(lambda e: e.max(out=mxS[:, 8:16], in_=impS2[:]), reads=[rimpS2], writes=[rmxS])
                S.dve(lambda e: e.tensor_scalar(out=m01[:], in0=impS[:], scalar1=mxS[:, 15:16], scalar2=None, op0=ALU.is_ge),
                      reads=[rimpS, rmxS], writes=[rm01])
                for r4 in range(4):
                    S.dve(lambda e, r4=r4: e.tensor_copy(out=sap(mexp, 0, 2, r4, [[4, 32]]), in_=m01[:, 0:32]), reads=[rm01], writes=[rmexp])
                S.pe(lambda e: e.matmul(XBK[:, 72:74], lhsT=mexp[:, :], rhs=identb[0:2, 0:2], start=True, stop=True),
                     reads=[rmexp, rIdb], writes=[rXBK])
                S.dve(lambda e: e.tensor_copy(out=maskTs[:], in_=XBK[:, 72:74]), reads=[rXBK], writes=[rmaskTs])
                i = nextP()
                for c in range(16):
                    S.pe(lambda e, c=c, i=i, qbd=qbd: e.matmul(Pf[i][:, c * 8:(c + 1) * 8], lhsT=XT[2][0][:, c, :], rhs=qbd, start=True, stop=True),
                         reads=[XT[2][1], rqTzz], writes=[rP[i]])
                S.act(lambda e, i=i: e.activation(out=Ps32[:].rearrange("p a b -> p (a b)"), in_=Pf[i][:, 0:128], func=AF.Exp, scale=SCALE),
                      reads=[rP[i]], writes=[rPs32])
                for h in range(2):
                    S.dve(lambda e, h=h: e.tensor_scalar(out=PsTs[:, :, 4 * h:4 * h + 4], in0=Ps32[:, :, 4 * h:4 * h + 4],
                                                         scalar1=maskTs[:, h:h + 1], scalar2=None, op0=ALU.mult),
                          reads=[rPs32, rmaskTs], writes=[rPsTs])
                for h in range(2):
                    for c in range(16):
                        S.pe(lambda e, h=h, c=c, b=b: e.matmul(AS[0:65, b * 8 + 4 * h:b * 8 + 4 * h + 4], lhsT=vsA[:, c, h, :],
                                                              rhs=PsTs[:, c, 4 * h:4 * h + 4], start=(c == 0), stop=(c == 15)),
                             reads=[rvsA, rPsTs], writes=[rAS])
                i = nextP()
                for c in range(4):
                    S.pe(lambda e, c=c, i=i, qbd=qbd: e.matmul(Pf[i][:, c * 8:(c + 1) * 8], lhsT=kwT[:, c, :], rhs=qbd, start=True, stop=True),
                         reads=[rkwT, rqTzz], writes=[rP[i]])
                S.act(lambda e, i=i: e.activation(out=Pw32[:].rearrange("p a b -> p (a b)"), in_=Pf[i][:, 0:32], func=AF.Exp, scale=SCALE),
                      reads=[rP[i]], writes=[rPw32])
                S.dve(lambda e: e.tensor_tensor(out=PwTs[:], in0=Pw32[:], in1=maskw[:], op=ALU.mult), reads=[rPw32, rmaskw], writes=[rPwTs])
                for h in range(2):
                    for c in range(4):
                        S.pe(lambda e, h=h, c=c, b=b: e.matmul(AW[0:65, b * 8 + 4 * h:b * 8 + 4 * h + 4], lhsT=vwA[:, c, h, :],
                                                              rhs=PwTs[:, c, 4 * h:4 * h + 4], start=(c == 0), stop=(c == 3)),
                             reads=[rvwA, rPwTs], writes=[rAW])

            rscr = Res("scr_dram", dram=True)
            for br, (ACC, rACC) in enumerate(((CCc, rCCc), (AS, rAS), (AW, rAW))):
                S.act(lambda e, ACC=ACC: e.copy(out=OsT[:, :], in_=ACC[0:65, 0:128]), reads=[rACC], writes=[rOsT])
                i = nextP()
                S.pe(lambda e, i=i: e.transpose(out=Pf[i][:, 0:65], in_=OsT[:, :], identity=identf[0:65, 0:65]),
                     reads=[rOsT, ridf], writes=[rP[i]])
                S.dve(lambda e, i=i: e.tensor_copy(out=Osb[:], in_=Pf[i][:, 0:65]), reads=[rP[i]], writes=[rOsb])
                S.dma(lambda e, br=br: e.dma_start(out=scr[br, :, :], in_=Osb[:]), reads=[rOsb], writes=[rscr])
                S.dma(lambda e, br=br: e.dma_start(out=Otok[0:NSAMP, br, :, :].rearrange("p a b -> p (a b)"),
                                                   in_=scr[br, :, :].rearrange("(b h) d -> b (h d)", h=8)),
                      reads=[rscr], writes=[rOtok])
            for wi_, (br, vcol) in enumerate(((1, C_VS), (2, C_VW))):
                for hd in range(8):
                    S.dve(lambda e, br=br, hd=hd, vcol=vcol, wi_=wi_: e.scalar_tensor_tensor(
                        out=Otok[:, br, hd, 0:64], in0=zS[:, vcol + 64 * (hd // 4):vcol + 64 * (hd // 4) + 64],
                        scalar=pnew[:, wi_, hd:hd + 1], in1=Otok[:, br, hd, 0:64], op0=ALU.mult, op1=ALU.add),
                        reads=[rzS, rpnew, rOtok], writes=[rOtok])
                S.dve(lambda e, br=br, wi_=wi_: e.tensor_tensor(out=Otok[:, br, :, 64], in0=Otok[:, br, :, 64], in1=pnew[:, wi_, :], op=ALU.add),
                      reads=[rOtok, rpnew], writes=[rOtok])
            for br in range(3):
                S.dve(lambda e, br=br: e.tensor_scalar(out=rdS[:], in0=Otok[:, br, :, 64], scalar1=1e-30, scalar2=None, op0=ALU.max),
                      reads=[rOtok], writes=[rrdS])
                S.dve(lambda e: e.reciprocal(out=rdS[:], in_=rdS[:]), reads=[rrdS], writes=[rrdS])
                S.dve(lambda e, br=br: e.tensor_tensor(out=facS[:], in0=rdS[:], in1=sap(gateS, 0, 128, br, [[3, 8]]), op=ALU.mult),
                      reads=[rrdS, rgateS], writes=[rfacS])
                for hd in range(8):
                    dst = mixS[:, 512 + hd * 64:512 + hd * 64 + 64]
                    if br == 0:
                        S.dve(lambda e, hd=hd, dst=dst: e.tensor_scalar(out=dst, in0=Otok[:, 0, hd, 0:64], scalar1=facS[:, hd:hd + 1],
                                                                        scalar2=None, op0=ALU.mult), reads=[rOtok, rfacS], writes=[rmixS])
                    else:
                        S.dve(lambda e, hd=hd, dst=dst, br=br: e.scalar_tensor_tensor(out=dst, in0=Otok[:, br, hd, 0:64], scalar=facS[:, hd:hd + 1],
                                                                                      in1=dst, op0=ALU.mult, op1=ALU.add),
                              reads=[rOtok, rfacS, rmixS], writes=[rmixS])
            S.dve(lambda e: e.tensor_tensor(out=hnS[:], in0=mixS[:], in1=zS[:, ZZA:ZZA + 1024], op=ALU.mult), reads=[rmixS, rzS], writes=[rhnS])
            i = nextP()
            for kc in range(8):
                S.pe(lambda e, kc=kc, i=i: e.transpose(out=Pb[i][:, kc * 128:(kc + 1) * 128], in_=hnS[:, kc * 128:(kc + 1) * 128],
                                                       identity=identb[:]), reads=[rhnS, rIdb], writes=[rP[i]])
            S.act(lambda e, i=i: e.copy(out=hnTS[:].rearrange("p a b -> p (a b)"), in_=Pb[i][:, :]), reads=[rP[i]], writes=[rhnTS])
            for cb in range(2):
                i = nextP()
                for kc in range(8):
                    S.pe(lambda e, kc=kc, i=i, cb=cb: e.matmul(Pf[i][:, :], lhsT=hnTS[:, kc, :], rhs=WOc[:, kc, cb * 512:(cb + 1) * 512],
                                                               start=(kc == 0), stop=(kc == 7)), reads=[rhnTS, rWOc], writes=[rP[i]])
                S.dve(lambda e, i=i, cb=cb: e.tensor_tensor(out=xS[:, cb * 512:(cb + 1) * 512], in0=Pf[i][:, :],
                                                            in1=xS[:, cb * 512:(cb + 1) * 512], op=ALU.add), reads=[rP[i], rxS], writes=[rxS])
            rms_rstd(xS, rxS, hnS, rhnS)
            S.dve(lambda e: e.scalar_tensor_tensor(out=xS[:], in0=xS[:], scalar=stat[:, 2:3], in1=fgB[:], op0=ALU.mult, op1=ALU.mult),
                  reads=[rxS, rstat, rfgB], writes=[rxS])
            S.dma(lambda e: e.dma_start(out=ys[:, :], in_=xS[0:NSAMP, :]), reads=[rxS], q="pool")
            S.flush()

        with ExitStack() as sab:
            if "nosab" in dbg or not with_prompt:
                return nc, dict(S.cnt)
            sbab = mk_sb(sab)
            KT2, rKT2 = sbab("KT2", [128, 2, SEQ], BF16)
            rKT = [[Res("KT_%d_%d" % (i, t)) for t in range(NT_ALL)] for i in range(4)]
            VA, rVAfull = sbab("VA", [128, NT_ALL, 4, 65], BF16)
            kccT, rkccT = sbab("kccT", [128, 4, 128], BF16)
            VC, rVC = sbab("VC", [128, 4, 2, 193], BF16)

            with ExitStack() as sa:
                sba = mk_sb(sa)
                KcT, rKcT = sba("KcT", [128, 2, SEQ], BF16)
                rA, rrA = sba("ropeA_sb", [128, NT_ALL, 16])
                xA = [sba("xA%d" % i, [128, D]) for i in range(2)]
                hnA = [sba("hnA%d" % i, [128, D], BF16) for i in range(2)]
                hnTA = [sba("hnTA%d" % i, [128, 8, 128], BF16) for i in range(2)]
                zkv = [sba("zkv%d" % i, [128, 768]) for i in range(2)]
                kin = [sba("kin%d" % i, [128, 512], BF16) for i in range(2)]
                WKV, rWKV = sba("WKV", [128, 8, 768], BF16)
                load_w(WKV, rWKV, C_KV, 768, xA, 0)
                ovl, rovl = zkv[1][0][:, 0:512].rearrange("p (a b) -> p a b", a=4), zkv[1][1]
                ovl2, rovl2 = zkv[0][0][:, 0:512].rearrange("p (a b) -> p a b", a=4), zkv[0][1]
                kcc_f, rkccf = sba("kcc_f", [128, 128])
                kcc_b, rkccb = sba("kcc_b", [128, 128], BF16)
                S.dma(lambda e: e.dma_start(out=rA[:], in_=ropeA[:, :, :]), writes=[rrA])
                S.pool(lambda e: e.memset(VC[:], 0.0), writes=[rVC])
                S.pool(lambda e: e.memset(kccT[:], 0.0), writes=[rkccT])
                S.pool(lambda e: e.iota(ovl, pattern=[[2048, 4], [-64, 128]], base=0, channel_multiplier=16,
                                        allow_small_or_imprecise_dtypes=True), writes=[rovl])
                S.dve(lambda e: e.tensor_scalar(out=ovl2, in0=ovl, scalar1=-32.0, scalar2=None, op0=ALU.is_gt),
                      reads=[rovl], writes=[rovl2])
                S.dve(lambda e: e.scalar_tensor_tensor(out=ovl, in0=ovl, scalar=64.0, in1=ovl2,
                                                       op0=ALU.is_lt, op1=ALU.mult), reads=[rovl, rovl2], writes=[rovl])
                for h in range(2):
                    S.dve(lambda e, h=h: e.tensor_copy(out=VC[:, :, h, 65:193], in_=ovl), reads=[rovl], writes=[rVC])
                S.dve(lambda e: e.memset(VC[:, :, :, 64:65], 1.0), reads=[], writes=[rVC])
                S.dve(lambda e: e.memset(VA[:, :, :, 64:65], 1.0), writes=[rVAfull])
                S.dve(lambda e: e.memset(kcc_f[:], 0.0), writes=[rkccf])
                Sbb = [S0.bitcast(BF16), S1.bitcast(BF16)]
                Abb = [(AS.bitcast(BF16), rAS), (AW.bitcast(BF16), rAW)]

                hnTA3 = hnTA + [sba("hnTA2", [128, 8, 128], BF16)]

                def stage1(t):
                    x_t, rx = xA[t % 2]
                    hn, rhn = hnA[t % 2]
                    hnT, rhnT = hnTA3[t % 3]
                    S.dma(lambda e: e.dma_start(out=x_t[:], in_=xb[t * 128:(t + 1) * 128, :]), writes=[rx])
                    rms_rstd(x_t, rx, hn, rhn)
                    S.dve(lambda e: e.tensor_scalar(out=hn[:], in0=x_t[:], scalar1=stat[:, 2:3], scalar2=None, op0=ALU.mult),
                          reads=[rx, rstat], writes=[rhn])
                    bk, rbk = Sbb[t % 2], rSb[t % 2]
                    for kc in range(8):
                        S.pe(lambda e, kc=kc: e.transpose(out=bk[:, kc * 128:(kc + 1) * 128], in_=hn[:, kc * 128:(kc + 1) * 128],
                                                          identity=identb[:]), reads=[rhn, rIdb], writes=[rbk])
                    S.act(lambda e: e.copy(out=hnT[:].rearrange("p a b -> p (a b)"), in_=bk[:, :]), reads=[rbk], writes=[rhnT])

                def stageMM(t):
                    hnT, rhnT = hnTA3[t % 3]
                    z, rz = zkv[t % 2]
                    for kc in range(8):
                        S.pe(lambda e, kc=kc: e.matmul(P0[:, :], lhsT=hnT[:, kc, :], rhs=WKV[:, kc, 0:512],
                                                       start=(kc == 0), stop=(kc == 7)), reads=[rhnT, rWKV], writes=[rP0])
                    S.act(lambda e: e.copy(out=z[:, 0:512], in_=P0[:, :]), reads=[rP0], writes=[rz])
                    for kc in range(8):
                        S.pe(lambda e, kc=kc: e.matmul(P1[:, 0:256], lhsT=hnT[:, kc, :], rhs=WKV[:, kc, 512:768],
                                                       start=(kc == 0), stop=(kc == 7)), reads=[rhnT, rWKV], writes=[rP1])
                    S.act(lambda e: e.copy(out=z[:, 512:768], in_=P1[:, 0:256]), reads=[rP1], writes=[rz])

                def stageR(t):
                    z, rz = zkv[t % 2]
                    kb, rkb = kin[t % 2]
                    rope(z, rz, 0, 4, rA, rrA, t)
                    S.dma(lambda e: e.dma_start(out=kv_all[t, :, :], in_=z[:]), reads=[rz], q="pool")
                    S.dve(lambda e: e.tensor_copy(out=kb[:], in_=z[:, 0:512]), reads=[rz], writes=[rkb])
                    S.dve(lambda e: e.tensor_copy(out=VA[:, t, :, 0:64], in_=z[:, 512:768].rearrange("p (a b) -> p a b", a=4)),
                          reads=[rz], writes=[rVAfull])

                def stageKT(t):
                    kb, rkb = kin[t % 2]
                    ab, rab = Abb[t % 2]
                    for a in range(4):
                        S.pe(lambda e, a=a: e.transpose(out=ab[:, a * 128:(a + 1) * 128], in_=kb[:, a * 128:(a + 1) * 128],
                                                        identity=identb[:]), reads=[rkb, rIdb], writes=[rab])
                    for a in range(2):
                        S.act(lambda e, a=a: e.copy(out=KT2[:, a, t * 128:(t + 1) * 128], in_=ab[:, a * 128:(a + 1) * 128]),
                              reads=[rab], writes=[rKT[a][t]])
                        S.act(lambda e, a=a: e.copy(out=KcT[:, a, t * 128:(t + 1) * 128], in_=ab[:, 256 + a * 128:256 + (a + 1) * 128]),
                              reads=[rab], writes=[rKT[2 + a][t]])

                for t in range(min(2, n_tiles_a)):
                    stage1(t)
                for t in range(n_tiles_a):
                    stageMM(t)
                    if t + 2 < n_tiles_a:
                        stage1(t + 2)
                    stageR(t)
                    if t >= 1:
                        stageKT(t - 1)
                if n_tiles_a > 0:
                    stageKT(n_tiles_a - 1)

                for N in range(4):
                    if 16 * N + 16 > n_tiles_a - (1 if N < 3 else 0):
                        break
                    nn = 128 if N < 3 else 127
                    rk_dep = [rKT[2][t] for t in range(16 * N, min(16 * N + 17, NT_ALL))]
                    rv_dep = [rKT[3][t] for t in range(16 * N, min(16 * N + 17, NT_ALL))]
                    cbanks = [(Sb[0], rSb[0]), (Sb[1], rSb[1]), (AS, rAS), (AW, rAW)]
                    for kv, (src_i, Wc_, rWc_, rdep) in enumerate(((0, Wck, rWck, rk_dep), (1, Wcv, rWcv, rv_dep))):
                        for h in range(2):
                            bk, rbk = cbanks[2 * kv + h]
                            for l in range(32):
                                S.pe(lambda e, l=l, h=h, N=N, nn=nn, bk=bk, src_i=src_i, Wc_=Wc_: e.matmul(
                                    bk[0:nn, 0:64],
                                    lhsT=sap(KcT, 64 * h, 64, src_i * SEQ + 2048 * N + l, [[16, nn]]),
                                    rhs=sap(Wc_, 64 * h, 64, l * 64, [[1, 64]]), start=(l == 0), stop=(l == 31)),
                                    reads=rdep + [rWc_], writes=[rbk])
                    for h in range(2):
                        bk, rbk = cbanks[h]
                        S.dve(lambda e, h=h, nn=nn, bk=bk: e.tensor_tensor(out=kcc_f[0:nn, 64 * h:64 * h + 64], in0=bk[0:nn, 0:64],
                                                                          in1=cKB[0:nn, 64 * h:64 * h + 64], op=ALU.add),
                              reads=[rbk, rcKB], writes=[rkccf])
                        bk, rbk = cbanks[2 + h]
                        S.dve(lambda e, h=h, nn=nn, N=N, bk=bk: e.tensor_tensor(
                            out=VC[0:nn, N, h, 0:64], in0=bk[0:nn, 0:64], in1=cVB[0:nn, 64 * h:64 * h + 64], op=ALU.add),
                            reads=[rbk, rcVB], writes=[rVC])
                    rope(kcc_f, rkccf, 0, 2, rC, rrC, N)
                    S.dve(lambda e: e.tensor_copy(out=kcc_b[:], in_=kcc_f[:]), reads=[rkccf], writes=[rkccb])
                    i = nextP()
                    S.pe(lambda e, i=i: e.transpose(out=Pb[i][:, 0:128], in_=kcc_b[:], identity=identb[:]),
                         reads=[rkccb, rIdb], writes=[rP[i]])
                    S.act(lambda e, i=i, N=N: e.copy(out=kccT[:, N, :], in_=Pb[i][:, 0:128]), reads=[rP[i]], writes=[rkccT])
                S.flush()

            with ExitStack() as sbs:
                sbb = mk_sb(sbs)
                WO, rWO = sbb("WO", [128, 8, D], BF16)
                EWq, rEWq = sbb("EWq", [128, 2048], BF16)
                xO = [sbb("xO%d" % i, [128, D]) for i in range(2)]
                zO, rzO = sbb("zO", [128, ZW])
                hnX = [sbb("hnX%d" % i, [128, D], BF16) for i in range(2)]
                hnTX = [sbb("hnTX%d" % i, [128, 8, 128], BF16) for i in range(2)]
                qb, rqb = sbb("qb", [128, 512], BF16)
                qTz = [sbb("qTz%d" % i, [128, 512], BF16) for i in range(2)]
                zb, rzb = sbb("zb", [128, 512], BF16)
                qpsX = [sbb("qps%d" % i, [128, 128]) for i in range(2)]
                cbcX = [sbb("cbc%d" % i, [128, 4, 128], BF16) for i in range(2)]
                cbsX = [sbb("cbs%d" % i, [128, 4, 128], BF16) for i in range(2)]
                cbwX = [sbb("cbw%d" % i, [128, 8, 128], BF16) for i in range(2)]
                wtmp, rwtmp = sbb("wtmp", [128, 128])
                PsT = [sbb("PsT%d" % i, [128, 512], BF16) for i in range(4)]
                rden, rrden = sbb("rden", [128, 16])
                fac, rfac = sbb("fac", [128, 4])
                imp, rimp = sbb("imp", [128, 128])
                imp2, rimp2 = sbb("imp2", [128, 128])
                mx8, rmx8 = sbb("mx8", [128, 16])
                nmask, rnmask = sbb("nmask", [128, 128], BF16)
                nmaskT = [[sbb("nmaskT%d_%d" % (i, gq), [128, 128], BF16) for gq in range(4)] for i in range(2)]
                gate, rgate = sbb("gate", [128, 24])
                mixf, rmixf = sbb("mixf", [128, D])
                rsg = Res("mixf_gmlp_half")
                vnb, rvnb = qb, rqb
                bnst, rbnst = sbb("bnst", [128, 8])
                sil, rsil = zO[:, ZZA:ZZA + 1024], rzO
                k64, rk64 = sbb("k64", [128, 128])
                e0, re0 = sbb("e0", [128, 128])
                ddt, rddt = sbb("ddt", [128, 128])
                fa, rfa = sbb("fa", [128, 128])
                ff, rff = sbb("ff", [128, 128])
                fbX = [sbb("fb%d" % i, [128, 128]) for i in range(2)]
                wst = xO
                for kc in range(8):
                    stg, rstg = wst[kc % 2]
                    S.dma(lambda e, stg=stg, kc=kc: e.dma_start(out=stg[:], in_=w_out[kc * 128:(kc + 1) * 128, :]),
                          writes=[rstg])
                    S.dve(lambda e, stg=stg, kc=kc: e.tensor_copy(out=WO[:, kc, :], in_=stg[:]),
                           reads=[rstg], writes=[rWO])
                S.pool(lambda e: e.memset(EWq[:], 1.0), writes=[rEWq])
                for gq in range(4):
                    S.pool(lambda e, gq=gq: e.affine_select(out=EWq[32 * gq:32 * gq + 32, :], in_=EWq[32 * gq:32 * gq + 32, :],
                                                            pattern=[[1, 2048]], compare_op=ALU.is_ge, fill=0.0, base=0,
                                                            channel_multiplier=-64), reads=[rEWq], writes=[rEWq])
                    S.pool(lambda e, gq=gq: e.affine_select(out=EWq[32 * gq:32 * gq + 32, :], in_=EWq[32 * gq:32 * gq + 32, :],
                                                            pattern=[[-1, 2048]], compare_op=ALU.is_ge, fill=0.0, base=63,
                                                            channel_multiplier=64), reads=[rEWq], writes=[rEWq])
                S.pool(lambda e: e.iota(k64[:], pattern=[[64, 128]], base=0, channel_multiplier=0,
                                        allow_small_or_imprecise_dtypes=True), writes=[rk64])
                S.pool(lambda e: e.memset(e0[:], 0.0), writes=[re0])
                S.pool(lambda e: e.memset(e0[:, 0:1], 1.0), reads=[re0], writes=[re0])
                psctr = [0]
                pcctr = [0]
                sbctr = [0]
                SB4 = [(S0, rS0), (S1, rS1), (P0, rP0)]
                TPb, rTP = Pb[1], rP1
                S.pool(lambda e: e.memset(zb[:], 0.0), writes=[rzb])
                for h in range(2):
                    S.pool(lambda e, h=h: e.memset(qTz[h][0][:], 0.0), writes=[qTz[h][1]])
                    for gq in range(4):
                        S.pool(lambda e, h=h, gq=gq: e.memset(nmaskT[h][gq][0][:], 0.0), writes=[nmaskT[h][gq][1]])

                def pre(s):
                    x_t, rx = xO[s % 2]
                    hnO, rhnO = hnX[s % 2]
                    hnTO, rhnTO = hnTX[s % 2]
                    qps, rqps = qpsX[s % 2]
                    cbc, rcbc = cbcX[s % 2]
                    cbs, rcbs = cbsX[s % 2]
                    cbw, rcbw = cbwX[s % 2]
                    fb, rfb = fbX[s % 2]
                    S.dma(lambda e: e.dma_start(out=x_t[:], in_=xo[s, :, :]), writes=[rx])
                    S.dma(lambda e: e.dma_start(out=qps[:], in_=bass.AP(qpos.tensor, s * 128, [[0, 128], [1, 128]])), writes=[rqps])
                    rms_rstd(x_t, rx, hnO, rhnO)
                    S.dve(lambda e: e.tensor_scalar(out=hnO[:], in0=x_t[:], scalar1=stat[:, 2:3], scalar2=None, op0=ALU.mult),
                          reads=[rx, rstat], writes=[rhnO])
                    for kc in range(8):
                        S.pe(lambda e, kc=kc: e.transpose(out=TPb[:, kc * 128:(kc + 1) * 128], in_=hnO[:, kc * 128:(kc + 1) * 128],
                                                          identity=identb[:]), reads=[rhnO, rIdb], writes=[rTP])
                    S.act(lambda e: e.copy(out=hnTO[:].rearrange("p a b -> p (a b)"), in_=TPb[:, :]), reads=[rTP], writes=[rhnTO])
                    S.dve(lambda e: e.tensor_scalar(out=ddt[:], in0=k64[:], scalar1=qposC[:, s:s + 1], scalar2=None,
                                                    op0=ALU.subtract), reads=[rk64, rqposC], writes=[rddt])
                    S.dve(lambda e: e.tensor_scalar(out=fa[:], in0=ddt[:], scalar1=0.0, scalar2=None, op0=ALU.is_le),
                          reads=[rddt], writes=[rfa])
                    S.dve(lambda e: e.scalar_tensor_tensor(out=ff[:], in0=ddt[:], scalar=-128.0, in1=fa[:],
                                                           op0=ALU.is_gt, op1=ALU.mult), reads=[rddt, rfa], writes=[rff])
                    S.dve(lambda e: e.tensor_tensor(out=ff[:], in0=ff[:], in1=e0[:], op=ALU.max), reads=[rff, re0], writes=[rff])
                    S.dve(lambda e: e.tensor_scalar(out=fa[:], in0=fa[:], scalar1=1.0, scalar2=1e30, op0=ALU.subtract,
                                                    op1=ALU.mult), reads=[rfa], writes=[rfa])
                    S.dve(lambda e: e.scalar_tensor_tensor(out=fb[:], in0=ff[:], scalar=1e4, in1=fa[:],
                                                           op0=ALU.mult, op1=ALU.add), reads=[rff, rfa], writes=[rfb])
                    n_ct = min(4, (32 * s + 31 + 127) // 128)
                    for nt in range(n_ct):
                        S.dve(lambda e, nt=nt: e.tensor_scalar(out=cbc[:, nt, :], in0=qps[:], scalar1=cend[:, nt:nt + 1],
                                                               scalar2=NEGM, op0=ALU.is_lt, op1=ALU.mult),
                              reads=[rqps, rcend], writes=[rcbc])
                    for jj in range(4):
                        j = 4 * s + jj
                        S.dve(lambda e, jj=jj, j=j: e.tensor_scalar(out=cbs[:, jj, :], in0=qps[:],
                                                                    scalar1=kpos[:, j:j + 1], scalar2=NEGM,
                                                                    op0=ALU.is_lt, op1=ALU.mult),
                              reads=[rqps, rkpos], writes=[rcbs])
                    for jj in range(8):
                        j = 4 * s - 4 + jj
                        if j < 0:
                            continue
                        S.dve(lambda e, j=j: e.tensor_scalar(out=wtmp[:], in0=qps[:], scalar1=-512.0,
                                                             scalar2=kpos[:, j:j + 1], op0=ALU.add, op1=ALU.is_ge),
                              reads=[rqps, rkpos], writes=[rwtmp])
                        if jj >= 4:
                            S.dve(lambda e, jj=jj: e.scalar_tensor_tensor(out=cbw[:, jj, :], in0=wtmp[:], scalar=NEGM,
                                                                          in1=cbs[:, jj - 4, :], op0=ALU.mult, op1=ALU.add),
                                  reads=[rwtmp, rcbs], writes=[rcbw])
                        else:
                            S.dve(lambda e, jj=jj: e.tensor_scalar(out=cbw[:, jj, :], in0=wtmp[:], scalar1=NEGM, scalar2=None,
                                                                   op0=ALU.mult), reads=[rwtmp], writes=[rcbw])

                if n_slots > 0:
                    pre(0)
                def slot_body(s):
                    x_t, rx = xO[s % 2]
                    hnO, rhnO = hnX[s % 2]
                    hnTO, rhnTO = hnTX[s % 2]
                    qps, rqps = qpsX[s % 2]
                    cbc, rcbc = cbcX[s % 2]
                    cbs, rcbs = cbsX[s % 2]
                    cbw, rcbw = cbwX[s % 2]
                    fb, rfb = fbX[s % 2]
                    blocks = [(512 * k, 512, 512 * k) for k in range(5)] + [(C_G, 24, ZG)]
                    for bi, (c0, cw, z0) in enumerate(blocks):
                        i = nextP()
                        for kc in range(8):
                            S.pe(lambda e, kc=kc, i=i, c0=c0, cw=cw: e.matmul(Pf[i][:, 0:cw], lhsT=hnTO[:, kc, :],
                                                                             rhs=WI[:, kc, c0:c0 + cw],
                                                                             start=(kc == 0), stop=(kc == 7)),
                                 reads=[rhnTO, rWI], writes=[rP[i]])
                        if bi % 2 == 0:
                            S.act(lambda e, i=i, z0=z0, cw=cw: e.copy(out=zO[:, z0:z0 + cw], in_=Pf[i][:, 0:cw]),
                                  reads=[rP[i]], writes=[rzO])
                        else:
                            S.dve(lambda e, i=i, z0=z0, cw=cw: e.tensor_copy(out=zO[:, z0:z0 + cw], in_=Pf[i][:, 0:cw]),
                                  reads=[rP[i]], writes=[rzO])
                    rope(zO, rzO, 0, 8, rO, rrO, s)
                    S.act(lambda e: e.copy(out=qb[:], in_=zO[:, 0:512]), reads=[rzO], writes=[rqb])
                    i = nextP()
                    for a in range(4):
                        S.pe(lambda e, a=a, i=i: e.transpose(out=Pb[i][:, a * 128:(a + 1) * 128],
                                                             in_=qb[:, a * 128:(a + 1) * 128], identity=identb[:]),
                             reads=[rqb, rIdb], writes=[rP[i]])
                    for h in range(2):
                        S.act(lambda e, i=i, h=h: e.copy(out=qTz[h][0][64 * h:64 * h + 64, :], in_=Pb[i][64 * h:64 * h + 64, 0:512]),
                              reads=[rP[i]], writes=[qTz[h][1]])
                    S.act(lambda e: e.activation(out=gate[:], in_=zO[:, ZG:ZG + 24], func=AF.Exp, scale=-1.0),
                          reads=[rzO], writes=[rgate])
                    S.dve(lambda e: e.tensor_scalar(out=gate[:], in0=gate[:], scalar1=1.0, scalar2=None, op0=ALU.add),
                          reads=[rgate], writes=[rgate])
                    S.dve(lambda e: e.reciprocal(out=gate[:], in_=gate[:]), reads=[rgate], writes=[rgate])
                    n_ct = min(4, (32 * s + 31 + 127) // 128)

                    def bc4(t_, col):
                        return sap(t_, 0, 128, col * 128, [[0, 4], [1, 128]])

                    def qTh(h):
                        return qTz[h][0][:, :]

                    def rqTh(h):
                        return qTz[h][1]

                    def zero_init(ACC, rACC, ncols):
                        S.pe(lambda e: e.matmul(ACC[:, 0:ncols], lhsT=zb[:, 0:128], rhs=zb[:, 0:ncols], start=True, stop=False),
                             reads=[rzb], writes=[rACC])

                    def gate_fac(h, br, den_ap, rsrc, clamp):
                        if clamp:
                            S.dve(lambda e: e.tensor_scalar(out=rden[:, 0:4], in0=den_ap, scalar1=1e-30, scalar2=None,
                                                            op0=ALU.max), reads=[rsrc], writes=[rrden])
                            S.dve(lambda e: e.reciprocal(out=rden[:, 0:4], in_=rden[:, 0:4]), reads=[rrden], writes=[rrden])
                        else:
                            S.dve(lambda e: e.reciprocal(out=rden[:, 0:4], in_=den_ap), reads=[rsrc], writes=[rrden])
                        S.dve(lambda e: e.tensor_tensor(out=fac[:], in0=rden[:, 0:4],
                                                        in1=sap(gate, 0, 128, 12 * h + br, [[3, 4]]), op=ALU.mult),
                              reads=[rrden, rgate], writes=[rfac])

                    items = []

                    def cmp_post(h):
                        den = sap(CC, 0, 128, 64, [[512, 2], [193, 2]])
                        gate_fac(h, 0, den, rCC, True)
                        for g in range(4):
                            hd = 4 * h + g
                            S.dve(lambda e, g=g, hd=hd: e.tensor_scalar(
                                out=mixf[:, 512 + hd * 64:512 + hd * 64 + 64],
                                in0=CC[:, g // 2, (g % 2) * 193:(g % 2) * 193 + 64],
                                scalar1=fac[:, g:g + 1], scalar2=None, op0=ALU.mult), reads=[rCC, rfac], writes=[rmixf])
                        for g in range(4):
                            src = fb[:] if g == 0 else imp[:]
                            S.dve(lambda e, g=g, src=src: e.scalar_tensor_tensor(
                                out=imp[:], in0=CC[:, g // 2, (g % 2) * 193 + 65:(g % 2) * 193 + 193],
                                scalar=rden[:, g:g + 1], in1=src, op0=ALU.mult, op1=ALU.add),
                                reads=[rCC, rrden, rfb, rimp], writes=[rimp])
                        S.dve(lambda e: e.max(out=mx8[:, 0:8], in_=imp[:]), reads=[rimp], writes=[rmx8])
                        S.dve(lambda e: e.match_replace(out=imp2[:], in_to_replace=mx8[:, 0:8], in_values=imp[:],
                                                        imm_value=-3e38), reads=[rimp, rmx8], writes=[rimp2])
                        S.dve(lambda e: e.max(out=mx8[:, 8:16], in_=imp2[:]), reads=[rimp2], writes=[rmx8])
                        S.dve(lambda e: e.tensor_scalar(out=nmask[:], in0=imp[:], scalar1=mx8[:, 15:16], scalar2=NEGM,
                                                        op0=ALU.is_lt, op1=ALU.mult), reads=[rimp, rmx8], writes=[rnmask])
                        S.pe(lambda e: e.transpose(out=TPb[:, 0:128], in_=nmask[:], identity=identb[:]),
                             reads=[rnmask, rIdb], writes=[rTP])
                        for gq in range(4):
                            nmT, rnmT = nmaskT[h][gq]
                            S.act(lambda e, nmT=nmT, gq=gq: e.copy(out=nmT[32 * gq:32 * gq + 32, :], in_=TPb[32 * gq:32 * gq + 32, 0:128]),
                                  reads=[rTP], writes=[rnmT])

                    def finish(ACC, rACC, br, h):
                        den = sap(ACC, 0, 128, 64, [[65, 4]])
                        gate_fac(h, br, den, rACC, False)
                        for g in range(4):
                            hd = 4 * h + g
                            dst = mixf[:, 512 + hd * 64:512 + hd * 64 + 64]
                            S.dve(lambda e, g=g, dst=dst: e.scalar_tensor_tensor(
                                out=dst, in0=ACC[:, g * 65:g * 65 + 64], scalar=fac[:, g:g + 1], in1=dst,
                                op0=ALU.mult, op1=ALU.add), reads=[rACC, rfac, rmixf], writes=[rmixf])

                    for h in range(2):
                        for nt in range(n_ct):
                            items.append(dict(
                                h=h, lhs_k=kccT[:, nt, :], rk=[rkccT], biases=[(identb[:], bc4(cbc, nt), [rIdb, rcbc])],
                                pv=[(CC[:, g // 2, (g % 2) * 193:(g % 2) * 193 + 193], g, VC[:, nt, h, :], rVC, rCC, g % 2 == 1) for g in range(4)],
                                pre=(lambda: [S.pe(lambda e, bkk=bkk: e.matmul(CC[:, bkk, 0:386], lhsT=zb[:, 0:128], rhs=zb[:, 0:386],
                                                                              start=True, stop=False), reads=[rzb], writes=[rCC])
                                              for bkk in range(2)]) if nt == 0 else None,
                                post=(lambda h=h: cmp_post(h)) if nt == n_ct - 1 else None))
                    for h in range(2):
                        jl = [jj for jj in range(8) if 4 * s - 4 + jj >= 0]
                        for ii, jj in enumerate(jl):
                            j = 4 * s - 4 + jj
                            items.append(dict(
                                h=h, lhs_k=KT2[:, 1, j * 128:(j + 1) * 128], rk=[rKT[1][j]],
                                biases=[(identb[:], bc4(cbw, jj), [rIdb, rcbw])],
                                pv=[(AW[:, g * 65:(g + 1) * 65], g, VA[:, j, 2 + h, :], rVAfull, rAW, g == 3) for g in range(4)],
                                pre=(lambda: zero_init(AW, rAW, 260)) if ii == 0 else None,
                                post=(lambda h=h: finish(AW, rAW, 2, h)) if ii == len(jl) - 1 else None))
                    for h in range(2):
                        nj = 4 * s + 4
                        for j in range(nj):
                            gq = j // 16
                            nmT, rnmT = nmaskT[h][gq]
                            biases = [(EWq[:, (j % 16) * 128:(j % 16 + 1) * 128], bc4(nmT, 0), [rEWq, rnmT])]
                            if j >= 4 * s:
                                biases.append((identb[:], bc4(cbs, j - 4 * s), [rIdb, rcbs]))
                            items.append(dict(
                                h=h, lhs_k=KT2[:, 0, j * 128:(j + 1) * 128], rk=[rKT[0][j]], biases=biases,
                                pv=[(AS[:, g * 65:(g + 1) * 65], g, VA[:, j, h, :], rVAfull, rAS, g == 3) for g in range(4)],
                                pre=(lambda: zero_init(AS, rAS, 260)) if j == 0 else None,
                                post=(lambda h=h: finish(AS, rAS, 1, h)) if j == nj - 1 else None))

                    def emit_scores(it):
                        si = sbctr[0] % len(SB4)
                        sbctr[0] += 1
                        bank, rbank = SB4[si]
                        nb = len(it["biases"])
                        S.pe(lambda e: e.matmul(bank[:, :], lhsT=it["lhs_k"], rhs=qTh(it["h"]), start=True, stop=(nb == 0)),
                             reads=it["rk"] + [rqTh(it["h"])], writes=[rbank])
                        for bi, (bl, br_, rdeps) in enumerate(it["biases"]):
                            S.pe(lambda e, bl=bl, br_=br_, bi=bi: e.matmul(bank[:, :], lhsT=bl, rhs=br_, start=False, stop=(bi == nb - 1)),
                                 reads=rdeps, writes=[rbank])
                        pt, rpt = PsT[psctr[0] % len(PsT)]
                        psctr[0] += 1
                        S.act(lambda e: e.activation(out=pt[:], in_=bank[:, :], func=AF.Exp, scale=SCALE), reads=[rbank], writes=[rpt])
                        it["pt"] = (pt, rpt)

                    def emit_pv(it):
                        if it["pre"] is not None:
                            it["pre"]()
                        pt, rpt = it["pt"]
                        npv = len(it["pv"])
                        for k_, (out_ap, g, rhs_ap, rrhs, racc, lastg) in enumerate(it["pv"]):
                            S.pe(lambda e, out_ap=out_ap, g=g, rhs_ap=rhs_ap, lastg=lastg: e.matmul(
                                out_ap, lhsT=pt[:, g * 128:(g + 1) * 128], rhs=rhs_ap, start=False,
                                stop=(it["post"] is not None and lastg)), reads=[rpt, rrhs], writes=[racc])
                        if it["post"] is not None:
                            it["post"]()

                    def mid_work():
                        for hf in range(2):
                            zz = zO[:, ZZA + 512 * hf:ZZA + 512 * (hf + 1)]
                            tmp = mixf[:, 0:512]
                            S.act(lambda e, zz=zz, tmp=tmp: e.activation(out=tmp, in_=zz, func=AF.Exp, scale=-1.0), reads=[rzO], writes=[rsg])
                            S.dve(lambda e, tmp=tmp: e.tensor_scalar(out=tmp, in0=tmp, scalar1=1.0, scalar2=None, op0=ALU.add), reads=[rsg], writes=[rsg])
                            S.dve(lambda e, tmp=tmp: e.reciprocal(out=tmp, in_=tmp), reads=[rsg], writes=[rsg])
                            S.dve(lambda e, zz=zz, tmp=tmp: e.tensor_tensor(out=zz, in0=tmp, in1=zz, op=ALU.mult), reads=[rsg, rzO], writes=[rzO])
                        zv = zO[:, ZV:ZV + 512]
                        S.dve(lambda e: e.bn_stats(out=bnst[:, 0:6], in_=zv), reads=[rzO], writes=[rbnst])
                        S.dve(lambda e: e.bn_aggr(out=stat[:, 4:6], in_=bnst[:, 0:6]), reads=[rbnst], writes=[rstat])
                        S.act(lambda e: e.activation(out=stat[:, 6:7], in_=stat[:, 5:6], func=AF.Ln, bias=epsT[:], scale=1.0),
                              reads=[rstat, repsT], writes=[rstat])
                        S.act(lambda e: e.activation(out=stat[:, 7:8], in_=stat[:, 6:7], func=AF.Exp, scale=-0.5),
                              reads=[rstat], writes=[rstat])
                        S.dve(lambda e: e.tensor_scalar(out=zv, in0=zv, scalar1=stat[:, 4:5],
                                                        scalar2=stat[:, 7:8], op0=ALU.subtract, op1=ALU.mult),
                              reads=[rzO, rstat], writes=[rzO])
                        S.dve(lambda e: e.tensor_tensor(out=zv, in0=zv, in1=lngB[:], op=ALU.mult), reads=[rzO, rlngB], writes=[rzO])
                        S.dve(lambda e: e.tensor_tensor(out=vnb[:], in0=zv, in1=lnbB[:], op=ALU.add), reads=[rzO, rlnbB], writes=[rvnb])
                        i = 1
                        for g in range(8):
                            S.pe(lambda e, g=g, i=i: e.matmul(Pf[i][:, g * 64:(g + 1) * 64], lhsT=wsT[:, g, :],
                                                             rhs=vnb[:, g * 64:(g + 1) * 64], start=True, stop=True),
                                 reads=[rwsT, rvnb], writes=[rP[i]])
                        for g in range(8):
                            S.dve(lambda e, i=i, g=g: e.scalar_tensor_tensor(
                                out=mixf[:, g * 64:(g + 1) * 64], in0=Pf[i][:, g * 64:(g + 1) * 64], scalar=bsT[:, g:g + 1],
                                in1=zO[:, ZU + g * 64:ZU + (g + 1) * 64], op0=ALU.add, op1=ALU.mult),
                                reads=[rP[i], rbsT, rzO], writes=[rsg])

                    DEPTH = 2
                    n_cmp_items = 2 * n_ct
                    E_ = n_cmp_items + DEPTH + 1
                    L2_ = max(E_ + 1, min(E_ + 40, len(items) + DEPTH - 1 - 8))
                    for k_ in range(len(items) + DEPTH):
                        if k_ < len(items):
                            emit_scores(items[k_])
                        if k_ >= DEPTH:
                            emit_pv(items[k_ - DEPTH])
                        if k_ == E_:
                            mid_work()
                        if k_ == L2_ and s + 1 < n_slots:
                            pre(s + 1)

                    S.dve(lambda e: e.tensor_tensor(out=hnO[:], in0=mixf[:], in1=sil, op=ALU.mult),
                          reads=[rmixf, rsg, rsil], writes=[rhnO])

                    i = nextP()
                    for kc in range(8):
                        S.pe(lambda e, kc=kc, i=i: e.transpose(out=Pb[i][:, kc * 128:(kc + 1) * 128],
                                                               in_=hnO[:, kc * 128:(kc + 1) * 128], identity=identb[:]),
                             reads=[rhnO, rIdb], writes=[rP[i]])
                    S.act(lambda e, i=i: e.copy(out=hnTO[:].rearrange("p a b -> p (a b)"), in_=Pb[i][:, :]),
                          reads=[rP[i]], writes=[rhnTO])
                    for cb in range(2):
                        i = nextP()
                        for kc in range(8):
                            S.pe(lambda e, kc=kc, i=i, cb=cb: e.matmul(Pf[i][:, :], lhsT=hnTO[:, kc, :],
                                                                       rhs=WO[:, kc, cb * 512:(cb + 1) * 512],
                                                                       start=(kc == 0), stop=(kc == 7)),
                                 reads=[rhnTO, rWO], writes=[rP[i]])
                        S.dve(lambda e, i=i, cb=cb, x_t=x_t: e.tensor_tensor(
                            out=x_t[:, cb * 512:(cb + 1) * 512], in0=Pf[i][:, :], in1=x_t[:, cb * 512:(cb + 1) * 512],
                            op=ALU.add), reads=[rP[i], rx], writes=[rx])
                    rms_rstd(x_t, rx, hnO, rhnO)
                    S.dve(lambda e, x_t=x_t: e.scalar_tensor_tensor(out=x_t[:], in0=x_t[:], scalar=stat[:, 2:3], in1=fgB[:],
                                                                    op0=ALU.mult, op1=ALU.mult),
                          reads=[rx, rstat, rfgB], writes=[rx])
                    S.dma(lambda e, s=s, x_t=x_t: e.dma_start(out=y_o[s, :, :], in_=x_t[:]), reads=[rx], q=("sp" if s == n_slots - 1 else "pool"))

                for s_ in range(n_slots):
                    slot_body(s_)
                S.flush()
    return nc, dict(S.cnt)


_PROG = {}


def _get_prog(key=("full",)):
    if key not in _PROG:
        _PROG[key] = build_program()
    return _PROG[key]


def prep_inputs(inp, with_sample=True, with_prompt=True):
    perm = _col_perm()
    samp_shared = {}
    if with_sample:
        cS, sS = _rope_tables(np.full((128,), 2048, dtype=np.int64))
        fbs = np.zeros((2, 33), np.float32)
        fbs[:, [0, 31, 32]] = 1e4
        bmask = np.zeros((8, 2), np.float32)
        bmask[0:4, 0] = 1.0
        bmask[4:8, 1] = 1.0
        samp_shared = {
            "pm8": (np.arange(128) % 8).astype(np.float32).reshape(128, 1),
            "ropeS": np.ascontiguousarray(np.concatenate([cS, sS], -1)),
            "ws00": np.ascontiguousarray(inp["w_s"][0][:, 0, 0][None, :]),
            "bs0": np.ascontiguousarray(inp["b_s"][0][:, 0][None, :]),
            "fbs": fbs,
            "bmask": bmask,
            "kc_pool": inp["cache_k_cmp"][0].reshape(20480, 2048),
            "vc_pool": inp["cache_v_cmp"][0].reshape(20480, 2048),
            "ks_pool": inp["cache_k_sel"][0].reshape(20480, 2048),
            "vs_pool": inp["cache_v_sel"][0].reshape(20480, 2048),
        }
    w_in_p = np.ascontiguousarray(inp["w_in"][0][:, perm])
    pos_all = np.arange(SEQ, dtype=np.int64)
    cA, sA = _rope_tables(pos_all)
    ropeA = np.concatenate([cA, sA], -1).reshape(NT_ALL, 128, 16).transpose(1, 0, 2)
    posC = 16 * np.arange(512, dtype=np.int64)
    cC, sC = _rope_tables(posC)
    ropeC = np.concatenate([cC, sC], -1).reshape(4, 128, 16).transpose(1, 0, 2)
    shared = {
        "w_in": w_in_p,
        "w_out": np.ascontiguousarray(inp["w_out"][0]),
        "final_g": np.ascontiguousarray(inp["final_g"][None, :]),
        "ln_g": np.ascontiguousarray(inp["ln_v_g"][0][None, :]),
        "ln_b": np.ascontiguousarray(inp["ln_v_b"][0][None, :]),
        "w_s": np.ascontiguousarray(inp["w_s"][0]),
        "b_sT": np.ascontiguousarray(inp["b_s"][0].T),
        "g_col": np.ascontiguousarray(inp["norm_g"][0].reshape(8, 128).T),
        "w_ck": np.ascontiguousarray(inp["w_ck"][0].transpose(1, 0, 2).reshape(64, 2048)),
        "w_cv": np.ascontiguousarray(inp["w_cv"][0].transpose(1, 0, 2).reshape(64, 2048)),
        "pe_ck": np.ascontiguousarray(inp["pe_ck"][0].T),
        "pe_cv": np.ascontiguousarray(inp["pe_cv"][0].T),
        "ropeA": np.ascontiguousarray(ropeA),
        "ropeC": np.ascontiguousarray(ropeC),
    }
    maps = []
    for c in range(8):
        b, r = c // 4, c % 4
        xbat = np.ascontiguousarray(inp["x_prompt"][b])
        tiles = xbat.reshape(NT_ALL, 128, D)
        own = np.arange(NSLOT) * 4 + r
        qp = (own[:, None] * 128 + np.arange(128)[None, :]).astype(np.int64)
        cO, sO = _rope_tables(qp.reshape(-1))
        ropeO = np.concatenate([cO, sO], -1).reshape(NSLOT, 128, 16).transpose(1, 0, 2)
        m = dict(shared)
        m["xb"] = xbat
        m["xo"] = np.ascontiguousarray(tiles[own])
        m["ropeO"] = np.ascontiguousarray(ropeO)
        m["qpos"] = np.ascontiguousarray(qp.reshape(1, -1).astype(np.float32))
        m["qposT"] = np.ascontiguousarray(qp.T.astype(np.float32))
        if with_sample:
            sl = slice(NSAMP * c, NSAMP * (c + 1))
            m["xs"] = np.ascontiguousarray(inp["x_sample"][sl, 0, :])
            pt = inp["page_table"][sl].astype(np.int32)
            m["pt_e"] = np.ascontiguousarray(np.repeat(pt.T, 8, axis=0))
            m["kwin"] = np.ascontiguousarray(inp["cache_k_win"][0, sl].reshape(NSAMP, 512, 128))
            m["vwin"] = np.ascontiguousarray(inp["cache_v_win"][0, sl].reshape(NSAMP, 512, 128))
            m.update(samp_shared)
        maps.append(m)
    return maps


def kernel(**inp):
    inp = {k: np.asarray(v) for k, v in inp.items()}
    nc, _ = _get_prog()
    maps = prep_inputs(inp)
    res = run_bass_kernel_spmd(nc, maps, core_ids=list(range(8)))
    return assemble(res.results)


def assemble(results, with_sample=True, with_prompt=True):
    B = 2
    outs_p = ()
    if with_prompt:
        y_prompt = np.zeros((B, SEQ, D), np.float32)
        kvp = np.zeros((B, SEQ, 768), np.float32)
        for c in range(8):
            b, r = c // 4, c % 4
            own = np.arange(NSLOT) * 4 + r
            y_prompt[b].reshape(NT_ALL, 128, D)[own] = results[c]["y_o"]
            if r == 0:
                kvp[b] = results[c]["kv_all"].reshape(SEQ, 768)

        def sl(c0):
            return np.ascontiguousarray(kvp[:, :, c0 - C_KV:c0 - C_KV + 128]).reshape(1, B, SEQ, 2, 64)
        kc, vc, ks, vs = sl(C_KC), sl(C_VC), sl(C_KS), sl(C_VS)
        kw = np.ascontiguousarray(sl(C_KW)[:, :, SEQ - 512:])
        vw = np.ascontiguousarray(sl(C_VW)[:, :, SEQ - 512:])
        outs_p = (y_prompt, kc, vc, ks, vs, kw, vw)
    if not with_sample:
        return outs_p
    ys = np.concatenate([results[c]["ys"] for c in range(8)], 0).reshape(128, 1, D)
    kvs = np.concatenate([results[c]["kvs"] for c in range(8)], 0)
    vns = np.concatenate([results[c]["vns"] for c in range(8)], 0).reshape(1, 128, 1, 512)
    kwo = np.concatenate([results[c]["kwin_o"] for c in range(8)], 0).reshape(1, 128, 512, 2, 64)
    vwo = np.concatenate([results[c]["vwin_o"] for c in range(8)], 0).reshape(1, 128, 512, 2, 64)

    def ss(c0):
        return np.ascontiguousarray(kvs[:, c0 - C_KV:c0 - C_KV + 128]).reshape(1, 128, 1, 2, 64)
    outs_s = (ss(C_KC), ss(C_VC), ss(C_KS), ss(C_VS), kwo, vwo, vns)
    if not with_prompt:
        return (ys,) + outs_s
    return (outs_p[0], ys) + outs_p[1:] + outs_s
```

```python
from contextlib import ExitStack
import numpy as np
import ml_dtypes
import concourse.bass as bass
import concourse.mybir as mybir
from concourse.bass_utils import run_bass_kernel_spmd

F32 = mybir.dt.float32
BF16 = mybir.dt.bfloat16
I32 = mybir.dt.int32
AF = mybir.ActivationFunctionType
ALU = mybir.AluOpType

ENG_NAMES = ("pe", "act", "dve", "pool", "sp")


class Res:
    __slots__ = ("name", "last_w", "readers", "dma_sem", "dma_cnt", "dram")

    def __init__(self, name, dram=False):
        self.name = name
        self.last_w = None
        self.readers = []
        self.dma_sem = {}
        self.dma_cnt = {}
        self.dram = dram


class Op:
    __slots__ = ("eng", "fn", "reads", "writes", "dma", "idx", "seq", "waits", "dres", "dval", "dkind")

    def __init__(self, eng, fn, reads, writes, dma):
        self.eng = eng
        self.fn = fn
        self.reads = reads
        self.writes = writes
        self.dma = dma
        self.waits = []


class Sched:
    def __init__(self, nc, sem_stack):
        self.nc = nc
        self.ops = []
        self.flushed = 0
        self.sem_stack = sem_stack
        self.cnt = {e: 0 for e in ENG_NAMES}
        self.esem = {e: sem_stack.enter_context(nc.semaphore("es_" + e)) for e in ENG_NAMES if e != "sp"}
        self.out_res = Res("dramonly")
        self.known = {e: {} for e in ENG_NAMES}

    def op(self, eng, fn, reads=(), writes=(), dma=False):
        o = Op(eng, fn, [r for r in reads if r is not None], [r for r in writes if r is not None], dma)
        o.idx = len(self.ops)
        self.ops.append(o)
        return o

    def pe(self, fn, reads=(), writes=()):
        return self.op("pe", fn, reads, writes)

    def act(self, fn, reads=(), writes=()):
        return self.op("act", fn, reads, writes)

    def dve(self, fn, reads=(), writes=()):
        return self.op("dve", fn, reads, writes)

    def pool(self, fn, reads=(), writes=()):
        return self.op("pool", fn, reads, writes)

    def dma(self, fn, reads=(), writes=(), q="sp"):
        return self.op(q, fn, reads, writes, dma=True)

    def flush(self):
        nc = self.nc
        ops = self.ops
        new = ops[self.flushed:]
        self.flushed = len(ops)
        if not new:
            return
        cnt = self.cnt
        esem = self.esem
        known = self.known
        for o in new:
            if not o.dma:
                cnt[o.eng] += 1
                o.seq = cnt[o.eng]
            else:
                o.seq = None

        def token_of(o):
            if o.dma:
                return ("d", o.dres, o.dval, o.dkind)
            return ("e", o.eng, o.seq)

        def add_wait(o, tok):
            if tok[0] == "e":
                key = ("e", tok[1])
                if tok[1] == o.eng and o.eng == "pe" and not o.dma:
                    return
            else:
                key = ("d", id(tok[1]), tok[3])
            val = tok[2]
            if tok[0] == "d":
                val = tok[1].dma_cnt[tok[3]]
            k = known[o.eng]
            if k.get(key, 0) >= val:
                return
            k[key] = val
            o.waits.append((tok, val))

        touched = []
        seen = set()
        for o in new:
            for r in o.reads:
                if r.last_w is not None:
                    add_wait(o, token_of(ops[r.last_w]))
            for r in o.writes:
                if r.last_w is not None:
                    add_wait(o, token_of(ops[r.last_w]))
                for ri in r.readers:
                    if ri != o.idx:
                        add_wait(o, token_of(ops[ri]))
            if o.dma:
                dres = None
                for r in list(o.writes) + list(o.reads):
                    if not r.dram:
                        dres = r
                        break
                if dres is None:
                    dres = self.out_res
                kind = "sw" if o.eng == "pool" else "hw"
                if kind not in dres.dma_sem:
                    dres.dma_sem[kind] = self.sem_stack.enter_context(nc.semaphore("ds%s_%s" % (kind, dres.name)))
                    dres.dma_cnt[kind] = 0
                dres.dma_cnt[kind] += 16
                o.dres = dres
                o.dkind = kind
                o.dval = dres.dma_cnt[kind]
                if (id(dres), kind) not in seen:
                    seen.add((id(dres), kind))
                    touched.append((dres, kind))
            for r in o.reads:
                if r not in o.writes:
                    r.readers.append(o.idx)
            for r in o.writes:
                r.last_w = o.idx
                r.readers = []
        by_eng = {e: [o for o in new if o.eng == e] for e in ENG_NAMES}

        def run_engine(ename, eng):
            for o in by_eng[ename]:
                for tok, val in o.waits:
                    if tok[0] == "e":
                        eng.wait_ge(esem[tok[1]], val)
                    else:
                        eng.wait_ge(tok[1].dma_sem[tok[3]], val)
                ins = o.fn(eng)
                if o.dma:
                    ins.then_inc(o.dres.dma_sem[o.dkind], 16)
                else:
                    ins.then_inc(esem[ename], 1)
            if ename == "sp":
                for r, kind in touched:
                    eng.wait_ge(r.dma_sem[kind], r.dma_cnt[kind])
                    known["sp"][("d", id(r), kind)] = r.dma_cnt[kind]

        with nc.Block() as block:
            @block.sync
            def _(e):
                run_engine("sp", e)

            @block.tensor
            def _(e):
                run_engine("pe", e)

            @block.scalar
            def _(e):
                run_engine("act", e)

            @block.vector
            def _(e):
                run_engine("dve", e)

            @block.gpsimd
            def _(e):
                run_engine("pool", e)


D = 1024
SEQ = 8192
NT_ALL = SEQ // 128
NSLOT = 16
DIN = 3352
C_KV = 2584
C_Q, C_KS, C_KW, C_KC, C_VC, C_VS, C_VW = 0, 2584, 2712, 2840, 2968, 3096, 3224
C_U, C_V, C_ZA, C_ZB, C_G = 512, 1024, 1536, 2048, 2560
NEGM = -32768.0
SCALE = 0.125
EPS = 1e-6
NSAMP = 16
NPAGE = 16
QHEAD_ORDER = [0, 4, 1, 5, 2, 6, 3, 7]


def _col_perm():
    o = {}
    acc = 0
    for name, sz in (("u", 512), ("v", 512), ("za", 512), ("q", 512), ("kc", 128), ("vc", 128),
                     ("ks", 128), ("vs", 128), ("kw", 128), ("vw", 128), ("g", 24), ("zb", 512)):
        o[name] = (acc, sz)
        acc += sz
    perm = []
    for hd in QHEAD_ORDER:
        perm += list(range(o["q"][0] + hd * 64, o["q"][0] + hd * 64 + 64))
    for name in ("u", "v", "za", "zb", "g", "ks", "kw", "kc", "vc", "vs", "vw"):
        perm += list(range(o[name][0], o[name][0] + o[name][1]))
    return np.array(perm, dtype=np.int64)


def _rope_tables(pos):
    half = 8
    inv = np.power(np.float32(500000.0), -np.arange(half, dtype=np.float32) / np.float32(half)).astype(np.float32)
    ang = pos.astype(np.float32)[:, None] * inv[None, :]
    return np.cos(ang).astype(np.float32), np.sin(ang).astype(np.float32)


def sap(t, p0, pn, off, dims):
    fs = 1
    for d in t.shape[1:]:
        fs *= d
    return bass.AP(t, p0 * fs + off, [[fs, pn]] + [[a, b] for a, b in dims])


ZQ, ZU, ZV, ZZA, ZZB, ZG, ZW = 0, 512, 1024, 1536, 2048, 2560, 2584


def build_program(with_sample=True, n_slots=NSLOT, n_tiles_a=NT_ALL, dbg=(), n_samp=NSAMP, with_prompt=True):
    nc = bass.Bass("TRN2", target_bir_lowering=False)
    dt = nc.dram_tensor

    def din(name, shape, dtype=F32):
        return dt(name, list(shape), dtype, kind="ExternalInput").ap()

    def dout(name, shape, dtype=F32):
        return dt(name, list(shape), dtype, kind="ExternalOutput").ap()

    xb = din("xb", [SEQ, D])
    xo = din("xo", [NSLOT, 128, D])
    w_in = din("w_in", [D, DIN])
    w_out = din("w_out", [D, D])
    final_g = din("final_g", [1, D])
    ln_g = din("ln_g", [1, 512])
    ln_b = din("ln_b", [1, 512])
    w_s = din("w_s", [8, 128, 128])
    w_ck = din("w_ck", [64, 2048])
    w_cv = din("w_cv", [64, 2048])
    pe_ck = din("pe_ck", [64, 32])
    pe_cv = din("pe_cv", [64, 32])
    qposT = din("qposT", [128, NSLOT])
    b_sT = din("b_sT", [128, 8])
    g_col = din("g_col", [128, 8])
    ropeA = din("ropeA", [128, NT_ALL, 16])
    ropeO = din("ropeO", [128, NSLOT, 16])
    ropeC = din("ropeC", [128, 4, 16])
    qpos = din("qpos", [1, NSLOT * 128])
    if with_sample:
        xs = din("xs", [NSAMP, D])
        pt_e = din("pt_e", [128, NSAMP], I32)
        pm8 = din("pm8", [128, 1])
        ropeS = din("ropeS", [128, 16])
        ws00 = din("ws00", [1, 8])
        bs0 = din("bs0", [1, 8])
        fbs = din("fbs", [2, 33])
        bmask = din("bmask", [8, 2])
        pools = [din(nm, [20480, 2048]) for nm in ("kc_pool", "vc_pool", "ks_pool", "vs_pool")]
        kwin = din("kwin", [NSAMP, 512, 128])
        vwin = din("vwin", [NSAMP, 512, 128])
        ys = dout("ys", [NSAMP, D])
        kvs = dout("kvs", [NSAMP, 768])
        vns = dout("vns", [NSAMP, 512])
        kwin_o = dout("kwin_o", [NSAMP, 512, 128])
        vwin_o = dout("vwin_o", [NSAMP, 512, 128])
        scr = dout("scr", [3, 128, 65])
    y_o = dout("y_o", [NSLOT, 128, D])
    kv_all = dout("kv_all", [NT_ALL, 128, 768])

    gst = ExitStack()
    with gst:
        S = Sched(nc, gst)

        def mk_sb(stack):
            def sb(name, shape, dtype=F32):
                t = stack.enter_context(nc.sbuf_tensor(name, list(shape), dtype))
                return t, Res(name)
            return sb

        def mk_ps(stack):
            def ps(name, shape, dtype=F32):
                t = stack.enter_context(nc.psum_tensor(name, list(shape), dtype))
                return t, Res(name)
            return ps

        sb = mk_sb(gst)
        ps = mk_ps(gst)

        WI, rWI = sb("WI", [128, 8, C_KV], BF16)
        Wck, rWck = sb("Wck", [128, 32, 64], BF16)
        Wcv, rWcv = sb("Wcv", [128, 32, 64], BF16)
        identb, rIdb = sb("identb", [128, 128], BF16)
        fgB, rfgB = sb("fgB", [128, D])
        lngB, rlngB = sb("lngB", [128, 512])
        lnbB, rlnbB = sb("lnbB", [128, 512])
        rO, rrO = sb("ropeO_sb", [128, NSLOT, 16])
        rC, rrC = sb("ropeC_sb", [128, 4, 16])
        qposC, rqposC = sb("qposC", [128, NSLOT])
        kpos, rkpos = sb("kpos", [128, NT_ALL])
        cend, rcend = sb("cend", [128, 4])
        cKB, rcKB = sb("cKB", [128, 128])
        cVB, rcVB = sb("cVB", [128, 128])
        wsT, rwsT = sb("wsT", [128, 8, 128], BF16)
        bsT, rbsT = sb("bsT", [128, 8])
        epsT, repsT = sb("epsT", [128, 1])
        gcol, rgcol = sb("gcol", [128, 8])
        stat, rstat = sb("stat", [128, 8])
        rtmp_t, rrtmp = sb("ropetmp", [128, 4 * 96])

        P0, rP0 = ps("P0", [128, 512])
        P1, rP1 = ps("P1", [128, 512])
        S0, rS0 = ps("S0", [128, 512])
        S1, rS1 = ps("S1", [128, 512])
        CC, rCC = ps("CC", [128, 2, 512])
        AS, rAS = ps("AS", [128, 512])
        AW, rAW = ps("AW", [128, 512])
        Pb = [P0.bitcast(BF16), P1.bitcast(BF16)]
        Pf = [P0, P1]
        rP = [rP0, rP1]
        Sb = [S0, S1]
        rSb = [rS0, rS1]
        pctr = [0]

        def nextP():
            i = pctr[0] % 2
            pctr[0] += 1
            return i

        sctr = [0]

        def nextS():
            i = sctr[0] % 2
            sctr[0] += 1
            return i

        with ExitStack() as s0:
            sb0 = mk_sb(s0)
            S.pool(lambda e: e.memset(identb[:], 0.0), writes=[rIdb])
            S.pool(lambda e: e.affine_select(out=identb[:], in_=identb[:], pattern=[[-1, 128]],
                                             compare_op=ALU.not_equal, fill=1.0, base=0,
                                             channel_multiplier=1), reads=[rIdb], writes=[rIdb])
            S.pool(lambda e: e.memset(epsT[:], EPS), writes=[repsT])

            def bload(dst, rdst, src, n):
                S.dma(lambda e: e.dma_start(out=dst[:], in_=bass.AP(src.tensor, 0, [[0, 128], [1, n]])),
                      writes=[rdst])
            bload(fgB, rfgB, final_g, D)
            bload(lngB, rlngB, ln_g, 512)
            bload(lnbB, rlnbB, ln_b, 512)
            S.dma(lambda e: e.dma_start(out=rO[:], in_=ropeO[:, :, :]), writes=[rrO])
            S.dma(lambda e: e.dma_start(out=rC[:], in_=ropeC[:, :, :]), writes=[rrC])
            S.dma(lambda e: e.dma_start(out=qposC[:], in_=qposT[:, :]), writes=[rqposC])
            S.dma(lambda e: e.dma_start(out=bsT[:], in_=b_sT[:, :]), writes=[rbsT])
            S.dma(lambda e: e.dma_start(out=gcol[:], in_=g_col[:, :]), writes=[rgcol])
            wst = [sb0("wstage%d" % i, [128, 1024]) for i in range(2)]
            wi = 0

            def load_w(dst, rdst, col0, ncols, stages, wi):
                for kc in range(8):
                    for c0 in range(0, ncols, 1024):
                        cw = min(1024, ncols - c0)
                        stg, rstg = stages[wi % 2]
                        wi += 1
                        S.dma(lambda e, stg=stg, kc=kc, c0=c0, cw=cw: e.dma_start(
                            out=stg[:, 0:cw], in_=w_in[kc * 128:(kc + 1) * 128, col0 + c0:col0 + c0 + cw]),
                            writes=[rstg])
                        S.dve(lambda e, stg=stg, kc=kc, c0=c0, cw=cw: e.tensor_scalar(
                            out=dst[:, kc, c0:c0 + cw], in0=stg[:, 0:cw], scalar1=gcol[:, kc:kc + 1], scalar2=None,
                            op0=ALU.mult), reads=[rstg, rgcol], writes=[rdst])
                return wi
            if 'no_w' not in dbg:
                wi = load_w(WI, rWI, 0, C_KV, wst, wi)
            for (wsrc, Wc, rWc) in ((w_ck, Wck, rWck), (w_cv, Wcv, rWcv)) if 'no_wc' not in dbg else ():
                for lh in range(2):
                    stg, rstg = wst[wi % 2]
                    wi += 1
                    for h in range(2):
                        S.dma(lambda e, h=h, wsrc=wsrc, stg=stg, lh=lh: e.dma_start(
                            out=stg[64 * h:64 * h + 64, 0:1024],
                            in_=wsrc[:, lh * 1024:(lh + 1) * 1024]), writes=[rstg])
                    S.dve(lambda e, Wc=Wc, stg=stg, lh=lh: e.tensor_copy(
                        out=Wc[:, lh * 16:(lh + 1) * 16, :].rearrange("p a b -> p (a b)"),
                        in_=stg[:, 0:1024]), reads=[rstg], writes=[rWc])
            pe2, rpe2 = sb0("pe2", [128, 2, 32])
            pe2b, rpe2b = sb0("pe2b", [128, 2, 32], BF16)
            crow, rcrow = sb0("crow", [1, 128])
            crow2, rcrow2 = sb0("crow2", [1, 2, 128], BF16)
            ones_f, rones = sb0("ones_f", [1, 128], BF16)
            S.pool(lambda e: e.memset(ones_f[:], 1.0), writes=[rones])
            for i, psrc in enumerate((pe_ck, pe_cv) if 'no_pe' not in dbg else ()):
                for h in range(2):
                    S.dma(lambda e, h=h, i=i, psrc=psrc: e.dma_start(
                        out=pe2[64 * h:64 * h + 64, i, :], in_=psrc[:, :]), writes=[rpe2])
            if 'no_pe' not in dbg:
                S.dve(lambda e: e.tensor_copy(out=pe2b[:], in_=pe2[:]), reads=[rpe2], writes=[rpe2b])
            for i, (Wc, rWc, cB, rcB) in enumerate(((Wck, rWck, cKB, rcKB), (Wcv, rWcv, cVB, rcVB)) if 'no_pe' not in dbg else ()):
                for h in range(2):
                    for l in range(32):
                        S.pe(lambda e, l=l, h=h, Wc=Wc, i=i: e.matmul(
                            Sb[h][0:1, 0:64], lhsT=sap(pe2b, 64 * h, 64, 32 * i + l, [[1, 1]]),
                            rhs=sap(Wc, 64 * h, 64, l * 64, [[1, 64]]), start=(l == 0), stop=(l == 31)),
                            reads=[rpe2b, rWc], writes=[rSb[h]])
                    S.dve(lambda e, h=h: e.tensor_copy(out=crow[:, 64 * h:64 * h + 64], in_=Sb[h][0:1, 0:64]),
                          reads=[rSb[h]], writes=[rcrow])
                S.dve(lambda e: e.tensor_copy(out=crow2[:, 0, :], in_=crow[:]), reads=[rcrow], writes=[rcrow2])
                S.dve(lambda e: e.tensor_tensor(out=crow2[:, 1, :], in0=crow[:], in1=crow2[:, 0, :], op=ALU.subtract),
                      reads=[rcrow, rcrow2], writes=[rcrow2])
                for hl in range(2):
                    S.pe(lambda e, hl=hl: e.matmul(P1[:, 0:128], lhsT=ones_f[:], rhs=crow2[:, hl, :], start=(hl == 0), stop=(hl == 1)),
                         reads=[rones, rcrow2], writes=[rP1])
                S.dve(lambda e, cB=cB: e.tensor_copy(out=cB[:], in_=P1[:, 0:128]), reads=[rP1], writes=[rcB])
            wsl, rwsl = sb0("wsl", [128, 8, 128])
            wslb, rwslb = sb0("wslb", [128, 8, 128], BF16)
            if 'no_ws' not in dbg:
                S.dma(lambda e: e.dma_start(out=wsl[:], in_=bass.AP(w_s.tensor, 0, [[128, 128], [128 * 128, 8], [1, 128]])),
                      writes=[rwsl])
                S.pool(lambda e: e.affine_select(out=wsl[:], in_=wsl[:], pattern=[[0, 8], [-1, 128]],
                                                 compare_op=ALU.is_ge, fill=0.0, base=0, channel_multiplier=1),
                       reads=[rwsl], writes=[rwsl])
                S.dve(lambda e: e.tensor_copy(out=wslb[:], in_=wsl[:]), reads=[rwsl], writes=[rwslb])
                for g in range(8):
                    i = g % 2
                    S.pe(lambda e, g=g, i=i: e.transpose(out=Pb[i][:, 0:128], in_=wslb[:, g, :], identity=identb[:]),
                         reads=[rwslb, rIdb], writes=[rP[i]])
                    S.act(lambda e, g=g, i=i: e.copy(out=wsT[:, g, :], in_=Pb[i][:, 0:128]), reads=[rP[i]], writes=[rwsT])
            S.pool(lambda e: e.iota(kpos[:], pattern=[[128, NT_ALL]], base=0, channel_multiplier=1,
                                    allow_small_or_imprecise_dtypes=True), writes=[rkpos])
            S.pool(lambda e: e.iota(cend[:], pattern=[[2048, 4]], base=31, channel_multiplier=16,
                                    allow_small_or_imprecise_dtypes=True), writes=[rcend])
            S.flush()

        def rms_rstd(xt, rxt, junk, rjunk):
            if "no_rms" in dbg:
                S.dve(lambda e: e.memset(stat[:, 0:3], 1.0), writes=[rstat])
                return
            if "no_accum" in dbg:
                S.dve(lambda e: e.tensor_tensor(out=junk[:], in0=xt[:], in1=xt[:], op=ALU.mult), reads=[rxt], writes=[rjunk])
                S.dve(lambda e: e.memset(stat[:, 0:1], 1024.0), writes=[rstat])
                S.act(lambda e: e.activation(out=stat[:, 1:2], in_=stat[:, 0:1], func=AF.Ln,
                                             bias=epsT[:], scale=1.0 / D), reads=[rstat, repsT], writes=[rstat])
                S.act(lambda e: e.activation(out=stat[:, 2:3], in_=stat[:, 1:2], func=AF.Exp,
                                             scale=-0.5), reads=[rstat], writes=[rstat])
                return
            S.dve(lambda e: e.scalar_tensor_tensor(out=junk[:], in0=xt[:], scalar=1.0, in1=xt[:], op0=ALU.mult,
                                                   op1=ALU.mult, accum_out=stat[:, 0:1]),
                  reads=[rxt], writes=[rjunk, rstat])
            S.act(lambda e: e.activation(out=stat[:, 1:2], in_=stat[:, 0:1], func=AF.Ln,
                                         bias=epsT[:], scale=1.0 / D), reads=[rstat, repsT], writes=[rstat])
            S.act(lambda e: e.activation(out=stat[:, 2:3], in_=stat[:, 1:2], func=AF.Exp,
                                         scale=-0.5), reads=[rstat], writes=[rstat])

        def rope(zt, rzt, c0, nh, tab, rtab, tcol):
            if "no_rope" in dbg:
                return
            x1 = sap(zt, 0, 128, c0, [[64, nh], [1, 8]])
            x2 = sap(zt, 0, 128, c0 + 8, [[64, nh], [1, 8]])
            cos = sap(tab, 0, 128, tcol * 16, [[0, nh], [1, 8]])
            sin = sap(tab, 0, 128, tcol * 16 + 8, [[0, nh], [1, 8]])
            t = [sap(rtmp_t, 0, 128, i * 96, [[8, nh], [1, 8]]) for i in range(4)]
            S.dve(lambda e: e.tensor_tensor(out=t[0], in0=x1, in1=cos, op=ALU.mult), reads=[rzt, rtab], writes=[rrtmp])
            S.dve(lambda e: e.tensor_tensor(out=t[1], in0=x2, in1=sin, op=ALU.mult), reads=[rzt, rtab], writes=[rrtmp])
            S.dve(lambda e: e.tensor_tensor(out=t[2], in0=x2, in1=cos, op=ALU.mult), reads=[rzt, rtab], writes=[rrtmp])
            S.dve(lambda e: e.tensor_tensor(out=t[3], in0=x1, in1=sin, op=ALU.mult), reads=[rzt, rtab], writes=[rrtmp])
            S.dve(lambda e: e.tensor_tensor(out=x1, in0=t[0], in1=t[1], op=ALU.subtract), reads=[rrtmp], writes=[rzt])
            S.dve(lambda e: e.tensor_tensor(out=x2, in0=t[2], in1=t[3], op=ALU.add), reads=[rrtmp], writes=[rzt])

        if with_sample:
          with ExitStack() as scs:
            sbc = mk_sb(scs)
            WOc, rWOc = sbc("WOc", [128, 8, D], BF16)
            WKVc, rWKVc = sbc("WKVc", [128, 8, 768], BF16)
            xS, rxS = sbc("xS", [128, D])
            zS, rzS = sbc("zS", [128, DIN])
            hnS, rhnS = sbc("hnS", [128, D], BF16)
            hnTS, rhnTS = sbc("hnTS", [128, 8, 128], BF16)
            mixS, rmixS = sbc("mixS", [128, D])
            stgc = [sbc("stgc%d" % i, [128, 1024]) for i in range(2)]
            qbS, rqbS = sbc("qbS", [128, 512], BF16)
            qTzz, rqTzz = sbc("qTzz", [128, 2, 4, 128], BF16)
            gateS, rgateS = sbc("gateS", [128, 24])
            ropeS_sb, rropeS = sbc("ropeS_sb", [128, 1, 16])
            ws00B, rws00B = sbc("ws00B", [128, 8])
            bs0B, rbs0B = sbc("bs0B", [128, 8])
            pte, rpte = sbc("pte", [128, NSAMP], I32)
            pm8t, rpm8 = sbc("pm8t", [128, 1])
            idxf, ridxf = sbc("idxf", [128, NSAMP])
            idxi, ridxi = sbc("idxi", [128, NSAMP], I32)
            fbs_t, rfbs = sbc("fbs_t", [2, 33])
            bmask_t, rbmask = sbc("bmask_t", [8, 2])
            maskw, rmaskw = sbc("maskw", [128, 4, 8])
            Ps32, rPs32 = sbc("Ps32", [128, 16, 8])
            Pw32, rPw32 = sbc("Pw32", [128, 4, 8])
            onesc, ronesc = sbc("onesc", [128, 1], BF16)
            identf, ridf = sbc("identf", [128, 128])
            bnS, rbnS = sbc("bnS", [128, 8])
            prodS, rprodS = mixS[:, 512:1024], rmixS
            pnew, rpnew = sbc("pnew", [128, 2, 8])
            G = [sbc("G%d" % i, [128, 16, 128]) for i in range(4)]
            XB = [sbc("XB%d" % i, [128, 16, 128], BF16) for i in range(3)]
            XT = [sbc("XT%d" % i, [128, 16, 128], BF16) for i in range(3)]
            vsA, rvsA = sbc("vsA", [128, 16, 2, 65], BF16)
            GW = [sbc("GW%d" % i, [128, 4, 128]) for i in range(2)]
            kwB, rkwB = sbc("kwB", [128, 4, 128], BF16)
            kwT, rkwT = sbc("kwT", [128, 4, 128], BF16)
            vwA, rvwA = sbc("vwA", [128, 4, 2, 65], BF16)
            kccf, rkccf2 = sbc("kccf_s", [128, 128])
            kccb, rkccb2 = sbc("kccb_s", [128, 128], BF16)
            kccTs, rkccTs = sbc("kccTs", [128, 128], BF16)
            VCs, rVCs = sbc("VCs", [128, 2, 98], BF16)
            ovt, rovt = sbc("ovt", [128, 33])
            ovt2, rovt2 = sbc("ovt2", [128, 33])
            ovs, rovs = sbc("ovs", [128, 33], BF16)
            PcTs, rPcTs = sbc("PcTs", [128, 8], BF16)
            sm8, rsm8 = sbc("sm8", [8, 4])
            Rm, rRm = sbc("Rm", [8, 2], BF16)
            Pm, rPm = sbc("Pm", [8, 128], BF16)
            Pn2s, rPn2s = sbc("Pn2s", [128, 2], BF16)
            impS, rimpS = sbc("impS", [2, 33])
            impS2, rimpS2 = sbc("impS2", [2, 33])
            mxS, rmxS = sbc("mxS", [2, 16])
            m01, rm01 = sbc("m01", [2, 33])
            mexp, rmexp = sbc("mexp", [2, 128], BF16)
            maskTs, rmaskTs = sbc("maskTs", [128, 2])
            PsTs, rPsTs = sbc("PsTs", [128, 16, 8], BF16)
            PwTs, rPwTs = sbc("PwTs", [128, 4, 8], BF16)
            OsT, rOsT = sbc("OsT", [65, 128])
            Osb, rOsb = sbc("Osb", [128, 65])
            Otok, rOtok = sbc("Otok", [128, 3, 8, 65])
            rdS, rrdS = sbc("rdS", [128, 8])
            facS, rfacS = sbc("facS", [128, 8])
            XBK, rXBK = CC[:, 1, :], Res("XBK")
            CCc, rCCc = CC[:, 0, :], Res("CCc")

            S.pool(lambda e: e.memset(xS[:], 0.0), writes=[rxS])
            S.dma(lambda e: e.dma_start(out=xS[0:NSAMP, :], in_=xs[:, :]), reads=[], writes=[rxS])
            for kc in range(8):
                stg, rstg = stgc[kc % 2]
                S.dma(lambda e, stg=stg, kc=kc: e.dma_start(out=stg[:], in_=w_out[kc * 128:(kc + 1) * 128, :]), writes=[rstg])
                S.dve(lambda e, stg=stg, kc=kc: e.tensor_copy(out=WOc[:, kc, :], in_=stg[:]), reads=[rstg], writes=[rWOc])
            load_w(WKVc, rWKVc, C_KV, 768, stgc, 0)
            S.dma(lambda e: e.dma_start(out=ropeS_sb[:, 0, :], in_=ropeS[:, :]), writes=[rropeS])
            S.dma(lambda e: e.dma_start(out=ws00B[:], in_=bass.AP(ws00.tensor, 0, [[0, 128], [1, 8]])), writes=[rws00B])
            S.dma(lambda e: e.dma_start(out=bs0B[:], in_=bass.AP(bs0.tensor, 0, [[0, 128], [1, 8]])), writes=[rbs0B])
            S.dma(lambda e: e.dma_start(out=pte[:], in_=pt_e[:, :]), writes=[rpte])
            S.dma(lambda e: e.dma_start(out=pm8t[:], in_=pm8[:, :]), writes=[rpm8])
            S.dma(lambda e: e.dma_start(out=fbs_t[:], in_=fbs[:, :]), writes=[rfbs])
            S.dma(lambda e: e.dma_start(out=bmask_t[:], in_=bmask[:, :]), writes=[rbmask])
            S.dve(lambda e: e.tensor_copy(out=idxf[:], in_=pte[:]), reads=[rpte], writes=[ridxf])
            S.dve(lambda e: e.tensor_scalar(out=idxf[:], in0=idxf[:], scalar1=8.0, scalar2=pm8t[:, 0:1], op0=ALU.mult, op1=ALU.add),
                  reads=[ridxf, rpm8], writes=[ridxf])
            S.dve(lambda e: e.tensor_copy(out=idxi[:], in_=idxf[:]), reads=[ridxf], writes=[ridxi])
            S.pool(lambda e: e.memset(maskw[:], 1.0), writes=[rmaskw])
            S.pool(lambda e: e.memset(maskw[0:1, 0, :], 0.0), reads=[rmaskw], writes=[rmaskw])
            S.pool(lambda e: e.memset(onesc[:], 1.0), writes=[ronesc])
            S.pool(lambda e: e.memset(identf[:], 0.0), writes=[ridf])
            S.pool(lambda e: e.affine_select(out=identf[:], in_=identf[:], pattern=[[-1, 128]], compare_op=ALU.not_equal,
                                             fill=1.0, base=0, channel_multiplier=1), reads=[ridf], writes=[ridf])
            S.pool(lambda e: e.memset(qTzz[:], 0.0), writes=[rqTzz])
            S.pool(lambda e: e.memset(vsA[:, :, :, 64:65], 1.0), writes=[rvsA])
            S.pool(lambda e: e.memset(vwA[:, :, :, 64:65], 1.0), writes=[rvwA])
            S.pool(lambda e: e.memset(VCs[:], 0.0), writes=[rVCs])
            S.pool(lambda e: e.memset(VCs[:, :, 64:65], 1.0), reads=[rVCs], writes=[rVCs])
            S.pool(lambda e: e.memset(kccf[:], 0.0), writes=[rkccf2])
            S.pool(lambda e: e.memset(Otok[:], 1.0), writes=[rOtok])
            zc, rzc = sbc("zc", [128, 128], BF16)
            S.pool(lambda e: e.memset(zc[:], 0.0), writes=[rzc])
            for (ACC_, rACC_) in ((CCc, rCCc), (AS, rAS), (AW, rAW)):
                S.pe(lambda e, ACC_=ACC_: e.matmul(ACC_[:, 0:128], lhsT=zc[:], rhs=zc[:], start=True, stop=True), reads=[rzc], writes=[rACC_])
            S.pool(lambda e: e.iota(ovt[:], pattern=[[-64, 33]], base=0, channel_multiplier=16,
                                    allow_small_or_imprecise_dtypes=True), writes=[rovt])
            S.dve(lambda e: e.tensor_scalar(out=ovt2[:], in0=ovt[:], scalar1=-32.0, scalar2=None, op0=ALU.is_gt),
                  reads=[rovt], writes=[rovt2])
            S.dve(lambda e: e.scalar_tensor_tensor(out=ovt[:], in0=ovt[:], scalar=64.0, in1=ovt2[:], op0=ALU.is_lt, op1=ALU.mult),
                  reads=[rovt, rovt2], writes=[rovt])
            S.dve(lambda e: e.tensor_copy(out=ovs[:], in_=ovt[:]), reads=[rovt], writes=[rovs])
            for h in range(2):
                S.dve(lambda e, h=h: e.tensor_copy(out=VCs[:, h, 65:98], in_=ovt[:]), reads=[rovt], writes=[rVCs])

            rms_rstd(xS, rxS, hnS, rhnS)
            S.dve(lambda e: e.tensor_scalar(out=hnS[:], in0=xS[:], scalar1=stat[:, 2:3], scalar2=None, op0=ALU.mult),
                  reads=[rxS, rstat], writes=[rhnS])
            i = nextP()
            for kc in range(8):
                S.pe(lambda e, kc=kc, i=i: e.transpose(out=Pb[i][:, kc * 128:(kc + 1) * 128], in_=hnS[:, kc * 128:(kc + 1) * 128],
                                                       identity=identb[:]), reads=[rhnS, rIdb], writes=[rP[i]])
            S.act(lambda e, i=i: e.copy(out=hnTS[:].rearrange("p a b -> p (a b)"), in_=Pb[i][:, :]), reads=[rP[i]], writes=[rhnTS])
            blocks = [(WI, rWI, 512 * k, 512, 512 * k) for k in range(5)] + [(WI, rWI, C_G, 24, ZG)] + \
                     [(WKVc, rWKVc, 0, 512, C_KV), (WKVc, rWKVc, 512, 256, C_KV + 512)]
            for bi, (W_, rW_, c0, cw, z0) in enumerate(blocks):
                i = nextP()
                for kc in range(8):
                    S.pe(lambda e, kc=kc, i=i, c0=c0, cw=cw, W_=W_: e.matmul(Pf[i][:, 0:cw], lhsT=hnTS[:, kc, :], rhs=W_[:, kc, c0:c0 + cw],
                                                                            start=(kc == 0), stop=(kc == 7)),
                         reads=[rhnTS, rW_], writes=[rP[i]])
                if bi % 2 == 0:
                    S.act(lambda e, i=i, z0=z0, cw=cw: e.copy(out=zS[:, z0:z0 + cw], in_=Pf[i][:, 0:cw]), reads=[rP[i]], writes=[rzS])
                else:
                    S.dve(lambda e, i=i, z0=z0, cw=cw: e.tensor_copy(out=zS[:, z0:z0 + cw], in_=Pf[i][:, 0:cw]), reads=[rP[i]], writes=[rzS])
            rope(zS, rzS, 0, 8, ropeS_sb, rropeS, 0)
            rope(zS, rzS, C_KV, 4, ropeS_sb, rropeS, 0)
            S.act(lambda e: e.copy(out=qbS[:], in_=zS[:, 0:512]), reads=[rzS], writes=[rqbS])
            i = nextP()
            for a in range(4):
                S.pe(lambda e, a=a, i=i: e.transpose(out=Pb[i][:, a * 128:(a + 1) * 128], in_=qbS[:, a * 128:(a + 1) * 128],
                                                     identity=identb[:]), reads=[rqbS, rIdb], writes=[rP[i]])
            for h in range(2):
                S.act(lambda e, i=i, h=h: e.copy(out=qTzz[64 * h:64 * h + 64, h, :, :],
                                                 in_=Pb[i][64 * h:64 * h + 64, 0:512].rearrange("p (a b) -> p a b", a=4)),
                      reads=[rP[i]], writes=[rqTzz])
            S.act(lambda e: e.activation(out=gateS[:], in_=zS[:, ZG:ZG + 24], func=AF.Exp, scale=-1.0), reads=[rzS], writes=[rgateS])
            S.dve(lambda e: e.tensor_scalar(out=gateS[:], in0=gateS[:], scalar1=1.0, scalar2=None, op0=ALU.add), reads=[rgateS], writes=[rgateS])
            S.dve(lambda e: e.reciprocal(out=gateS[:], in_=gateS[:]), reads=[rgateS], writes=[rgateS])
            S.act(lambda e: e.activation(out=mixS[:], in_=zS[:, ZZA:ZZA + 1024], func=AF.Exp, scale=-1.0), reads=[rzS], writes=[rmixS])
            S.dve(lambda e: e.tensor_scalar(out=mixS[:], in0=mixS[:], scalar1=1.0, scalar2=None, op0=ALU.add), reads=[rmixS], writes=[rmixS])
            S.dve(lambda e: e.reciprocal(out=mixS[:], in_=mixS[:]), reads=[rmixS], writes=[rmixS])
            S.dve(lambda e: e.tensor_tensor(out=zS[:, ZZA:ZZA + 1024], in0=mixS[:], in1=zS[:, ZZA:ZZA + 1024], op=ALU.mult),
                  reads=[rmixS, rzS], writes=[rzS])
            zvS = zS[:, ZV:ZV + 512]
            S.dve(lambda e: e.bn_stats(out=bnS[:, 0:6], in_=zvS), reads=[rzS], writes=[rbnS])
            S.dve(lambda e: e.bn_aggr(out=stat[:, 4:6], in_=bnS[:, 0:6]), reads=[rbnS], writes=[rstat])
            S.act(lambda e: e.activation(out=stat[:, 6:7], in_=stat[:, 5:6], func=AF.Ln, bias=epsT[:], scale=1.0), reads=[rstat, repsT], writes=[rstat])
            S.act(lambda e: e.activation(out=stat[:, 7:8], in_=stat[:, 6:7], func=AF.Exp, scale=-0.5), reads=[rstat], writes=[rstat])
            S.dve(lambda e: e.tensor_scalar(out=zvS, in0=zvS, scalar1=stat[:, 4:5], scalar2=stat[:, 7:8], op0=ALU.subtract, op1=ALU.mult),
                  reads=[rzS, rstat], writes=[rzS])
            S.dve(lambda e: e.tensor_tensor(out=zvS, in0=zvS, in1=lngB[:], op=ALU.mult), reads=[rzS, rlngB], writes=[rzS])
            S.dve(lambda e: e.tensor_tensor(out=zvS, in0=zvS, in1=lnbB[:], op=ALU.add), reads=[rzS, rlnbB], writes=[rzS])
            S.dma(lambda e: e.dma_start(out=kvs[:, :], in_=zS[0:NSAMP, C_KV:C_KV + 768]), reads=[rzS], q="pool")
            kvdr = Res("dram_win", dram=True)
            S.dma(lambda e: e.dma_start(out=kwin_o[:, 0:511, :], in_=kwin[:, 1:512, :]), reads=[kvdr], writes=[])
            S.dma(lambda e: e.dma_start(out=vwin_o[:, 0:511, :], in_=vwin[:, 1:512, :]), reads=[kvdr], writes=[])
            S.dma(lambda e: e.dma_start(out=kwin_o[:, 511, :], in_=zS[0:NSAMP, C_KW:C_KW + 128]), reads=[rzS], q="pool")
            S.dma(lambda e: e.dma_start(out=vwin_o[:, 511, :], in_=zS[0:NSAMP, C_VW:C_VW + 128]), reads=[rzS], q="pool")
            S.dma(lambda e: e.dma_start(out=vns[:, :], in_=zS[0:NSAMP, ZV:ZV + 512]), reads=[rzS], q="pool")
            for g in range(8):
                S.dve(lambda e, g=g: e.tensor_scalar(out=mixS[:, g * 64:(g + 1) * 64], in0=zS[:, ZV + g * 64:ZV + (g + 1) * 64],
                                                     scalar1=ws00B[:, g:g + 1], scalar2=bs0B[:, g:g + 1], op0=ALU.mult, op1=ALU.add),
                      reads=[rzS, rws00B, rbs0B], writes=[rmixS])
            S.dve(lambda e: e.tensor_tensor(out=mixS[:, 0:512], in0=mixS[:, 0:512], in1=zS[:, ZU:ZU + 512], op=ALU.mult),
                  reads=[rmixS, rzS], writes=[rmixS])
            for wi_, kcol in enumerate((C_KS, C_KW)):
                S.dve(lambda e, kcol=kcol: e.tensor_tensor(
                    out=prodS.rearrange("p (a h d) -> p a h d", a=4, h=2),
                    in0=zS[:, 0:512].rearrange("p (a h d) -> p a h d", a=4, h=2),
                    in1=sap(zS, 0, 128, kcol, [[0, 4], [64, 2], [1, 64]]), op=ALU.mult), reads=[rzS], writes=[rprodS])
                S.dve(lambda e, wi_=wi_: e.tensor_reduce(
                    out=sap(pnew, 0, 128, wi_ * 8, [[1, 4], [4, 2]]),
                    in_=prodS.rearrange("p (a h d) -> p a h d", a=4, h=2), axis=mybir.AxisListType.X, op=ALU.add),
                    reads=[rprodS], writes=[rpnew])
            S.act(lambda e: e.activation(out=pnew[:], in_=pnew[:], func=AF.Exp, scale=SCALE), reads=[rpnew], writes=[rpnew])

            for b in range(n_samp):
                qbd = sap(qTzz, 0, 128, b, [[512, 2], [128, 4]])
                for ci in range(4):
                    g_, rg_ = G[ci]
                    S.dma(lambda e, ci=ci, g_=g_, b=b: e.indirect_dma_start(
                        out=g_[:].rearrange("p a b -> p (a b)"), out_offset=None, in_=pools[ci][:, :],
                        in_offset=bass.IndirectOffsetOnAxis(ap=idxi[:, b:b + 1], axis=0)),
                        reads=[ridxi], writes=[rg_], q="pool")
                for wi_, wsrc in enumerate((kwin, vwin)):
                    gw, rgw = GW[wi_]
                    S.dma(lambda e, gw=gw, wsrc=wsrc, b=b: e.dma_start(
                        out=gw[:], in_=wsrc[b, :, :].rearrange("(p c) f -> p c f", c=4)), writes=[rgw])
                for ci in range(3):
                    S.dve(lambda e, ci=ci: e.tensor_copy(out=XB[ci][0][:], in_=G[ci][0][:]), reads=[G[ci][1]], writes=[XB[ci][1]])
                S.act(lambda e: e.copy(out=vsA[:, :, :, 0:64], in_=G[3][0][:].rearrange("p c (h d) -> p c h d", h=2)),
                      reads=[G[3][1]], writes=[rvsA])
                S.dve(lambda e: e.tensor_copy(out=kwB[:], in_=GW[0][0][:]), reads=[GW[0][1]], writes=[rkwB])
                S.act(lambda e: e.copy(out=vwA[:, :, :, 0:64], in_=GW[1][0][:].rearrange("p c (h d) -> p c h d", h=2)),
                      reads=[GW[1][1]], writes=[rvwA])
                for ci in range(3):
                    for half in range(2):
                        i = nextP()
                        for cc in range(8):
                            c = half * 8 + cc
                            S.pe(lambda e, ci=ci, c=c, cc=cc, i=i: e.transpose(out=Pb[i][:, cc * 128:(cc + 1) * 128], in_=XB[ci][0][:, c, :],
                                                                                identity=identb[:]), reads=[XB[ci][1], rIdb], writes=[rP[i]])
                        S.act(lambda e, ci=ci, half=half, i=i: e.copy(
                            out=XT[ci][0][:, half * 8:(half + 1) * 8, :].rearrange("p a b -> p (a b)"), in_=Pb[i][:, :]),
                            reads=[rP[i]], writes=[XT[ci][1]])
                i = nextP()
                for c in range(4):
                    S.pe(lambda e, c=c, i=i: e.transpose(out=Pb[i][:, c * 128:(c + 1) * 128], in_=kwB[:, c, :], identity=identb[:]),
                         reads=[rkwB, rIdb], writes=[rP[i]])
                S.act(lambda e, i=i: e.copy(out=kwT[:].rearrange("p a b -> p (a b)"), in_=Pb[i][:, 0:512]), reads=[rP[i]], writes=[rkwT])
                for kv in range(2):
                    Wc_, rWc_ = (Wck, rWck) if kv == 0 else (Wcv, rWcv)
                    for h in range(2):
                        for l in range(32):
                            c, sh = l % 16, l // 16
                            S.pe(lambda e, l=l, c=c, sh=sh, h=h, kv=kv, Wc_=Wc_: e.matmul(
                                Sb[h][0:127, 0:64], lhsT=sap(XT[kv][0], 64 * h, 64, c * 128 + sh, [[1, 127]]),
                                rhs=sap(Wc_, 64 * h, 64, l * 64, [[1, 64]]), start=(l == 0), stop=(l == 31)),
                                reads=[XT[kv][1], rWc_], writes=[rSb[h]])
                    for h in range(2):
                        if kv == 0:
                            S.dve(lambda e, h=h: e.tensor_tensor(out=kccf[0:127, 64 * h:64 * h + 64], in0=Sb[h][0:127, 0:64],
                                                                 in1=cKB[0:127, 64 * h:64 * h + 64], op=ALU.add),
                                  reads=[rSb[h], rcKB], writes=[rkccf2])
                        else:
                            S.dve(lambda e, h=h: e.tensor_tensor(out=VCs[0:127, h, 0:64], in0=Sb[h][0:127, 0:64],
                                                                 in1=cVB[0:127, 64 * h:64 * h + 64], op=ALU.add),
                                  reads=[rSb[h], rcVB], writes=[rVCs])
                rope(kccf, rkccf2, 0, 2, rC, rrC, 0)
                S.dve(lambda e: e.tensor_copy(out=kccb[:], in_=kccf[:]), reads=[rkccf2], writes=[rkccb2])
                i = nextP()
                S.pe(lambda e, i=i: e.transpose(out=Pb[i][:, 0:128], in_=kccb[:], identity=identb[:]), reads=[rkccb2, rIdb], writes=[rP[i]])
                S.act(lambda e, i=i: e.copy(out=kccTs[:], in_=Pb[i][:, 0:128]), reads=[rP[i]], writes=[rkccTs])
                S.pe(lambda e, qbd=qbd: e.matmul(XBK[0:127, 0:8], lhsT=kccTs[:, 0:127], rhs=qbd, start=True, stop=True),
                     reads=[rkccTs, rqTzz], writes=[rXBK])
                S.act(lambda e: e.activation(out=PcTs[0:127, :], in_=XBK[0:127, 0:8], func=AF.Exp, scale=SCALE), reads=[rXBK], writes=[rPcTs])
                for h in range(2):
                    S.pe(lambda e, h=h, b=b: e.matmul(CCc[0:65, b * 8 + 4 * h:b * 8 + 4 * h + 4], lhsT=VCs[0:127, h, 0:65],
                                                     rhs=PcTs[0:127, 4 * h:4 * h + 4], start=True, stop=True),
                         reads=[rVCs, rPcTs], writes=[rCCc])
                S.pe(lambda e: e.matmul(XBK[0:8, 16:17], lhsT=PcTs[0:127, :], rhs=onesc[0:127, :], start=True, stop=True),
                     reads=[rPcTs, ronesc], writes=[rXBK])
                S.dve(lambda e: e.reciprocal(out=sm8[:, 0:1], in_=XBK[0:8, 16:17]), reads=[rXBK], writes=[rsm8])
                S.dve(lambda e: e.tensor_scalar(out=Rm[:], in0=bmask_t[:], scalar1=sm8[:, 0:1], scalar2=None, op0=ALU.mult),
                      reads=[rbmask, rsm8], writes=[rRm])
                i = nextP()
                S.pe(lambda e, i=i: e.transpose(out=Pb[i][0:8, 0:127], in_=PcTs[0:127, :], identity=identb[0:127, 0:127]),
                     reads=[rPcTs, rIdb], writes=[rP[i]])
                S.act(lambda e, i=i: e.copy(out=Pm[:, 0:127], in_=Pb[i][0:8, 0:127]), reads=[rP[i]], writes=[rPm])
                S.pe(lambda e: e.matmul(XBK[0:127, 24:26], lhsT=Pm[:, 0:127], rhs=Rm[:], start=True, stop=True),
                     reads=[rPm, rRm], writes=[rXBK])
                S.act(lambda e: e.copy(out=Pn2s[0:127, :], in_=XBK[0:127, 24:26]), reads=[rXBK], writes=[rPn2s])
                S.pe(lambda e: e.matmul(XBK[0:2, 32:65], lhsT=Pn2s[0:127, :], rhs=ovs[0:127, :], start=True, stop=True),
                     reads=[rPn2s, rovs], writes=[rXBK])
                S.dve(lambda e: e.tensor_tensor(out=impS[:], in0=XBK[0:2, 32:65], in1=fbs_t[:], op=ALU.add), reads=[rXBK, rfbs], writes=[rimpS])
                S.dve(lambda e: e.max(out=mxS[:, 0:8], in_=impS[:]), reads=[rimpS], writes=[rmxS])
                S.dve(lambda e: e.match_replace(out=impS2[:], in_to_replace=mxS[:, 0:8], in_values=impS[:], imm_value=-3e38),
                      reads=[rimpS, rmxS], writes=[rimpS2])
                S.dve(lambda e: e.max(out=mxS[:, 8:16], in_=impS2[:]), reads=[rimpS2], writes=[rmxS])
                S.dve(lambda e: e.tensor_scalar(out=m01[:], in0=impS[:], scalar1=mxS[:, 15:16], scalar2=None, op0=ALU.is_ge),
                      reads=[rimpS, rmxS], writes=[rm01])
                for r4 in range(4):
                    S.dve(lambda e, r4=r4: e.tensor_copy(out=sap(mexp, 0, 2, r4, [[4, 32]]), in_=m01[:, 0:32]), reads=[rm01], writes=[rmexp])
                S.pe(lambda e: e.matmul(XBK[:, 72:74], lhsT=mexp[:, :], rhs=identb[0:2, 0:2], start=True, stop=True),
                     reads=[rmexp, rIdb], writes=[rXBK])
                S.dve(lambda e: e.tensor_copy(out=maskTs[:], in_=XBK[:, 72:74]), reads=[rXBK], writes=[rmaskTs])
                i = nextP()
                for c in range(16):
                    S.pe(lambda e, c=c, i=i, qbd=qbd: e.matmul(Pf[i][:, c * 8:(c + 1) * 8], lhsT=XT[2][0][:, c, :], rhs=qbd, start=True, stop=True),
                         reads=[XT[2][1], rqTzz], writes=[rP[i]])
                S.act(lambda e, i=i: e.activation(out=Ps32[:].rearrange("p a b -> p (a b)"), in_=Pf[i][:, 0:128], func=AF.Exp, scale=SCALE),
                      reads=[rP[i]], writes=[rPs32])
                for h in range(2):
                    S.dve(lambda e, h=h: e.tensor_scalar(out=PsTs[:, :, 4 * h:4 * h + 4], in0=Ps32[:, :, 4 * h:4 * h + 4],
                                                         scalar1=maskTs[:, h:h + 1], scalar2=None, op0=ALU.mult),
                          reads=[rPs32, rmaskTs], writes=[rPsTs])
                for h in range(2):
                    for c in range(16):
                        S.pe(lambda e, h=h, c=c, b=b: e.matmul(AS[0:65, b * 8 + 4 * h:b * 8 + 4 * h + 4], lhsT=vsA[:, c, h, :],
                                                              rhs=PsTs[:, c, 4 * h:4 * h + 4], start=(c == 0), stop=(c == 15)),
                             reads=[rvsA, rPsTs], writes=[rAS])
                i = nextP()
                for c in range(4):
                    S.pe(lambda e, c=c, i=i, qbd=qbd: e.matmul(Pf[i][:, c * 8:(c + 1) * 8], lhsT=kwT[:, c, :], rhs=qbd, start=True, stop=True),
                         reads=[rkwT, rqTzz], writes=[rP[i]])
                S.act(lambda e, i=i: e.activation(out=Pw32[:].rearrange("p a b -> p (a b)"), in_=Pf[i][:, 0:32], func=AF.Exp, scale=SCALE),
                      reads=[rP[i]], writes=[rPw32])
                S.dve(lambda e: e.tensor_tensor(out=PwTs[:], in0=Pw32[:], in1=maskw[:], op=ALU.mult), reads=[rPw32, rmaskw], writes=[rPwTs])
                for h in range(2):
                    for c in range(4):
                        S.pe(lambda e, h=h, c=c, b=b: e.matmul(AW[0:65, b * 8 + 4 * h:b * 8 + 4 * h + 4], lhsT=vwA[:, c, h, :],
                                                              rhs=PwTs[:, c, 4 * h:4 * h + 4], start=(c == 0), stop=(c == 3)),
                             reads=[rvwA, rPwTs], writes=[rAW])

            rscr = Res("scr_dram", dram=True)
            for br, (ACC, rACC) in enumerate(((CCc, rCCc), (AS, rAS), (AW, rAW))):
                S.act(lambda e, ACC=ACC: e.copy(out=OsT[:, :], in_=ACC[0:65, 0:128]), reads=[rACC], writes=[rOsT])
                i = nextP()
                S.pe(lambda e, i=i: e.transpose(out=Pf[i][:, 0:65], in_=OsT[:, :], identity=identf[0:65, 0:65]),
                     reads=[rOsT, ridf], writes=[rP[i]])
                S.dve(lambda e, i=i: e.tensor_copy(out=Osb[:], in_=Pf[i][:, 0:65]), reads=[rP[i]], writes=[rOsb])
                S.dma(lambda e, br=br: e.dma_start(out=scr[br, :, :], in_=Osb[:]), reads=[rOsb], writes=[rscr])
                S.dma(lambda e, br=br: e.dma_start(out=Otok[0:NSAMP, br, :, :].rearrange("p a b -> p (a b)"),
                                                   in_=scr[br, :, :].rearrange("(b h) d -> b (h d)", h=8)),
                      reads=[rscr], writes=[rOtok])
            for wi_, (br, vcol) in enumerate(((1, C_VS), (2, C_VW))):
                for hd in range(8):
                    S.dve(lambda e, br=br, hd=hd, vcol=vcol, wi_=wi_: e.scalar_tensor_tensor(
                        out=Otok[:, br, hd, 0:64], in0=zS[:, vcol + 64 * (hd // 4):vcol + 64 * (hd // 4) + 64],
                        scalar=pnew[:, wi_, hd:hd + 1], in1=Otok[:, br, hd, 0:64], op0=ALU.mult, op1=ALU.add),
                        reads=[rzS, rpnew, rOtok], writes=[rOtok])
                S.dve(lambda e, br=br, wi_=wi_: e.tensor_tensor(out=Otok[:, br, :, 64], in0=Otok[:, br, :, 64], in1=pnew[:, wi_, :], op=ALU.add),
                      reads=[rOtok, rpnew], writes=[rOtok])
            for br in range(3):
                S.dve(lambda e, br=br: e.tensor_scalar(out=rdS[:], in0=Otok[:, br, :, 64], scalar1=1e-30, scalar2=None, op0=ALU.max),
                      reads=[rOtok], writes=[rrdS])
                S.dve(lambda e: e.reciprocal(out=rdS[:], in_=rdS[:]), reads=[rrdS], writes=[rrdS])
                S.dve(lambda e, br=br: e.tensor_tensor(out=facS[:], in0=rdS[:], in1=sap(gateS, 0, 128, br, [[3, 8]]), op=ALU.mult),
                      reads=[rrdS, rgateS], writes=[rfacS])
                for hd in range(8):
                    dst = mixS[:, 512 + hd * 64:512 + hd * 64 + 64]
                    if br == 0:
                        S.dve(lambda e, hd=hd, dst=dst: e.tensor_scalar(out=dst, in0=Otok[:, 0, hd, 0:64], scalar1=facS[:, hd:hd + 1],
                                                                        scalar2=None, op0=ALU.mult), reads=[rOtok, rfacS], writes=[rmixS])
                    else:
                        S.dve(lambda e, hd=hd, dst=dst, br=br: e.scalar_tensor_tensor(out=dst, in0=Otok[:, br, hd, 0:64], scalar=facS[:, hd:hd + 1],
                                                                                      in1=dst, op0=ALU.mult, op1=ALU.add),
                              reads=[rOtok, rfacS, rmixS], writes=[rmixS])
            S.dve(lambda e: e.tensor_tensor(out=hnS[:], in0=mixS[:], in1=zS[:, ZZA:ZZA + 1024], op=ALU.mult), reads=[rmixS, rzS], writes=[rhnS])
            i = nextP()
            for kc in range(8):
                S.pe(lambda e, kc=kc, i=i: e.transpose(out=Pb[i][:, kc * 128:(kc + 1) * 128], in_=hnS[:, kc * 128:(kc + 1) * 128],
                                                       identity=identb[:]), reads=[rhnS, rIdb], writes=[rP[i]])
            S.act(lambda e, i=i: e.copy(out=hnTS[:].rearrange("p a b -> p (a b)"), in_=Pb[i][:, :]), reads=[rP[i]], writes=[rhnTS])
            for cb in range(2):
                i = nextP()
                for kc in range(8):
                    S.pe(lambda e, kc=kc, i=i, cb=cb: e.matmul(Pf[i][:, :], lhsT=hnTS[:, kc, :], rhs=WOc[:, kc, cb * 512:(cb + 1) * 512],
                                                               start=(kc == 0), stop=(kc == 7)), reads=[rhnTS, rWOc], writes=[rP[i]])
                S.dve(lambda e, i=i, cb=cb: e.tensor_tensor(out=xS[:, cb * 512:(cb + 1) * 512], in0=Pf[i][:, :],
                                                            in1=xS[:, cb * 512:(cb + 1) * 512], op=ALU.add), reads=[rP[i], rxS], writes=[rxS])
            rms_rstd(xS, rxS, hnS, rhnS)
            S.dve(lambda e: e.scalar_tensor_tensor(out=xS[:], in0=xS[:], scalar=stat[:, 2:3], in1=fgB[:], op0=ALU.mult, op1=ALU.mult),
                  reads=[rxS, rstat, rfgB], writes=[rxS])
            S.dma(lambda e: e.dma_start(out=ys[:, :], in_=xS[0:NSAMP, :]), reads=[rxS], q="pool")
            S.flush()

        with ExitStack() as sab:
            if "nosab" in dbg or not with_prompt:
                return nc, dict(S.cnt)
            sbab = mk_sb(sab)
            KT2, rKT2 = sbab("KT2", [128, 2, SEQ], BF16)
            rKT = [[Res("KT_%d_%d" % (i, t)) for t in range(NT_ALL)] for i in range(4)]
            VA, rVAfull = sbab("VA", [128, NT_ALL, 4, 65], BF16)
            kccT, rkccT = sbab("kccT", [128, 4, 128], BF16)
            VC, rVC = sbab("VC", [128, 4, 2, 193], BF16)

            with ExitStack() as sa:
                sba = mk_sb(sa)
                KcT, rKcT = sba("KcT", [128, 2, SEQ], BF16)
                rA, rrA = sba("ropeA_sb", [128, NT_ALL, 16])
                xA = [sba("xA%d" % i, [128, D]) for i in range(2)]
                hnA = [sba("hnA%d" % i, [128, D], BF16) for i in range(2)]
                hnTA = [sba("hnTA%d" % i, [128, 8, 128], BF16) for i in range(2)]
                zkv = [sba("zkv%d" % i, [128, 768]) for i in range(2)]
                kin = [sba("kin%d" % i, [128, 512], BF16) for i in range(2)]
                WKV, rWKV = sba("WKV", [128, 8, 768], BF16)
                load_w(WKV, rWKV, C_KV, 768, xA, 0)
                ovl, rovl = zkv[1][0][:, 0:512].rearrange("p (a b) -> p a b", a=4), zkv[1][1]
                ovl2, rovl2 = zkv[0][0][:, 0:512].rearrange("p (a b) -> p a b", a=4), zkv[0][1]
                kcc_f, rkccf = sba("kcc_f", [128, 128])
                kcc_b, rkccb = sba("kcc_b", [128, 128], BF16)
                S.dma(lambda e: e.dma_start(out=rA[:], in_=ropeA[:, :, :]), writes=[rrA])
                S.pool(lambda e: e.memset(VC[:], 0.0), writes=[rVC])
                S.pool(lambda e: e.memset(kccT[:], 0.0), writes=[rkccT])
                S.pool(lambda e: e.iota(ovl, pattern=[[2048, 4], [-64, 128]], base=0, channel_multiplier=16,
                                        allow_small_or_imprecise_dtypes=True), writes=[rovl])
                S.dve(lambda e: e.tensor_scalar(out=ovl2, in0=ovl, scalar1=-32.0, scalar2=None, op0=ALU.is_gt),
                      reads=[rovl], writes=[rovl2])
                S.dve(lambda e: e.scalar_tensor_tensor(out=ovl, in0=ovl, scalar=64.0, in1=ovl2,
                                                       op0=ALU.is_lt, op1=ALU.mult), reads=[rovl, rovl2], writes=[rovl])
                for h in range(2):
                    S.dve(lambda e, h=h: e.tensor_copy(out=VC[:, :, h, 65:193], in_=ovl), reads=[rovl], writes=[rVC])
                S.dve(lambda e: e.memset(VC[:, :, :, 64:65], 1.0), reads=[], writes=[rVC])
                S.dve(lambda e: e.memset(VA[:, :, :, 64:65], 1.0), writes=[rVAfull])
                S.dve(lambda e: e.memset(kcc_f[:], 0.0), writes=[rkccf])
                Sbb = [S0.bitcast(BF16), S1.bitcast(BF16)]
                Abb = [(AS.bitcast(BF16), rAS), (AW.bitcast(BF16), rAW)]

                hnTA3 = hnTA + [sba("hnTA2", [128, 8, 128], BF16)]

                def stage1(t):
                    x_t, rx = xA[t % 2]
                    hn, rhn = hnA[t % 2]
                    hnT, rhnT = hnTA3[t % 3]
                    S.dma(lambda e: e.dma_start(out=x_t[:], in_=xb[t * 128:(t + 1) * 128, :]), writes=[rx])
                    rms_rstd(x_t, rx, hn, rhn)
                    S.dve(lambda e: e.tensor_scalar(out=hn[:], in0=x_t[:], scalar1=stat[:, 2:3], scalar2=None, op0=ALU.mult),
                          reads=[rx, rstat], writes=[rhn])
                    bk, rbk = Sbb[t % 2], rSb[t % 2]
                    for kc in range(8):
                        S.pe(lambda e, kc=kc: e.transpose(out=bk[:, kc * 128:(kc + 1) * 128], in_=hn[:, kc * 128:(kc + 1) * 128],
                                                          identity=identb[:]), reads=[rhn, rIdb], writes=[rbk])
                    S.act(lambda e: e.copy(out=hnT[:].rearrange("p a b -> p (a b)"), in_=bk[:, :]), reads=[rbk], writes=[rhnT])

                def stageMM(t):
                    hnT, rhnT = hnTA3[t % 3]
                    z, rz = zkv[t % 2]
                    for kc in range(8):
                        S.pe(lambda e, kc=kc: e.matmul(P0[:, :], lhsT=hnT[:, kc, :], rhs=WKV[:, kc, 0:512],
                                                       start=(kc == 0), stop=(kc == 7)), reads=[rhnT, rWKV], writes=[rP0])
                    S.act(lambda e: e.copy(out=z[:, 0:512], in_=P0[:, :]), reads=[rP0], writes=[rz])
                    for kc in range(8):
                        S.pe(lambda e, kc=kc: e.matmul(P1[:, 0:256], lhsT=hnT[:, kc, :], rhs=WKV[:, kc, 512:768],
                                                       start=(kc == 0), stop=(kc == 7)), reads=[rhnT, rWKV], writes=[rP1])
                    S.act(lambda e: e.copy(out=z[:, 512:768], in_=P1[:, 0:256]), reads=[rP1], writes=[rz])

                def stageR(t):
                    z, rz = zkv[t % 2]
                    kb, rkb = kin[t % 2]
                    rope(z, rz, 0, 4, rA, rrA, t)
                    S.dma(lambda e: e.dma_start(out=kv_all[t, :, :], in_=z[:]), reads=[rz], q="pool")
                    S.dve(lambda e: e.tensor_copy(out=kb[:], in_=z[:, 0:512]), reads=[rz], writes=[rkb])
                    S.dve(lambda e: e.tensor_copy(out=VA[:, t, :, 0:64], in_=z[:, 512:768].rearrange("p (a b) -> p a b", a=4)),
                          reads=[rz], writes=[rVAfull])

                def stageKT(t):
                    kb, rkb = kin[t % 2]
                    ab, rab = Abb[t % 2]
                    for a in range(4):
                        S.pe(lambda e, a=a: e.transpose(out=ab[:, a * 128:(a + 1) * 128], in_=kb[:, a * 128:(a + 1) * 128],
                                                        identity=identb[:]), reads=[rkb, rIdb], writes=[rab])
                    for a in range(2):
                        S.act(lambda e, a=a: e.copy(out=KT2[:, a, t * 128:(t + 1) * 128], in_=ab[:, a * 128:(a + 1) * 128]),
                              reads=[rab], writes=[rKT[a][t]])
                        S.act(lambda e, a=a: e.copy(out=KcT[:, a, t * 128:(t + 1) * 128], in_=ab[:, 256 + a * 128:256 + (a + 1) * 128]),
                              reads=[rab], writes=[rKT[2 + a][t]])

                for t in range(min(2, n_tiles_a)):
                    stage1(t)
                for t in range(n_tiles_a):
                    stageMM(t)
                    if t + 2 < n_tiles_a:
                        stage1(t + 2)
                    stageR(t)
                    if t >= 1:
                        stageKT(t - 1)
                if n_tiles_a > 0:
                    stageKT(n_tiles_a - 1)

                for N in range(4):
                    if 16 * N + 16 > n_tiles_a - (1 if N < 3 else 0):
                        break
                    nn = 128 if N < 3 else 127
                    rk_dep = [rKT[2][t] for t in range(16 * N, min(16 * N + 17, NT_ALL))]
                    rv_dep = [rKT[3][t] for t in range(16 * N, min(16 * N + 17, NT_ALL))]
                    cbanks = [(Sb[0], rSb[0]), (Sb[1], rSb[1]), (AS, rAS), (AW, rAW)]
                    for kv, (src_i, Wc_, rWc_, rdep) in enumerate(((0, Wck, rWck, rk_dep), (1, Wcv, rWcv, rv_dep))):
                        for h in range(2):
                            bk, rbk = cbanks[2 * kv + h]
                            for l in range(32):
                                S.pe(lambda e, l=l, h=h, N=N, nn=nn, bk=bk, src_i=src_i, Wc_=Wc_: e.matmul(
                                    bk[0:nn, 0:64],
                                    lhsT=sap(KcT, 64 * h, 64, src_i * SEQ + 2048 * N + l, [[16, nn]]),
                                    rhs=sap(Wc_, 64 * h, 64, l * 64, [[1, 64]]), start=(l == 0), stop=(l == 31)),
                                    reads=rdep + [rWc_], writes=[rbk])
                    for h in range(2):
                        bk, rbk = cbanks[h]
                        S.dve(lambda e, h=h, nn=nn, bk=bk: e.tensor_tensor(out=kcc_f[0:nn, 64 * h:64 * h + 64], in0=bk[0:nn, 0:64],
                                                                          in1=cKB[0:nn, 64 * h:64 * h + 64], op=ALU.add),
                              reads=[rbk, rcKB], writes=[rkccf])
                        bk, rbk = cbanks[2 + h]
                        S.dve(lambda e, h=h, nn=nn, N=N, bk=bk: e.tensor_tensor(
                            out=VC[0:nn, N, h, 0:64], in0=bk[0:nn, 0:64], in1=cVB[0:nn, 64 * h:64 * h + 64], op=ALU.add),
                            reads=[rbk, rcVB], writes=[rVC])
                    rope(kcc_f, rkccf, 0, 2, rC, rrC, N)
                    S.dve(lambda e: e.tensor_copy(out=kcc_b[:], in_=kcc_f[:]), reads=[rkccf], writes=[rkccb])
                    i = nextP()
                    S.pe(lambda e, i=i: e.transpose(out=Pb[i][:, 0:128], in_=kcc_b[:], identity=identb[:]),
                         reads=[rkccb, rIdb], writes=[rP[i]])
                    S.act(lambda e, i=i, N=N: e.copy(out=kccT[:, N, :], in_=Pb[i][:, 0:128]), reads=[rP[i]], writes=[rkccT])
                S.flush()

            with ExitStack() as sbs:
                sbb = mk_sb(sbs)
                WO, rWO = sbb("WO", [128, 8, D], BF16)
                EWq, rEWq = sbb("EWq", [128, 2048], BF16)
                xO = [sbb("xO%d" % i, [128, D]) for i in range(2)]
                zO, rzO = sbb("zO", [128, ZW])
                hnX = [sbb("hnX%d" % i, [128, D], BF16) for i in range(2)]
                hnTX = [sbb("hnTX%d" % i, [128, 8, 128], BF16) for i in range(2)]
                qb, rqb = sbb("qb", [128, 512], BF16)
                qTz = [sbb("qTz%d" % i, [128, 512], BF16) for i in range(2)]
                zb, rzb = sbb("zb", [128, 512], BF16)
                qpsX = [sbb("qps%d" % i, [128, 128]) for i in range(2)]
                cbcX = [sbb("cbc%d" % i, [128, 4, 128], BF16) for i in range(2)]
                cbsX = [sbb("cbs%d" % i, [128, 4, 128], BF16) for i in range(2)]
                cbwX = [sbb("cbw%d" % i, [128, 8, 128], BF16) for i in range(2)]
                wtmp, rwtmp = sbb("wtmp", [128, 128])
                PsT = [sbb("PsT%d" % i, [128, 512], BF16) for i in range(4)]
                rden, rrden = sbb("rden", [128, 16])
                fac, rfac = sbb("fac", [128, 4])
                imp, rimp = sbb("imp", [128, 128])
                imp2, rimp2 = sbb("imp2", [128, 128])
                mx8, rmx8 = sbb("mx8", [128, 16])
                nmask, rnmask = sbb("nmask", [128, 128], BF16)
                nmaskT = [[sbb("nmaskT%d_%d" % (i, gq), [128, 128], BF16) for gq in range(4)] for i in range(2)]
                gate, rgate = sbb("gate", [128, 24])
                mixf, rmixf = sbb("mixf", [128, D])
                rsg = Res("mixf_gmlp_half")
                vnb, rvnb = qb, rqb
                bnst, rbnst = sbb("bnst", [128, 8])
                sil, rsil = zO[:, ZZA:ZZA + 1024], rzO
                k64, rk64 = sbb("k64", [128, 128])
                e0, re0 = sbb("e0", [128, 128])
                ddt, rddt = sbb("ddt", [128, 128])
                fa, rfa = sbb("fa", [128, 128])
                ff, rff = sbb("ff", [128, 128])
                fbX = [sbb("fb%d" % i, [128, 128]) for i in range(2)]
                wst = xO
                for kc in range(8):
                    stg, rstg = wst[kc % 2]
                    S.dma(lambda e, stg=stg, kc=kc: e.dma_start(out=stg[:], in_=w_out[kc * 128:(kc + 1) * 128, :]),
                          writes=[rstg])
                    S.dve(lambda e, stg=stg, kc=kc: e.tensor_copy(out=WO[:, kc, :], in_=stg[:]),
                           reads=[rstg], writes=[rWO])
                S.pool(lambda e: e.memset(EWq[:], 1.0), writes=[rEWq])
                for gq in range(4):
                    S.pool(lambda e, gq=gq: e.affine_select(out=EWq[32 * gq:32 * gq + 32, :], in_=EWq[32 * gq:32 * gq + 32, :],
                                                            pattern=[[1, 2048]], compare_op=ALU.is_ge, fill=0.0, base=0,
                                                            channel_multiplier=-64), reads=[rEWq], writes=[rEWq])
                    S.pool(lambda e, gq=gq: e.affine_select(out=EWq[32 * gq:32 * gq + 32, :], in_=EWq[32 * gq:32 * gq + 32, :],
                                                            pattern=[[-1, 2048]], compare_op=ALU.is_ge, fill=0.0, base=63,
                                                            channel_multiplier=64), reads=[rEWq], writes=[rEWq])
                S.pool(lambda e: e.iota(k64[:], pattern=[[64, 128]], base=0, channel_multiplier=0,
                                        allow_small_or_imprecise_dtypes=True), writes=[rk64])
                S.pool(lambda e: e.memset(e0[:], 0.0), writes=[re0])
                S.pool(lambda e: e.memset(e0[:, 0:1], 1.0), reads=[re0], writes=[re0])
                psctr = [0]
                pcctr = [0]
                sbctr = [0]
                SB4 = [(S0, rS0), (S1, rS1), (P0, rP0)]
                TPb, rTP = Pb[1], rP1
                S.pool(lambda e: e.memset(zb[:], 0.0), writes=[rzb])
                for h in range(2):
                    S.pool(lambda e, h=h: e.memset(qTz[h][0][:], 0.0), writes=[qTz[h][1]])
                    for gq in range(4):
                        S.pool(lambda e, h=h, gq=gq: e.memset(nmaskT[h][gq][0][:], 0.0), writes=[nmaskT[h][gq][1]])

                def pre(s):
                    x_t, rx = xO[s % 2]
                    hnO, rhnO = hnX[s % 2]
                    hnTO, rhnTO = hnTX[s % 2]
                    qps, rqps = qpsX[s % 2]
                    cbc, rcbc = cbcX[s % 2]
                    cbs, rcbs = cbsX[s % 2]
                    cbw, rcbw = cbwX[s % 2]
                    fb, rfb = fbX[s % 2]
                    S.dma(lambda e: e.dma_start(out=x_t[:], in_=xo[s, :, :]), writes=[rx])
                    S.dma(lambda e: e.dma_start(out=qps[:], in_=bass.AP(qpos.tensor, s * 128, [[0, 128], [1, 128]])), writes=[rqps])
                    rms_rstd(x_t, rx, hnO, rhnO)
                    S.dve(lambda e: e.tensor_scalar(out=hnO[:], in0=x_t[:], scalar1=stat[:, 2:3], scalar2=None, op0=ALU.mult),
                          reads=[rx, rstat], writes=[rhnO])
                    for kc in range(8):
                        S.pe(lambda e, kc=kc: e.transpose(out=TPb[:, kc * 128:(kc + 1) * 128], in_=hnO[:, kc * 128:(kc + 1) * 128],
                                                          identity=identb[:]), reads=[rhnO, rIdb], writes=[rTP])
                    S.act(lambda e: e.copy(out=hnTO[:].rearrange("p a b -> p (a b)"), in_=TPb[:, :]), reads=[rTP], writes=[rhnTO])
                    S.dve(lambda e: e.tensor_scalar(out=ddt[:], in0=k64[:], scalar1=qposC[:, s:s + 1], scalar2=None,
                                                    op0=ALU.subtract), reads=[rk64, rqposC], writes=[rddt])
                    S.dve(lambda e: e.tensor_scalar(out=fa[:], in0=ddt[:], scalar1=0.0, scalar2=None, op0=ALU.is_le),
                          reads=[rddt], writes=[rfa])
                    S.dve(lambda e: e.scalar_tensor_tensor(out=ff[:], in0=ddt[:], scalar=-128.0, in1=fa[:],
                                                           op0=ALU.is_gt, op1=ALU.mult), reads=[rddt, rfa], writes=[rff])
                    S.dve(lambda e: e.tensor_tensor(out=ff[:], in0=ff[:], in1=e0[:], op=ALU.max), reads=[rff, re0], writes=[rff])
                    S.dve(lambda e: e.tensor_scalar(out=fa[:], in0=fa[:], scalar1=1.0, scalar2=1e30, op0=ALU.subtract,
                                                    op1=ALU.mult), reads=[rfa], writes=[rfa])
                    S.dve(lambda e: e.scalar_tensor_tensor(out=fb[:], in0=ff[:], scalar=1e4, in1=fa[:],
                                                           op0=ALU.mult, op1=ALU.add), reads=[rff, rfa], writes=[rfb])
                    n_ct = min(4, (32 * s + 31 + 127) // 128)
                    for nt in range(n_ct):
                        S.dve(lambda e, nt=nt: e.tensor_scalar(out=cbc[:, nt, :], in0=qps[:], scalar1=cend[:, nt:nt + 1],
                                                               scalar2=NEGM, op0=ALU.is_lt, op1=ALU.mult),
                              reads=[rqps, rcend], writes=[rcbc])
                    for jj in range(4):
                        j = 4 * s + jj
                        S.dve(lambda e, jj=jj, j=j: e.tensor_scalar(out=cbs[:, jj, :], in0=qps[:],
                                                                    scalar1=kpos[:, j:j + 1], scalar2=NEGM,
                                                                    op0=ALU.is_lt, op1=ALU.mult),
                              reads=[rqps, rkpos], writes=[rcbs])
                    for jj in range(8):
                        j = 4 * s - 4 + jj
                        if j < 0:
                            continue
                        S.dve(lambda e, j=j: e.tensor_scalar(out=wtmp[:], in0=qps[:], scalar1=-512.0,
                                                             scalar2=kpos[:, j:j + 1], op0=ALU.add, op1=ALU.is_ge),
                              reads=[rqps, rkpos], writes=[rwtmp])
                        if jj >= 4:
                            S.dve(lambda e, jj=jj: e.scalar_tensor_tensor(out=cbw[:, jj, :], in0=wtmp[:], scalar=NEGM,
                                                                          in1=cbs[:, jj - 4, :], op0=ALU.mult, op1=ALU.add),
                                  reads=[rwtmp, rcbs], writes=[rcbw])
                        else:
                            S.dve(lambda e, jj=jj: e.tensor_scalar(out=cbw[:, jj, :], in0=wtmp[:], scalar1=NEGM, scalar2=None,
                                                                   op0=ALU.mult), reads=[rwtmp], writes=[rcbw])

                if n_slots > 0:
                    pre(0)
                def slot_body(s):
                    x_t, rx = xO[s % 2]
                    hnO, rhnO = hnX[s % 2]
                    hnTO, rhnTO = hnTX[s % 2]
                    qps, rqps = qpsX[s % 2]
                    cbc, rcbc = cbcX[s % 2]
                    cbs, rcbs = cbsX[s % 2]
                    cbw, rcbw = cbwX[s % 2]
                    fb, rfb = fbX[s % 2]
                    blocks = [(512 * k, 512, 512 * k) for k in range(5)] + [(C_G, 24, ZG)]
                    for bi, (c0, cw, z0) in enumerate(blocks):
                        i = nextP()
                        for kc in range(8):
                            S.pe(lambda e, kc=kc, i=i, c0=c0, cw=cw: e.matmul(Pf[i][:, 0:cw], lhsT=hnTO[:, kc, :],
                                                                             rhs=WI[:, kc, c0:c0 + cw],
                                                                             start=(kc == 0), stop=(kc == 7)),
                                 reads=[rhnTO, rWI], writes=[rP[i]])
                        if bi % 2 == 0:
                            S.act(lambda e, i=i, z0=z0, cw=cw: e.copy(out=zO[:, z0:z0 + cw], in_=Pf[i][:, 0:cw]),
                                  reads=[rP[i]], writes=[rzO])
                        else:
                            S.dve(lambda e, i=i, z0=z0, cw=cw: e.tensor_copy(out=zO[:, z0:z0 + cw], in_=Pf[i][:, 0:cw]),
                                  reads=[rP[i]], writes=[rzO])
                    rope(zO, rzO, 0, 8, rO, rrO, s)
                    S.act(lambda e: e.copy(out=qb[:], in_=zO[:, 0:512]), reads=[rzO], writes=[rqb])
                    i = nextP()
                    for a in range(4):
                        S.pe(lambda e, a=a, i=i: e.transpose(out=Pb[i][:, a * 128:(a + 1) * 128],
                                                             in_=qb[:, a * 128:(a + 1) * 128], identity=identb[:]),
                             reads=[rqb, rIdb], writes=[rP[i]])
                    for h in range(2):
                        S.act(lambda e, i=i, h=h: e.copy(out=qTz[h][0][64 * h:64 * h + 64, :], in_=Pb[i][64 * h:64 * h + 64, 0:512]),
                              reads=[rP[i]], writes=[qTz[h][1]])
                    S.act(lambda e: e.activation(out=gate[:], in_=zO[:, ZG:ZG + 24], func=AF.Exp, scale=-1.0),
                          reads=[rzO], writes=[rgate])
                    S.dve(lambda e: e.tensor_scalar(out=gate[:], in0=gate[:], scalar1=1.0, scalar2=None, op0=ALU.add),
                          reads=[rgate], writes=[rgate])
                    S.dve(lambda e: e.reciprocal(out=gate[:], in_=gate[:]), reads=[rgate], writes=[rgate])
                    n_ct = min(4, (32 * s + 31 + 127) // 128)

                    def bc4(t_, col):
                        return sap(t_, 0, 128, col * 128, [[0, 4], [1, 128]])

                    def qTh(h):
                        return qTz[h][0][:, :]

                    def rqTh(h):
                        return qTz[h][1]

                    def zero_init(ACC, rACC, ncols):
                        S.pe(lambda e: e.matmul(ACC[:, 0:ncols], lhsT=zb[:, 0:128], rhs=zb[:, 0:ncols], start=True, stop=False),
                             reads=[rzb], writes=[rACC])

                    def gate_fac(h, br, den_ap, rsrc, clamp):
                        if clamp:
                            S.dve(lambda e: e.tensor_scalar(out=rden[:, 0:4], in0=den_ap, scalar1=1e-30, scalar2=None,
                                                            op0=ALU.max), reads=[rsrc], writes=[rrden])
                            S.dve(lambda e: e.reciprocal(out=rden[:, 0:4], in_=rden[:, 0:4]), reads=[rrden], writes=[rrden])
                        else:
                            S.dve(lambda e: e.reciprocal(out=rden[:, 0:4], in_=den_ap), reads=[rsrc], writes=[rrden])
                        S.dve(lambda e: e.tensor_tensor(out=fac[:], in0=rden[:, 0:4],
                                                        in1=sap(gate, 0, 128, 12 * h + br, [[3, 4]]), op=ALU.mult),
                              reads=[rrden, rgate], writes=[rfac])

                    items = []

                    def cmp_post(h):
                        den = sap(CC, 0, 128, 64, [[512, 2], [193, 2]])
                        gate_fac(h, 0, den, rCC, True)
                        for g in range(4):
                            hd = 4 * h + g
                            S.dve(lambda e, g=g, hd=hd: e.tensor_scalar(
                                out=mixf[:, 512 + hd * 64:512 + hd * 64 + 64],
                                in0=CC[:, g // 2, (g % 2) * 193:(g % 2) * 193 + 64],
                                scalar1=fac[:, g:g + 1], scalar2=None, op0=ALU.mult), reads=[rCC, rfac], writes=[rmixf])
                        for g in range(4):
                            src = fb[:] if g == 0 else imp[:]
                            S.dve(lambda e, g=g, src=src: e.scalar_tensor_tensor(
                                out=imp[:], in0=CC[:, g // 2, (g % 2) * 193 + 65:(g % 2) * 193 + 193],
                                scalar=rden[:, g:g + 1], in1=src, op0=ALU.mult, op1=ALU.add),
                                reads=[rCC, rrden, rfb, rimp], writes=[rimp])
                        S.dve(lambda e: e.max(out=mx8[:, 0:8], in_=imp[:]), reads=[rimp], writes=[rmx8])
                        S.dve(lambda e: e.match_replace(out=imp2[:], in_to_replace=mx8[:, 0:8], in_values=imp[:],
                                                        imm_value=-3e38), reads=[rimp, rmx8], writes=[rimp2])
                        S.dve(lambda e: e.max(out=mx8[:, 8:16], in_=imp2[:]), reads=[rimp2], writes=[rmx8])
                        S.dve(lambda e: e.tensor_scalar(out=nmask[:], in0=imp[:], scalar1=mx8[:, 15:16], scalar2=NEGM,
                                                        op0=ALU.is_lt, op1=ALU.mult), reads=[rimp, rmx8], writes=[rnmask])
                        S.pe(lambda e: e.transpose(out=TPb[:, 0:128], in_=nmask[:], identity=identb[:]),
                             reads=[rnmask, rIdb], writes=[rTP])
                        for gq in range(4):
                            nmT, rnmT = nmaskT[h][gq]
                            S.act(lambda e, nmT=nmT, gq=gq: e.copy(out=nmT[32 * gq:32 * gq + 32, :], in_=TPb[32 * gq:32 * gq + 32, 0:128]),
                                  reads=[rTP], writes=[rnmT])

                    def finish(ACC, rACC, br, h):
                        den = sap(ACC, 0, 128, 64, [[65, 4]])
                        gate_fac(h, br, den, rACC, False)
                        for g in range(4):
                            hd = 4 * h + g
                            dst = mixf[:, 512 + hd * 64:512 + hd * 64 + 64]
                            S.dve(lambda e, g=g, dst=dst: e.scalar_tensor_tensor(
                                out=dst, in0=ACC[:, g * 65:g * 65 + 64], scalar=fac[:, g:g + 1], in1=dst,
                                op0=ALU.mult, op1=ALU.add), reads=[rACC, rfac, rmixf], writes=[rmixf])

                    for h in range(2):
                        for nt in range(n_ct):
                            items.append(dict(
                                h=h, lhs_k=kccT[:, nt, :], rk=[rkccT], biases=[(identb[:], bc4(cbc, nt), [rIdb, rcbc])],
                                pv=[(CC[:, g // 2, (g % 2) * 193:(g % 2) * 193 + 193], g, VC[:, nt, h, :], rVC, rCC, g % 2 == 1) for g in range(4)],
                                pre=(lambda: [S.pe(lambda e, bkk=bkk: e.matmul(CC[:, bkk, 0:386], lhsT=zb[:, 0:128], rhs=zb[:, 0:386],
                                                                              start=True, stop=False), reads=[rzb], writes=[rCC])
                                              for bkk in range(2)]) if nt == 0 else None,
                                post=(lambda h=h: cmp_post(h)) if nt == n_ct - 1 else None))
                    for h in range(2):
                        jl = [jj for jj in range(8) if 4 * s - 4 + jj >= 0]
                        for ii, jj in enumerate(jl):
                            j = 4 * s - 4 + jj
                            items.append(dict(
                                h=h, lhs_k=KT2[:, 1, j * 128:(j + 1) * 128], rk=[rKT[1][j]],
                                biases=[(identb[:], bc4(cbw, jj), [rIdb, rcbw])],
                                pv=[(AW[:, g * 65:(g + 1) * 65], g, VA[:, j, 2 + h, :], rVAfull, rAW, g == 3) for g in range(4)],
                                pre=(lambda: zero_init(AW, rAW, 260)) if ii == 0 else None,
                                post=(lambda h=h: finish(AW, rAW, 2, h)) if ii == len(jl) - 1 else None))
                    for h in range(2):
                        nj = 4 * s + 4
                        for j in range(nj):
                            gq = j // 16
                            nmT, rnmT = nmaskT[h][gq]
                            biases = [(EWq[:, (j % 16) * 128:(j % 16 + 1) * 128], bc4(nmT, 0), [rEWq, rnmT])]
                            if j >= 4 * s:
                                biases.append((identb[:], bc4(cbs, j - 4 * s), [rIdb, rcbs]))
                            items.append(dict(
                                h=h, lhs_k=KT2[:, 0, j * 128:(j + 1) * 128], rk=[rKT[0][j]], biases=biases,
                                pv=[(AS[:, g * 65:(g + 1) * 65], g, VA[:, j, h, :], rVAfull, rAS, g == 3) for g in range(4)],
                                pre=(lambda: zero_init(AS, rAS, 260)) if j == 0 else None,
                                post=(lambda h=h: finish(AS, rAS, 1, h)) if j == nj - 1 else None))

                    def emit_scores(it):
                        si = sbctr[0] % len(SB4)
                        sbctr[0] += 1
                        bank, rbank = SB4[si]
                        nb = len(it["biases"])
                        S.pe(lambda e: e.matmul(bank[:, :], lhsT=it["lhs_k"], rhs=qTh(it["h"]), start=True, stop=(nb == 0)),
                             reads=it["rk"] + [rqTh(it["h"])], writes=[rbank])
                        for bi, (bl, br_, rdeps) in enumerate(it["biases"]):
                            S.pe(lambda e, bl=bl, br_=br_, bi=bi: e.matmul(bank[:, :], lhsT=bl, rhs=br_, start=False, stop=(bi == nb - 1)),
                                 reads=rdeps, writes=[rbank])
                        pt, rpt = PsT[psctr[0] % len(PsT)]
                        psctr[0] += 1
                        S.act(lambda e: e.activation(out=pt[:], in_=bank[:, :], func=AF.Exp, scale=SCALE), reads=[rbank], writes=[rpt])
                        it["pt"] = (pt, rpt)

                    def emit_pv(it):
                        if it["pre"] is not None:
                            it["pre"]()
                        pt, rpt = it["pt"]
                        npv = len(it["pv"])
                        for k_, (out_ap, g, rhs_ap, rrhs, racc, lastg) in enumerate(it["pv"]):
                            S.pe(lambda e, out_ap=out_ap, g=g, rhs_ap=rhs_ap, lastg=lastg: e.matmul(
                                out_ap, lhsT=pt[:, g * 128:(g + 1) * 128], rhs=rhs_ap, start=False,
                                stop=(it["post"] is not None and lastg)), reads=[rpt, rrhs], writes=[racc])
                        if it["post"] is not None:
                            it["post"]()

                    def silu_half(hf):
                        zz = zO[:, ZZA + 512 * hf:ZZA + 512 * (hf + 1)]
                        tmp = mixf[:, 0:512]
                        S.act(lambda e: e.activation(out=tmp, in_=zz, func=AF.Exp, scale=-1.0), reads=[rzO], writes=[rsg])
                        S.dve(lambda e: e.tensor_scalar(out=tmp, in0=tmp, scalar1=1.0, scalar2=None, op0=ALU.add), reads=[rsg], writes=[rsg])
                        S.dve(lambda e: e.reciprocal(out=tmp, in_=tmp), reads=[rsg], writes=[rsg])
                        S.dve(lambda e: e.tensor_tensor(out=zz, in0=tmp, in1=zz, op=ALU.mult), reads=[rsg, rzO], writes=[rzO])

                    def mid_a():
                        zv = zO[:, ZV:ZV + 512]
                        S.dve(lambda e: e.bn_stats(out=bnst[:, 0:6], in_=zv), reads=[rzO], writes=[rbnst])
                        S.dve(lambda e: e.bn_aggr(out=stat[:, 4:6], in_=bnst[:, 0:6]), reads=[rbnst], writes=[rstat])
                        silu_half(0)

                    def mid_b():
                        zv = zO[:, ZV:ZV + 512]
                        silu_half(1)
                        S.act(lambda e: e.activation(out=stat[:, 6:7], in_=stat[:, 5:6], func=AF.Ln, bias=epsT[:], scale=1.0),
                              reads=[rstat, repsT], writes=[rstat])
                        S.act(lambda e: e.activation(out=stat[:, 7:8], in_=stat[:, 6:7], func=AF.Exp, scale=-0.5),
                              reads=[rstat], writes=[rstat])
                        S.dve(lambda e: e.tensor_scalar(out=zv, in0=zv, scalar1=stat[:, 4:5],
                                                        scalar2=stat[:, 7:8], op0=ALU.subtract, op1=ALU.mult),
                              reads=[rzO, rstat], writes=[rzO])
                        S.dve(lambda e: e.tensor_tensor(out=zv, in0=zv, in1=lngB[:], op=ALU.mult), reads=[rzO, rlngB], writes=[rzO])
                        S.dve(lambda e: e.tensor_tensor(out=vnb[:], in0=zv, in1=lnbB[:], op=ALU.add), reads=[rzO, rlnbB], writes=[rvnb])

                    def mid_c():
                        i = 1
                        for g in range(8):
                            S.pe(lambda e, g=g, i=i: e.matmul(Pf[i][:, g * 64:(g + 1) * 64], lhsT=wsT[:, g, :],
                                                             rhs=vnb[:, g * 64:(g + 1) * 64], start=True, stop=True),
                                 reads=[rwsT, rvnb], writes=[rP[i]])
                        for g in range(8):
                            S.dve(lambda e, i=i, g=g: e.scalar_tensor_tensor(
                                out=mixf[:, g * 64:(g + 1) * 64], in0=Pf[i][:, g * 64:(g + 1) * 64], scalar=bsT[:, g:g + 1],
                                in1=zO[:, ZU + g * 64:ZU + (g + 1) * 64], op0=ALU.add, op1=ALU.mult),
                                reads=[rP[i], rbsT, rzO], writes=[rsg])

                    DEPTH = 2
                    n_cmp_items = 2 * n_ct
                    last_ = len(items) + DEPTH - 1
                    E_ = n_cmp_items + DEPTH + 1
                    B_ = min(E_ + 14, last_ - 2)
                    C_ = max(B_, min(E_ + 34, last_ - 1))
                    L2_ = max(C_, min(E_ + 44, last_ - 1))
                    for k_ in range(len(items) + DEPTH):
                        if k_ < len(items):
                            emit_scores(items[k_])
                        if k_ >= DEPTH:
                            emit_pv(items[k_ - DEPTH])
                        if k_ == E_:
                            mid_a()
                        if k_ == B_:
                            mid_b()
                        if k_ == C_:
                            mid_c()
                        if k_ == L2_ and s + 1 < n_slots:
                            pre(s + 1)

                    S.dve(lambda e: e.tensor_tensor(out=hnO[:], in0=mixf[:], in1=sil, op=ALU.mult),
                          reads=[rmixf, rsg, rsil], writes=[rhnO])

                    i = nextP()
                    for kc in range(8):
                        S.pe(lambda e, kc=kc, i=i: e.transpose(out=Pb[i][:, kc * 128:(kc + 1) * 128],
                                                               in_=hnO[:, kc * 128:(kc + 1) * 128], identity=identb[:]),
                             reads=[rhnO, rIdb], writes=[rP[i]])
                    S.act(lambda e, i=i: e.copy(out=hnTO[:].rearrange("p a b -> p (a b)"), in_=Pb[i][:, :]),
                          reads=[rP[i]], writes=[rhnTO])
                    for cb in range(2):
                        i = nextP()
                        for kc in range(8):
                            S.pe(lambda e, kc=kc, i=i, cb=cb: e.matmul(Pf[i][:, :], lhsT=hnTO[:, kc, :],
                                                                       rhs=WO[:, kc, cb * 512:(cb + 1) * 512],
                                                                       start=(kc == 0), stop=(kc == 7)),
                                 reads=[rhnTO, rWO], writes=[rP[i]])
                        S.dve(lambda e, i=i, cb=cb, x_t=x_t: e.tensor_tensor(
                            out=x_t[:, cb * 512:(cb + 1) * 512], in0=Pf[i][:, :], in1=x_t[:, cb * 512:(cb + 1) * 512],
                            op=ALU.add), reads=[rP[i], rx], writes=[rx])
                    rms_rstd(x_t, rx, hnO, rhnO)
                    S.dve(lambda e, x_t=x_t: e.scalar_tensor_tensor(out=x_t[:], in0=x_t[:], scalar=stat[:, 2:3], in1=fgB[:],
                                                                    op0=ALU.mult, op1=ALU.mult),
                          reads=[rx, rstat, rfgB], writes=[rx])
                    S.dma(lambda e, s=s, x_t=x_t: e.dma_start(out=y_o[s, :, :], in_=x_t[:]), reads=[rx], q=("sp" if s == n_slots - 1 else "pool"))

                for s_ in range(n_slots):
                    slot_body(s_)
                S.flush()
    return nc, dict(S.cnt)


_PROG = {}


def _get_prog(key=("full",)):
    if key not in _PROG:
        _PROG[key] = build_program()
    return _PROG[key]


def prep_inputs(inp, with_sample=True, with_prompt=True):
    perm = _col_perm()
    samp_shared = {}
    if with_sample:
        cS, sS = _rope_tables(np.full((128,), 2048, dtype=np.int64))
        fbs = np.zeros((2, 33), np.float32)
        fbs[:, [0, 31, 32]] = 1e4
        bmask = np.zeros((8, 2), np.float32)
        bmask[0:4, 0] = 1.0
        bmask[4:8, 1] = 1.0
        samp_shared = {
            "pm8": (np.arange(128) % 8).astype(np.float32).reshape(128, 1),
            "ropeS": np.ascontiguousarray(np.concatenate([cS, sS], -1)),
            "ws00": np.ascontiguousarray(inp["w_s"][0][:, 0, 0][None, :]),
            "bs0": np.ascontiguousarray(inp["b_s"][0][:, 0][None, :]),
            "fbs": fbs,
            "bmask": bmask,
            "kc_pool": inp["cache_k_cmp"][0].reshape(20480, 2048),
            "vc_pool": inp["cache_v_cmp"][0].reshape(20480, 2048),
            "ks_pool": inp["cache_k_sel"][0].reshape(20480, 2048),
            "vs_pool": inp["cache_v_sel"][0].reshape(20480, 2048),
        }
    w_in_p = np.ascontiguousarray(inp["w_in"][0][:, perm])
    pos_all = np.arange(SEQ, dtype=np.int64)
    cA, sA = _rope_tables(pos_all)
    ropeA = np.concatenate([cA, sA], -1).reshape(NT_ALL, 128, 16).transpose(1, 0, 2)
    posC = 16 * np.arange(512, dtype=np.int64)
    cC, sC = _rope_tables(posC)
    ropeC = np.concatenate([cC, sC], -1).reshape(4, 128, 16).transpose(1, 0, 2)
    shared = {
        "w_in": w_in_p,
        "w_out": np.ascontiguousarray(inp["w_out"][0]),
        "final_g": np.ascontiguousarray(inp["final_g"][None, :]),
        "ln_g": np.ascontiguousarray(inp["ln_v_g"][0][None, :]),
        "ln_b": np.ascontiguousarray(inp["ln_v_b"][0][None, :]),
        "w_s": np.ascontiguousarray(inp["w_s"][0]),
        "b_sT": np.ascontiguousarray(inp["b_s"][0].T),
        "g_col": np.ascontiguousarray(inp["norm_g"][0].reshape(8, 128).T),
        "w_ck": np.ascontiguousarray(inp["w_ck"][0].transpose(1, 0, 2).reshape(64, 2048)),
        "w_cv": np.ascontiguousarray(inp["w_cv"][0].transpose(1, 0, 2).reshape(64, 2048)),
        "pe_ck": np.ascontiguousarray(inp["pe_ck"][0].T),
        "pe_cv": np.ascontiguousarray(inp["pe_cv"][0].T),
        "ropeA": np.ascontiguousarray(ropeA),
        "ropeC": np.ascontiguousarray(ropeC),
    }
    maps = []
    for c in range(8):
        b, r = c // 4, c % 4
        xbat = np.ascontiguousarray(inp["x_prompt"][b])
        tiles = xbat.reshape(NT_ALL, 128, D)
        own = np.arange(NSLOT) * 4 + r
        qp = (own[:, None] * 128 + np.arange(128)[None, :]).astype(np.int64)
        cO, sO = _rope_tables(qp.reshape(-1))
        ropeO = np.concatenate([cO, sO], -1).reshape(NSLOT, 128, 16).transpose(1, 0, 2)
        m = dict(shared)
        m["xb"] = xbat
        m["xo"] = np.ascontiguousarray(tiles[own])
        m["ropeO"] = np.ascontiguousarray(ropeO)
        m["qpos"] = np.ascontiguousarray(qp.reshape(1, -1).astype(np.float32))
        m["qposT"] = np.ascontiguousarray(qp.T.astype(np.float32))
        if with_sample:
            sl = slice(NSAMP * c, NSAMP * (c + 1))
            m["xs"] = np.ascontiguousarray(inp["x_sample"][sl, 0, :])
            pt = inp["page_table"][sl].astype(np.int32)
            m["pt_e"] = np.ascontiguousarray(np.repeat(pt.T, 8, axis=0))
            m["kwin"] = np.ascontiguousarray(inp["cache_k_win"][0, sl].reshape(NSAMP, 512, 128))
            m["vwin"] = np.ascontiguousarray(inp["cache_v_win"][0, sl].reshape(NSAMP, 512, 128))
            m.update(samp_shared)
        maps.append(m)
    return maps


def kernel(**inp):
    inp = {k: np.asarray(v) for k, v in inp.items()}
    nc, _ = _get_prog()
    maps = prep_inputs(inp)
    res = run_bass_kernel_spmd(nc, maps, core_ids=list(range(8)))
    return assemble(res.results)


def assemble(results, with_sample=True, with_prompt=True):
    B = 2
    outs_p = ()
    if with_prompt:
        y_prompt = np.zeros((B, SEQ, D), np.float32)
        kvp = np.zeros((B, SEQ, 768), np.float32)
        for c in range(8):
            b, r = c // 4, c % 4
            own = np.arange(NSLOT) * 4 + r
            y_prompt[b].reshape(NT_ALL, 128, D)[own] = results[c]["y_o"]
            if r == 0:
                kvp[b] = results[c]["kv_all"].reshape(SEQ, 768)

        def sl(c0):
            return np.ascontiguousarray(kvp[:, :, c0 - C_KV:c0 - C_KV + 128]).reshape(1, B, SEQ, 2, 64)
        kc, vc, ks, vs = sl(C_KC), sl(C_VC), sl(C_KS), sl(C_VS)
        kw = np.ascontiguousarray(sl(C_KW)[:, :, SEQ - 512:])
        vw = np.ascontiguousarray(sl(C_VW)[:, :, SEQ - 512:])
        outs_p = (y_prompt, kc, vc, ks, vs, kw, vw)
    if not with_sample:
        return outs_p
    ys = np.concatenate([results[c]["ys"] for c in range(8)], 0).reshape(128, 1, D)
    kvs = np.concatenate([results[c]["kvs"] for c in range(8)], 0)
    vns = np.concatenate([results[c]["vns"] for c in range(8)], 0).reshape(1, 128, 1, 512)
    kwo = np.concatenate([results[c]["kwin_o"] for c in range(8)], 0).reshape(1, 128, 512, 2, 64)
    vwo = np.concatenate([results[c]["vwin_o"] for c in range(8)], 0).reshape(1, 128, 512, 2, 64)

    def ss(c0):
        return np.ascontiguousarray(kvs[:, c0 - C_KV:c0 - C_KV + 128]).reshape(1, 128, 1, 2, 64)
    outs_s = (ss(C_KC), ss(C_VC), ss(C_KS), ss(C_VS), kwo, vwo, vns)
    if not with_prompt:
        return (ys,) + outs_s
    return (outs_p[0], ys) + outs_p[1:] + outs_s
```
